# Optimizing a Trainium2 kernel written in Bass

```python
import math
import jax, jax.numpy as jnp
from jax import lax
import numpy as np

D_MODEL = 1024
BATCH = 16
SEQ = 256
DEPTH = 1
DEC_BATCH = 4
DEC_SEQ = 2048
PAST_LEN = 256

GRID_W = 64
N_HEADS = 8
N_KV_HEADS = 2
HEAD_DIM = 128
GROUP = N_HEADS // N_KV_HEADS
ATTN_WIDTH = N_HEADS * HEAD_DIM
KV_WIDTH = N_KV_HEADS * HEAD_DIM
WINDOW = 128
BLOCK = 128
ATTN_SCALE = HEAD_DIM ** -0.5
ROPE_THETA = 10000.0
HYENA_WIDTH = D_MODEL // 2
SHORT_CONV = 3
FILTER_BANDS = 8
FILTER_FEAT = 1 + 2 * FILTER_BANDS
FILTER_HIDDEN = 64
DECAY_TARGET = 1e-2
FAST_DECAY_PCT = 0.3
SLOW_DECAY_PCT = 1.5
D_FF = -(-8 * D_MODEL // (3 * 256)) * 256
IN_WIDTH = ATTN_WIDTH + 2 * KV_WIDTH + 3 * HYENA_WIDTH
EPS = 1e-6
NEG_INF = -1e30

kernel_name = "hybrid_swa_hyena_prefix_diffusion_step"


def rmsnorm(x, g):
    xf = x.astype(jnp.float32)
    y = xf * lax.rsqrt(jnp.mean(xf * xf, axis=-1, keepdims=True) + EPS)
    return (y * g.astype(jnp.float32)).astype(x.dtype)


def modulation(cond, w_mod, b_mod):
    m = jax.nn.silu(cond) @ w_mod + b_mod
    return jnp.split(m[:, None, :], 6, axis=-1)


def axial_rope(x):
    L = x.shape[1]
    rows = L // GRID_W
    row_ids = jnp.repeat(jnp.arange(rows), GRID_W)
    col_ids = jnp.tile(jnp.arange(GRID_W), rows)
    half = HEAD_DIM // 2
    inv_freq = ROPE_THETA ** (-jnp.arange(0, half, 2, dtype=jnp.float32) / half)

    def rot(xa, pos):
        ang = pos.astype(jnp.float32)[:, None] * inv_freq[None, :]
        cos = jnp.cos(ang)[None, :, None, :]
        sin = jnp.sin(ang)[None, :, None, :]
        x1, x2 = jnp.split(xa.astype(jnp.float32), 2, axis=-1)
        return jnp.concatenate([x1 * cos - x2 * sin, x1 * sin + x2 * cos], axis=-1)

    xr, xc = jnp.split(x, 2, axis=-1)
    return jnp.concatenate([rot(xr, row_ids), rot(xc, col_ids)], axis=-1).astype(x.dtype)


def context_attention(q, k, v, sink):
    B, L = q.shape[:2]
    qg = q.reshape(B, L, N_KV_HEADS, GROUP, HEAD_DIM)
    s = jnp.einsum('bqkgd,bckd->bkgqc', qg, k).astype(jnp.float32) * ATTN_SCALE
    sink_b = jnp.broadcast_to(sink.astype(jnp.float32).reshape(1, N_KV_HEADS, GROUP, 1, 1), s.shape[:-1] + (1,))
    p = jax.nn.softmax(jnp.concatenate([s, sink_b], axis=-1), axis=-1)[..., :-1]
    o = jnp.einsum('bkgqc,bckd->bqkgd', p.astype(v.dtype), v)
    return o.reshape(B, L, ATTN_WIDTH)


def latent_attention(q, k, v, ctx_k, ctx_v, sink):
    B, L = q.shape[:2]
    nb = L // BLOCK
    qb = q.reshape(B, nb, BLOCK, N_KV_HEADS, GROUP, HEAD_DIM)
    pad = ((0, 0), (BLOCK, BLOCK), (0, 0), (0, 0))
    kp = jnp.pad(k, pad).reshape(B, nb + 2, BLOCK, N_KV_HEADS, HEAD_DIM)
    vp = jnp.pad(v, pad).reshape(B, nb + 2, BLOCK, N_KV_HEADS, HEAD_DIM)
    kwin = jnp.concatenate([kp[:, :-2], kp[:, 1:-1], kp[:, 2:]], axis=2)
    vwin = jnp.concatenate([vp[:, :-2], vp[:, 1:-1], vp[:, 2:]], axis=2)
    blk = jnp.arange(nb)[:, None]
    qpos = blk * BLOCK + jnp.arange(BLOCK)[None, :]
    kpos = (blk - 1) * BLOCK + jnp.arange(3 * BLOCK)[None, :]
    rel = kpos[:, None, :] - qpos[:, :, None]
    valid = (jnp.abs(rel) <= WINDOW) & (kpos[:, None, :] >= 0) & (kpos[:, None, :] < L)
    s_win = jnp.einsum('bnqkgd,bnskd->bnkgqs', qb, kwin).astype(jnp.float32) * ATTN_SCALE
    s_win = jnp.where(valid[None, :, None, None], s_win, NEG_INF)
    s_ctx = jnp.einsum('bnqkgd,bckd->bnkgqc', qb, ctx_k).astype(jnp.float32) * ATTN_SCALE
    sink_b = jnp.broadcast_to(sink.astype(jnp.float32).reshape(1, 1, N_KV_HEADS, GROUP, 1, 1), s_win.shape[:-1] + (1,))
    p = jax.nn.softmax(jnp.concatenate([s_win, s_ctx, sink_b], axis=-1), axis=-1)
    p_win = p[..., :3 * BLOCK].astype(v.dtype)
    p_ctx = p[..., 3 * BLOCK:-1].astype(v.dtype)
    o = (jnp.einsum('bnkgqs,bnskd->bnqkgd', p_win, vwin)
         + jnp.einsum('bnkgqc,bckd->bnqkgd', p_ctx, ctx_v))
    return o.reshape(B, L, ATTN_WIDTH)


def hyena_filter_spectrum(L, w1, b1, fr1, w2, b2, fr2, w3, b3, decay):
    f32 = jnp.float32
    t = jnp.arange(L, dtype=f32)
    t_norm = t / L
    bands = jnp.linspace(1e-4, FILTER_BANDS - 1, FILTER_BANDS).astype(f32)
    ang = (2.0 * math.pi * t / L)[:, None] * bands[None, :]
    feat = jnp.concatenate([t_norm[:, None], jnp.cos(ang), jnp.sin(ang)], axis=-1)
    h = jnp.sin(fr1.astype(f32) * (feat @ w1.astype(f32) + b1.astype(f32)))
    h = jnp.sin(fr2.astype(f32) * (h @ w2.astype(f32) + b2.astype(f32)))
    h = h @ w3.astype(f32) + b3.astype(f32)
    h = h * jnp.exp(-t_norm[:, None] * jnp.abs(decay.astype(f32))[None, :])
    fwd, bwd = jnp.split(h, 2, axis=-1)
    filt = jnp.concatenate([fwd, jnp.zeros((1, HYENA_WIDTH), f32), bwd[:0:-1]], axis=0)
    return jnp.fft.rfft(filt, axis=0)


def short_conv(u, w, b):
    up = jnp.pad(u, ((0, 0), (1, 1), (0, 0)))
    return up[:, :-2] * w[0] + up[:, 1:-1] * w[1] + up[:, 2:] * w[2] + b


def hyena(u, conv_w, conv_b, spectrum, skip):
    L = u.shape[1]
    u = short_conv(u, conv_w, conv_b)
    x0, x1, v = jnp.split(u, 3, axis=-1)
    z = (v * x1).astype(jnp.float32)
    zf = jnp.fft.rfft(z, n=2 * L, axis=1)
    y = jnp.fft.irfft(zf * spectrum[None], n=2 * L, axis=1)[:, :L] + z * skip.astype(jnp.float32)
    return (x0.astype(jnp.float32) * y).astype(u.dtype)


def mixer(h, p, latent, ctx_k, ctx_v):
    B, L = h.shape[:2]
    proj = h @ p['w_in']
    q, k, v, hy = jnp.split(proj, [ATTN_WIDTH, ATTN_WIDTH + KV_WIDTH, ATTN_WIDTH + 2 * KV_WIDTH], axis=-1)
    q = q.reshape(B, L, N_HEADS, HEAD_DIM)
    k = k.reshape(B, L, N_KV_HEADS, HEAD_DIM)
    v = v.reshape(B, L, N_KV_HEADS, HEAD_DIM)
    if latent:
        attn = latent_attention(axial_rope(q), axial_rope(k), v, ctx_k, ctx_v, p['attn_sink'])
    else:
        attn = context_attention(q, k, v, p['attn_sink'])
    spectrum = hyena_filter_spectrum(L, p['filt_w1'], p['filt_b1'], p['filt_freq1'], p['filt_w2'], p['filt_b2'],
                                     p['filt_freq2'], p['filt_w3'], p['filt_b3'], p['filt_decay'])
    hy_out = hyena(hy, p['conv_w'], p['conv_b'], spectrum, p['hyena_skip'])
    gates = jax.nn.sigmoid((h @ p['w_gate'] + p['b_gate']).astype(jnp.float32)).astype(h.dtype)
    g_a, g_h = jnp.split(gates, 2, axis=-1)
    merged = g_a * (attn @ p['w_pa']) + g_h * (hy_out @ p['w_ph'])
    return merged @ p['w_o'], k, v


def layer(x, cond, p, latent, ctx_k, ctx_v):
    sh1, sc1, g1, sh2, sc2, g2 = modulation(cond, p['w_mod'], p['b_mod'])
    h = rmsnorm(x, p['norm_mix_pre']) * (1 + sc1) + sh1
    mix, k, v = mixer(h, p, latent, ctx_k, ctx_v)
    x = x + g1 * rmsnorm(mix, p['norm_mix_post'])
    h = rmsnorm(x, p['norm_ffn_pre']) * (1 + sc2) + sh2
    gt, up = jnp.split(h @ p['w_up'], 2, axis=-1)
    f = (jax.nn.silu(gt) * up) @ p['w_down']
    x = x + g2 * rmsnorm(f, p['norm_ffn_post'])
    return x, k, v


def setup_inputs(seed: int = 0) -> dict:
    key = jax.random.key(seed)
    ks = iter(jax.random.split(key, 40))
    nrm = lambda shape, s=1.0: jax.random.normal(next(ks), shape, jnp.float32) * s
    D = D_MODEL
    max_decay = abs(math.log(DECAY_TARGET)) / FAST_DECAY_PCT
    min_decay = abs(math.log(DECAY_TARGET)) / SLOW_DECAY_PCT
    return {
        'x_prompt': nrm((BATCH, SEQ, D)),
        'x_sample': nrm((DEC_BATCH, DEC_SEQ, D)),
        'c': nrm((DEC_BATCH, D)),
        'cache_k': nrm((DEC_BATCH, DEPTH, PAST_LEN, N_KV_HEADS, HEAD_DIM)),
        'cache_v': nrm((DEC_BATCH, DEPTH, PAST_LEN, N_KV_HEADS, HEAD_DIM)),
        'c_ctx': nrm((D,)),
        'norm_mix_pre': 1.0 + nrm((DEPTH, D), 0.1),
        'norm_mix_post': 1.0 + nrm((DEPTH, D), 0.1),
        'norm_ffn_pre': 1.0 + nrm((DEPTH, D), 0.1),
        'norm_ffn_post': 1.0 + nrm((DEPTH, D), 0.1),
        'w_mod': nrm((DEPTH, D, 6 * D), 0.5 * D ** -0.5),
        'b_mod': nrm((DEPTH, 6 * D), 0.01),
        'w_in': nrm((DEPTH, D, IN_WIDTH), D ** -0.5),
        'attn_sink': nrm((DEPTH, N_HEADS)),
        'conv_w': nrm((DEPTH, SHORT_CONV, 3 * HYENA_WIDTH), 0.5),
        'conv_b': nrm((DEPTH, 3 * HYENA_WIDTH), 0.01),
        'filt_w1': nrm((DEPTH, FILTER_FEAT, FILTER_HIDDEN), FILTER_FEAT ** -0.5),
        'filt_b1': nrm((DEPTH, FILTER_HIDDEN), 0.1),
        'filt_freq1': 1.0 + nrm((DEPTH, FILTER_HIDDEN), 0.1),
        'filt_w2': nrm((DEPTH, FILTER_HIDDEN, FILTER_HIDDEN), FILTER_HIDDEN ** -0.5),
        'filt_b2': nrm((DEPTH, FILTER_HIDDEN), 0.1),
        'filt_freq2': 1.0 + nrm((DEPTH, FILTER_HIDDEN), 0.1),
        'filt_w3': nrm((DEPTH, FILTER_HIDDEN, 2 * HYENA_WIDTH), FILTER_HIDDEN ** -0.5),
        'filt_b3': nrm((DEPTH, 2 * HYENA_WIDTH), 0.01),
        'filt_decay': jax.random.uniform(next(ks), (DEPTH, 2 * HYENA_WIDTH), jnp.float32, min_decay, max_decay),
        'hyena_skip': nrm((DEPTH, HYENA_WIDTH), 0.5),
        'w_pa': nrm((DEPTH, ATTN_WIDTH, D), ATTN_WIDTH ** -0.5),
        'w_ph': nrm((DEPTH, HYENA_WIDTH, D), HYENA_WIDTH ** -0.5),
        'w_gate': nrm((DEPTH, D, 2 * D), D ** -0.5),
        'b_gate': nrm((DEPTH, 2 * D), 0.01),
        'w_o': nrm((DEPTH, D, D), D ** -0.5),
        'w_up': nrm((DEPTH, D, 2 * D_FF), D ** -0.5),
        'w_down': nrm((DEPTH, D_FF, D), D_FF ** -0.5),
    }


def reference(x_prompt, x_sample, c, cache_k, cache_v, c_ctx,
              norm_mix_pre, norm_mix_post, norm_ffn_pre, norm_ffn_post, w_mod, b_mod,
              w_in, attn_sink, conv_w, conv_b, filt_w1, filt_b1, filt_freq1, filt_w2, filt_b2,
              filt_freq2, filt_w3, filt_b3, filt_decay, hyena_skip, w_pa, w_ph, w_gate, b_gate,
              w_o, w_up, w_down):
    y_prompt = x_prompt
    y_sample = x_sample
    new_k_layers = []
    new_v_layers = []
    for l in range(DEPTH):
        p = {
            'norm_mix_pre': norm_mix_pre[l], 'norm_mix_post': norm_mix_post[l],
            'norm_ffn_pre': norm_ffn_pre[l], 'norm_ffn_post': norm_ffn_post[l],
            'w_mod': w_mod[l], 'b_mod': b_mod[l], 'w_in': w_in[l], 'attn_sink': attn_sink[l],
            'conv_w': conv_w[l], 'conv_b': conv_b[l],
            'filt_w1': filt_w1[l], 'filt_b1': filt_b1[l], 'filt_freq1': filt_freq1[l],
            'filt_w2': filt_w2[l], 'filt_b2': filt_b2[l], 'filt_freq2': filt_freq2[l],
            'filt_w3': filt_w3[l], 'filt_b3': filt_b3[l], 'filt_decay': filt_decay[l],
            'hyena_skip': hyena_skip[l], 'w_pa': w_pa[l], 'w_ph': w_ph[l],
            'w_gate': w_gate[l], 'b_gate': b_gate[l], 'w_o': w_o[l],
            'w_up': w_up[l], 'w_down': w_down[l],
        }
        y_prompt, k_ctx, v_ctx = layer(y_prompt, c_ctx[None, :], p, False, None, None)
        new_k_layers.append(k_ctx)
        new_v_layers.append(v_ctx)
        y_sample, _, _ = layer(y_sample, c, p, True, cache_k[:, l], cache_v[:, l])
    new_k = jnp.stack(new_k_layers, axis=1)
    new_v = jnp.stack(new_v_layers, axis=1)
    return (y_prompt, y_sample, new_k, new_v)
```

```python
import contextlib
import math
import numpy as np
import ml_dtypes
import concourse.bass as bass
import concourse.mybir as mybir
from concourse.bass_utils import run_bass_kernel_spmd

F32 = mybir.dt.float32
BF16 = mybir.dt.bfloat16
AF = mybir.ActivationFunctionType
ALU = mybir.AluOpType
BF = ml_dtypes.bfloat16

D = 1024
NH, NKV, HD, GRP = 8, 2, 128, 4
LS, LP = 2048, 256
NOWN = 1024
NKEY = 1152
NM = 1536
HW = 512
DFF = 2816
EPS = 1e-6
SCALE = HD ** -0.5
NEG = -30000.0

_CL = {}
_off = 0
for _n, _w in [("ccond", 16), ("gpre1", 8), ("gpre2", 8), ("bmodc", 32), ("convw", 36), ("convb", 12),
               ("bgate", 16), ("skip", 4), ("fvec", 4), ("negt2048", 16), ("negt256", 2),
               ("wP2048", 16), ("wQ2048", 16), ("wP256", 2), ("wQ256", 2), ("sink", 8)]:
    _CL[_n] = (_off, _w)
    _off += _w
NCOL = _off


class Res:
    __slots__ = ("name", "w", "r", "excl")

    def __init__(self, name, inherit=(), excl=False):
        self.name = name
        self.w = None
        self.r = list(inherit)
        self.excl = excl


class Tok:
    __slots__ = ("kind", "key", "val")

    def __init__(self, kind, key, val):
        self.kind, self.key, self.val = kind, key, val


class Sched:
    ENGS = ("pe", "act", "dve", "pool", "sp")

    def __init__(self, nc, ndma=8):
        self.nc = nc
        self.q = {e: [] for e in self.ENGS}
        self.n = {e: 0 for e in self.ENGS}
        self.pending = {e: [] for e in self.ENGS}
        self.seen = {e: {} for e in self.ENGS}
        self.ndma = ndma
        self.dma_i = {e: 0 for e in self.ENGS}
        self.dma_cnt = {}
        self.ninstr = 0

    def _need_waits(self, eng, toks):
        best = {}
        for t in toks:
            if t is None:
                continue
            if t.kind == "e":
                if t.key == eng and eng == "pe":
                    continue
                if t.val is None:
                    raise RuntimeError(f"dependency on pending token of {t.key} from {eng}")
            key = (t.kind, t.key)
            if self.seen[eng].get(key, 0) >= t.val:
                continue
            best[key] = max(best.get(key, 0), t.val)
        for k, v in best.items():
            self.seen[eng][k] = v
        return list(best.items())

    def _deps(self, eng, reads, writes):
        toks = []
        for r in reads:
            toks.append(r.w)
        for w in writes:
            if w.w is not None and not (w.w.kind == "e" and w.w.key == eng):
                toks.append(w.w)
            for t in w.r:
                if t.kind == "e" and t.key == eng:
                    continue
                toks.append(t)
        return toks

    def _finish(self, tok, reads, writes):
        for r in reads:
            r.r.append(tok)
            if len(r.r) > 24:
                r.r = _reduce(r.r)
        for w in writes:
            w.w = tok
            w.r = []
        self.ninstr += 1
        return tok

    def op(self, eng, fn, reads=(), writes=(), inc=True, extra=()):
        ex = [r for r in reads if r.excl]
        if ex:
            reads = [r for r in reads if not r.excl]
            writes = list(writes) + [r for r in ex if r not in writes]
        toks = self._deps(eng, reads, writes) + list(extra)
        waits = self._need_waits(eng, toks)
        if inc:
            self.n[eng] += 1
            tok = Tok("e", eng, self.n[eng])
            for p in self.pending[eng]:
                p.val = self.n[eng]
            self.pending[eng] = []
        else:
            tok = Tok("e", eng, None)
            self.pending[eng].append(tok)
        self.q[eng].append((waits, fn, ("e", eng) if inc else None))
        return self._finish(tok, reads, writes)

    def dma(self, eng, fn, reads=(), writes=(), extra=()):
        toks = self._deps(eng, reads, writes) + list(extra)
        slot = self.dma_i[eng] % self.ndma
        self.dma_i[eng] += 1
        key = (eng, slot)
        prev = self.dma_cnt.get(key, 0)
        if prev:
            toks.append(Tok("d", key, prev))
        waits = self._need_waits(eng, toks)
        self.dma_cnt[key] = prev + 16
        tok = Tok("d", key, prev + 16)
        self.q[eng].append((waits, fn, ("d", key)))
        return self._finish(tok, reads, writes)

    def wait_all(self, eng, toks):
        waits = self._need_waits(eng, [t for t in toks if t is not None])
        self.q[eng].append((waits, None, None))

    def run(self, stack):
        nc = self.nc
        semobj = {}
        for e in self.ENGS:
            if self.n[e] > 0:
                semobj[("e", e)] = stack.enter_context(nc.semaphore(f"s_{e}"))
        for key in self.dma_cnt:
            semobj[("d", key)] = stack.enter_context(nc.semaphore(f"d_{key[0]}_{key[1]}"))
        block = stack.enter_context(nc.Block())
        names = {"pe": "tensor", "act": "scalar", "dve": "vector", "pool": "gpsimd", "sp": "sync"}

        def make(e):
            q = self.q[e]

            def body(h):
                for waits, fn, inc in q:
                    for key, val in waits:
                        h.wait_ge(semobj[key], val)
                    if fn is None:
                        continue
                    ins = fn()
                    if inc is not None:
                        ins.then_inc(semobj[inc], 16 if inc[0] == "d" else 1)
            return body

        for e in self.ENGS:
            if self.q[e]:
                getattr(block, names[e])(make(e))


def _reduce(toks):
    best = {}
    out = []
    for t in toks:
        if t is None:
            continue
        if t.val is None:
            out.append(t)
            continue
        k = (t.kind, t.key)
        if k not in best or best[k].val < t.val:
            best[k] = t
    return out + list(best.values())


class Buf:
    def __init__(self, ap, off, nbytes, inherit):
        self.ap = ap
        self.off = off
        self.nbytes = nbytes
        self.inherit = inherit
        self.res = {}

    def R(self, key=0):
        r = self.res.get(key)
        if r is None:
            r = Res(key, self.inherit)
            self.res[key] = r
        return r

    def __getitem__(self, idx):
        return self.ap[idx]


class Arena:
    def __init__(self, base_ap, nbytes):
        self.base = base_ap
        self.free = [(0, nbytes)]
        self.hist = []

    def alloc(self, shape, dtype):
        esz = 4 if dtype == F32 else 2
        n = esz
        for s in shape[1:]:
            n *= s
        n = (n + 63) // 64 * 64
        for i, (s, e) in enumerate(self.free):
            if e - s >= n:
                off = s
                if e - s == n:
                    self.free.pop(i)
                else:
                    self.free[i] = (s + n, e)
                break
        else:
            raise RuntimeError(f"arena out of memory for {shape} ({n} B); free={self.free}")
        inherit = []
        keep = []
        for (hs, he, toks) in self.hist:
            if hs < off + n and off < he:
                inherit += toks
                if hs < off:
                    keep.append((hs, off, toks))
                if he > off + n:
                    keep.append((off + n, he, toks))
            else:
                keep.append((hs, he, toks))
        self.hist = keep
        ap = self.base[0:shape[0], off // 2:(off + n) // 2]
        if dtype == F32:
            ap = ap.bitcast(F32)
        cnt = 1
        for s in shape[1:]:
            cnt *= s
        ap = ap[:, 0:cnt]
        if len(shape) == 3:
            ap = ap.rearrange("p (a b) -> p a b", a=shape[1])
        elif len(shape) == 4:
            ap = ap.rearrange("p (a b c) -> p a b c", a=shape[1], b=shape[2])
        return Buf(ap, off, n, _reduce(inherit))

    def release(self, buf):
        toks = list(buf.inherit)
        for r in buf.res.values():
            toks.append(r.w)
            toks += r.r
        toks = _reduce(toks)
        self.hist.append((buf.off, buf.off + buf.nbytes, toks))
        self.free.append((buf.off, buf.off + buf.nbytes))
        self.free.sort()
        merged = []
        for s, e in self.free:
            if merged and merged[-1][1] == s:
                merged[-1] = (merged[-1][0], e)
            else:
                merged.append((s, e))
        self.free = merged


def build(debug=(), stop=None):
    nc = bass.Bass("TRN2", target_bir_lowering=False)
    st = contextlib.ExitStack()
    dbg_outs = {}

    def din(name, shape, dt=F32):
        return nc.dram_tensor(name, list(shape), dt, kind="ExternalInput").ap()

    def dout(name, shape, dt=F32):
        return nc.dram_tensor(name, list(shape), dt, kind="ExternalOutput").ap()

    xs_d = din("xs", [LS, D])
    xp_d = din("xp", [2 * LP, D])
    cols_d = din("cols", [128, NCOL])
    rows_d = din("rows", [128, 5, D])
    kc_d = din("kc", [LP, 256])
    vc_d = din("vc", [LP, 256])
    ropec_d = din("ropec", [128, NKEY])
    ropes_d = din("ropes", [128, NKEY])
    ident_d = din("ident", [128, 128], BF16)
    permf_d = din("permf", [128, 128])
    maskl_d = din("maskl", [128, 128], BF16)
    maskr_d = din("maskr", [128, 128], BF16)
    feat_d = {LS: din("feat2048", [17, LS]), LP: din("feat256", [17, LP])}
    cm_d = {LS: din("cm2048", [LS // 256, 128, LS // 128, 256], BF16), LP: din("cm256", [1, 128, 2, 256], BF16)}
    m2f_d = {LS: din("m2f2048", [LS // 256, 128, LS // 128, 256], BF16), LP: din("m2f256", [1, 128, 2, 256], BF16)}
    m2i_d = {LS: din("m2i2048", [NOWN // 256, 128, LS // 128, 256], BF16), LP: din("m2i256", [1, 128, 2, 256], BF16)}
    fw1_d = din("fw1", [17, 64])
    fw2_d = din("fw2", [64, 64])
    fw3_d = din("fw3", [65, 2 * HW])
    wmod_d = din("w_mod", [D, 6 * D])
    win_d = din("w_in", [D, 3072])
    wgate_d = din("w_gate", [D, 2 * D])
    wpa_d = din("w_pa", [D, D])
    wph_d = din("w_ph", [HW, D])
    wo_d = din("w_o", [D, D])
    wup_d = din("w_up", [D, 2 * DFF])
    wdown_d = din("w_down", [DFF, D])
    ys_d = dout("ys", [NOWN, D])
    yp_d = dout("yp", [2 * LP, D])
    nk_d = dout("nk", [2 * LP, 256])
    nv_d = dout("nv", [2 * LP, 256])
    spec_d = {LS: nc.dram_tensor("spec2048", [17, 128, 1024], BF16).ap(),
              LP: nc.dram_tensor("spec256", [3, 128, 1024], BF16).ap()}

    ARENA_BYTES = 207 * 1024
    arena_t = st.enter_context(nc.sbuf_tensor("arena", [128, ARENA_BYTES // 2], BF16))
    AR = Arena(arena_t[:, :], ARENA_BYTES)
    banks = [st.enter_context(nc.psum_tensor(f"bank{i}", [128, 512], F32)) for i in range(8)]
    BK = [Res(f"bank{i}", excl=True) for i in range(8)]
    S = Sched(nc)
    out_toks = []
    dram_res = {}

    def DR(name):
        if name not in dram_res:
            dram_res[name] = Res(name)
        return dram_res[name]

    def MM(out, lhsT, rhs, st_, sp_, rd, wr, inc=True):
        return S.op("pe", lambda: nc.tensor.matmul(out, lhsT=lhsT, rhs=rhs, start=st_, stop=sp_,
                                                   skip_group_check=True), reads=rd, writes=wr, inc=inc)

    def TR(out, in_, rd, wr, inc=True):
        return S.op("pe", lambda: nc.tensor.transpose(out, in_, ident[:, :]), reads=rd + [ident.R()], writes=wr, inc=inc)

    def ACT(out, in_, func, rd, wr, scale=None, bias=None, accum=None):
        kw = {}
        if scale is not None:
            kw["scale"] = scale
        if bias is not None:
            kw["bias"] = bias
        if accum is not None:
            kw["accum_out"] = accum
        return S.op("act", lambda: nc.scalar.activation(out=out, in_=in_, func=func, **kw), reads=rd, writes=wr)

    def ENG(e):
        return nc.vector if e == "dve" else nc.gpsimd

    def TT(e, out, a, b, op, rd, wr):
        return S.op(e, lambda: ENG(e).tensor_tensor(out=out, in0=a, in1=b, op=op), reads=rd, writes=wr)

    def TS(e, out, a, s1, s2, op0, op1, rd, wr):
        if op1 is None:
            return S.op(e, lambda: ENG(e).tensor_scalar(out=out, in0=a, scalar1=s1, scalar2=None, op0=op0), reads=rd, writes=wr)
        return S.op(e, lambda: ENG(e).tensor_scalar(out=out, in0=a, scalar1=s1, scalar2=s2, op0=op0, op1=op1), reads=rd, writes=wr)

    def STT(out, in0, scalar, in1, op0, op1, rd, wr):
        return S.op("dve", lambda: nc.vector.scalar_tensor_tensor(out=out, in0=in0, scalar=scalar, in1=in1, op0=op0, op1=op1),
                    reads=rd, writes=wr)

    def CP(e, out, in_, rd, wr):
        if e == "act":
            return S.op("act", lambda: nc.scalar.copy(out=out, in_=in_), reads=rd, writes=wr)
        return S.op(e, lambda: ENG(e).tensor_copy(out=out, in_=in_), reads=rd, writes=wr)

    def MSET(e, ap, val, wr):
        return S.op(e, lambda: ENG(e).memset(ap, val), writes=wr)

    def RECIP(out, in_, rd, wr):
        return S.op("dve", lambda: nc.vector.reciprocal(out=out, in_=in_), reads=rd, writes=wr)

    def DMA(e, out, in_, rd, wr):
        h = {"sp": nc.sync, "pool": nc.gpsimd, "act": nc.scalar}[e]
        return S.dma(e, lambda: h.dma_start(out=out, in_=in_), reads=rd, writes=wr)

    def DBG(name, buf, rd):
        if name not in debug:
            return
        shape = list(buf.ap.shape)
        dt = buf.ap.dtype
        d = nc.dram_tensor("dbg_" + name, shape, dt, kind="ExternalOutput").ap()
        dbg_outs[name] = d
        out_toks.append(DMA("sp", d, buf.ap, rd, [DR("dbg_" + name)]))

    def finish():
        S.wait_all("sp", out_toks)
        S.run(st)
        st.close()
        return nc, dbg_outs, S

    ident = AR.alloc([128, 128], BF16)
    permf = AR.alloc([128, 128], F32)
    maskl = AR.alloc([128, 128], BF16)
    maskr = AR.alloc([128, 128], BF16)
    cols = AR.alloc([128, NCOL], F32)
    zeros = AR.alloc([128, 128], F32)
    junk = AR.alloc([128, 1024], BF16)
    DMA("sp", ident.ap, ident_d, [], [ident.R()])
    DMA("sp", permf.ap, permf_d, [], [permf.R()])
    DMA("sp", maskl.ap, maskl_d, [], [maskl.R()])
    DMA("sp", maskr.ap, maskr_d, [], [maskr.R()])
    DMA("sp", cols.ap, cols_d, [], [cols.R()])
    MSET("dve", zeros.ap, 0.0, [zeros.R()])

    def col(name, j=0, n=1):
        o, w = _CL[name]
        return cols[:, o + j:o + j + n]

    NRING = 4
    ring = [AR.alloc([128, 8, 512], BF16) for _ in range(NRING)]
    ring_i = [0]

    def wload(dram_ap, kc, ncols):
        slot = ring[ring_i[0] % len(ring)]
        ring_i[0] += 1
        DMA("pool", slot[:, 0:kc, 0:ncols], dram_ap.rearrange("(k p) n -> p k n", p=128), [], [slot.R()])
        return slot

    wmod_pre = []
    for pi_, part_ in enumerate([0, 1]):
        for half_ in range(2):
            if len(wmod_pre) < 3:
                wmod_pre.append(wload(wmod_d[:, part_ * D + half_ * 512: part_ * D + (half_ + 1) * 512], 8, 512))
    fw1 = AR.alloc([17, 64], F32)
    fw2 = AR.alloc([64, 64], F32)
    fw3 = AR.alloc([65, 2 * HW], F32)
    adec = AR.alloc([128, 2 * HW], F32)
    fb = AR.alloc([64, 2], F32)
    DMA("sp", fw1.ap, fw1_d, [], [fw1.R()])
    DMA("sp", fw2.ap, fw2_d, [], [fw2.R()])
    DMA("sp", fw3.ap, fw3_d, [], [fw3.R()])
    fw3b = AR.alloc([65, 2 * HW], BF16)
    CP("dve", fw3b.ap, fw3.ap, [fw3.R()], [fw3b.R()])
    DMA("sp", adec.ap, rows_d[:, 4, :], [], [adec.R()])
    ACT(adec.ap, adec.ap, AF.Abs, [adec.R()], [adec.R()])
    fo = _CL["fvec"][0]
    fv = cols.ap
    TT("dve", fb[:, 0:1], fv[0:64, fo + 0:fo + 1], fv[0:64, fo + 1:fo + 2], ALU.mult, [cols.R()], [fb.R()])
    TT("dve", fb[:, 1:2], fv[0:64, fo + 2:fo + 3], fv[0:64, fo + 3:fo + 4], ALU.mult, [cols.R()], [fb.R()])

    def wrap_pi(a, ares, t, tres):
        TS("dve", t, a, -math.pi, 2 * math.pi, ALU.is_lt, ALU.mult, [ares], [tres])
        TT("dve", a, a, t, ALU.add, [ares, tres], [ares])
        TS("dve", t, a, math.pi, -2 * math.pi, ALU.is_gt, ALU.mult, [ares], [tres])
        TT("dve", a, a, t, ALU.add, [ares, tres], [ares])

    def filter_spectrum(L):
        nt = L // 128
        feat = AR.alloc([17, L], F32)
        nb_ = max(1, L // 512)
        h1s = [AR.alloc([64, 512], F32) for _ in range(nb_)]
        args = [AR.alloc([64, 512], F32) for _ in range(nb_)]
        h2 = AR.alloc([65, L], BF16)
        wtmps = [AR.alloc([64, 512], F32) for _ in range(nb_)]
        DMA("sp", feat.ap, feat_d[L], [], [feat.R()])
        MSET("dve", h2[64:65, :], 1.0, [h2.R()])
        nb = max(1, L // 512)
        bw = min(512, L)
        def blk(tb):
            return slice(tb * bw, (tb + 1) * bw), h1s[tb], args[tb], wtmps[tb], (2 * tb) % 8, (2 * tb + 1) % 8

        for tb in range(nb):
            sl, h1, arg, wtmp, b0, b1 = blk(tb)
            MM(banks[b0][0:64, 0:bw], fw1.ap, feat[:, sl], True, True, [fw1.R(), feat.R()], [BK[b0]])
            ACT(arg[:, 0:bw], banks[b0][0:64, 0:bw], AF.Identity, [BK[b0], cols.R(), fb.R()], [arg.R()],
                scale=fv[0:64, fo + 1:fo + 2], bias=fb[:, 0:1])
        yield "mlp"
        for tb in range(nb):
            sl, h1, arg, wtmp, b0, b1 = blk(tb)
            wrap_pi(arg[:, 0:bw], arg.R(), wtmp[:, 0:bw], wtmp.R())
        yield "mlp"
        for tb in range(nb):
            sl, h1, arg, wtmp, b0, b1 = blk(tb)
            ACT(h1[:, 0:bw], arg[:, 0:bw], AF.Sin, [arg.R()], [h1.R()])
            MM(banks[b1][0:64, 0:bw], fw2.ap, h1[:, 0:bw], True, True, [fw2.R(), h1.R()], [BK[b1]])
            ACT(arg[:, 0:bw], banks[b1][0:64, 0:bw], AF.Identity, [BK[b1], cols.R(), fb.R()], [arg.R()],
                scale=fv[0:64, fo + 3:fo + 4], bias=fb[:, 1:2])
        yield "mlp"
        for tb in range(nb):
            sl, h1, arg, wtmp, b0, b1 = blk(tb)
            wrap_pi(arg[:, 0:bw], arg.R(), wtmp[:, 0:bw], wtmp.R())
        yield "mlp"
        for tb in range(nb):
            sl, h1, arg, wtmp, b0, b1 = blk(tb)
            ACT(h2[0:64, sl], arg[:, 0:bw], AF.Sin, [arg.R()], [h2.R()])
        for b_ in h1s + args + wtmps:
            AR.release(b_)
        AR.release(feat)
        eT = AR.alloc([128, nt, HW], BF16)
        oT = AR.alloc([128, nt, HW], BF16)
        edec = [AR.alloc([128, 2 * HW], F32) for _ in range(2)]
        ftap = [AR.alloc([128, 2 * HW], F32) for _ in range(2)]
        ngt = "negt2048" if L == LS else "negt256"
        for j in range(nt):
            ed = edec[j % 2]
            ft = ftap[j % 2]
            ACT(ed.ap, adec.ap, AF.Exp, [adec.R(), cols.R()], [ed.R()], scale=col(ngt, j))
            for hlf in range(2):
                b = 2 + (2 * j + hlf) % 4
                MM(banks[b][:, :], h2[0:65, j * 128:(j + 1) * 128], fw3b[0:65, hlf * HW:(hlf + 1) * HW], True, True,
                   [h2.R(), fw3b.R()], [BK[b]])
                TT("dve", ft[:, hlf * HW:(hlf + 1) * HW], banks[b][:, :], ed[:, hlf * HW:(hlf + 1) * HW], ALU.mult,
                   [BK[b], ed.R()], [ft.R()])
            if j == 0:
                MSET("dve", ft[0:1, HW:2 * HW], 0.0, [ft.R()])
            TT("dve", eT[:, j, :], ft[:, 0:HW], ft[:, HW:2 * HW], ALU.add, [ft.R()], [eT.R(j)])
            TT("pool", oT[:, j, :], ft[:, 0:HW], ft[:, HW:2 * HW], ALU.subtract, [ft.R()], [oT.R(j)])
            if j % 4 == 3:
                yield "taps"
        for b_ in edec + ftap:
            AR.release(b_)
        AR.release(h2)
        yield "spec"
        wPn, wQn = ("wP2048", "wQ2048") if L == LS else ("wP256", "wQ256")
        nfp = L // 256
        dslots = [AR.alloc([128, nt, 256], BF16) for _ in range(4)]
        pq = [AR.alloc([128, 2, HW], BF16) for _ in range(2)]
        pl = AR.alloc([1, HW], F32)
        pv0 = AR.alloc([128, HW], BF16)
        di = 0
        for fp in range(nfp):
            cs = dslots[di % 4]
            ms = dslots[(di + 1) % 4]
            di += 2
            DMA("sp", cs.ap, cm_d[L][fp], [], [cs.R()])
            DMA("sp", ms.ap, m2f_d[L][fp], [], [ms.R()])
            for f2 in range(2):
                ft_i = fp * 2 + f2
                bP = 4 + (ft_i % 2) * 2
                bQ = bP + 1
                for k in range(nt):
                    MM(banks[bP][:, :], cs[:, k, f2 * 128:(f2 + 1) * 128], eT[:, k, :], k == 0, k == nt - 1,
                       [cs.R(), eT.R(k)], [BK[bP]], inc=(k == nt - 1))
                for k in range(nt):
                    MM(banks[bQ][:, :], ms[:, k, f2 * 128:(f2 + 1) * 128], oT[:, k, :], k == 0, k == nt - 1,
                       [ms.R(), oT.R(k)], [BK[bQ]], inc=(k == nt - 1))
                p_ = pq[ft_i % 2]
                ACT(p_[:, 0, :], banks[bP][:, :], AF.Copy, [BK[bP], cols.R()], [p_.R()], scale=col(wPn, ft_i))
                ACT(p_[:, 1, :], banks[bQ][:, :], AF.Copy, [BK[bQ], cols.R()], [p_.R()], scale=col(wQn, ft_i))
                DMA("act", spec_d[L][ft_i].rearrange("p (a c) -> p a c", a=2), p_.ap, [p_.R()], [DR(f"spec{L}")])
                if ft_i == 0:
                    for k in range(nt):
                        MM(banks[3][0:1, :], ms[:, k, 0:1], eT[:, k, :], k == 0, k == nt - 1,
                           [ms.R(), eT.R(k)], [BK[3]], inc=(k == nt - 1))
                    CP("dve", pv0.ap, p_[:, 0, :], [p_.R()], [pv0.R()])
                    ACT(pv0[0:1, :], banks[3][0:1, :], AF.Copy, [BK[3]], [pv0.R()], scale=1.0 / (2 * L))
                    DMA("act", spec_d[L][nt][:, 0:HW], pv0.ap, [pv0.R()], [DR(f"spec{L}")])
                yield "ft"
        for b_ in dslots + pq + [pl, pv0, eT, oT]:
            AR.release(b_)

    genS = filter_spectrum(LS)
    genP = filter_spectrum(LP)
    tS = tP = None
    while tS != "spec" or tP != "spec":
        if tS != "spec":
            tS = next(genS)
        if tP != "spec":
            tP = next(genP)
    for b_ in [fw1, fw2, fw3, fw3b, adec, fb]:
        AR.release(b_)

    pstate = {"lp": True}

    def pump(n=1):
        for _ in range(n):
            if pstate["lp"]:
                try:
                    next(genP)
                except StopIteration:
                    pstate["lp"] = False
            try:
                next(genS)
            except StopIteration:
                return
    pump(2)

    if stop == 'F':
        pump(100)
        return finish()
    scol = AR.alloc([128, 8, 2], BF16)
    srep = AR.alloc([128, 8, 2, 128], BF16)
    modc = AR.alloc([128, 4, 8, 2], F32)
    gain = AR.alloc([128, 2, 8, 2], F32)
    oc = _CL["ccond"][0]
    ACT(scol.ap.rearrange("p k c -> p (k c)"), cols[:, oc:oc + 16], AF.Silu, [cols.R()], [scol.R()])
    scolf = AR.alloc([128, 16], F32)
    ACT(scolf.ap, cols[:, oc:oc + 16], AF.Silu, [cols.R()], [scolf.R()])
    for k in range(8):
        for c in range(2):
            ACT(srep[:, k, c, :], zeros[:, 0:128], AF.Identity, [zeros.R(), scolf.R()], [srep.R()],
                bias=scolf[:, 2 * k + c:2 * k + c + 1], scale=1.0)
    parts = [0, 1, 3, 4]
    for pi, part in enumerate(parts):
        for half in range(2):
            slot = wmod_pre.pop(0) if wmod_pre else wload(wmod_d[:, part * D + half * 512: part * D + (half + 1) * 512], 8, 512)
            for f4 in range(4):
                fc = half * 4 + f4
                o_ = (pi * 8 + fc) * 2
                for k in range(8):
                    MM(banks[0][:, o_:o_ + 2], slot[:, k, f4 * 128:(f4 + 1) * 128], scol[:, k, :], k == 0, k == 7,
                       [slot.R(), scol.R()], [BK[0]], inc=(k == 7))
            pump(1)
    ob = _CL["bmodc"][0]
    for c in range(2):
        TT("dve", modc.ap.rearrange("p a k c -> p (a k) c")[:, :, c],
           banks[0][:, 0:64].rearrange("p (a c) -> p a c", c=2)[:, :, c],
           cols[:, ob:ob + 32], ALU.add, [BK[0], cols.R()], [modc.R()])
    for n_, (gname, sc_i) in enumerate([("gpre1", 1), ("gpre2", 3)]):
        og = _CL[gname][0]
        for c in range(2):
            STT(gain[:, n_, :, c], modc[:, sc_i, :, c], 1.0, cols[:, og:og + 8], ALU.add, ALU.mult,
                [modc.R(), cols.R()], [gain.R()])
    AR.release(scolf)

    if stop == '0':
        pump(100)
        return finish()
    def prenorm_run(groups, between=None):
        ssb = [AR.alloc([128, 8], F32) for _ in range(2)]
        xnb = [[AR.alloc([128, D], BF16) for _ in range(4)] for _ in range(2)]

        def part1(gi):
            G = groups[gi]
            srcs, src_res = G["srcs"]()
            G["n"] = len(srcs)
            ss = ssb[gi % 2]
            xn = xnb[gi % 2]
            for a, src in enumerate(srcs):
                S.op("dve", lambda src=src, acc=ss[:, a:a + 1]: nc.vector.scalar_tensor_tensor(
                    out=junk.ap, in0=src, scalar=1.0, in1=src, op0=ALU.mult, op1=ALU.mult, accum_out=acc),
                    reads=[src_res[a]], writes=[junk.R(), ss.R(a)])
            for a, src in enumerate(srcs):
                ACT(ss[:, a:a + 1], ss[:, a:a + 1], AF.Sqrt, [ss.R(a), epsc.R()], [ss.R(a)], scale=1.0 / D, bias=epsc[:, 0:1])
            for a, src in enumerate(srcs):
                RECIP(ss[:, 4 + a:5 + a], ss[:, a:a + 1], [ss.R(a)], [ss.R(a)])
            for a, src in enumerate(srcs):
                if a == 0:
                    TS("dve", xn[a].ap, src, ss[:, 4 + a:5 + a], 0.0, ALU.mult, ALU.add, [src_res[a], ss.R(a)], [xn[a].R()])
                else:
                    ACT(xn[a].ap, src, AF.Copy, [src_res[a], ss.R(a)], [xn[a].R()], scale=ss[:, 4 + a:5 + a])

        def part2(gi):
            G = groups[gi]
            nt_ = G["n"]
            xn = xnb[gi % 2]
            dst_aps, dst_res, n_idx, conds = G["dst_aps"], G["dst_res"], G["n_idx"], G["conds"]
            for a in range(nt_):
                for k in range(8):
                    b = k // 2
                    pT = banks[b][:, :].bitcast(BF16)
                    TR(pT[:, (k % 2) * 512 + a * 128:(k % 2) * 512 + (a + 1) * 128], xn[a][:, k * 128:(k + 1) * 128],
                       [xn[a].R()], [BK[b]], inc=(k % 2 == 1))
            c = conds[0]
            for k in range(8):
                b = k // 2
                pT = banks[b][:, :].bitcast(BF16)
                i_ap = pT[:, (k % 2) * 512:(k % 2) * 512 + nt_ * 128]
                sc_ap = gain[:, n_idx, k, c:c + 1]
                sh_ap = modc[:, 0 if n_idx == 0 else 2, k, c:c + 1]
                o_ap = dst_aps[k]
                if k % 2 == 0:
                    ACT(o_ap, i_ap, AF.Identity, [BK[b], gain.R(), modc.R()], [dst_res[k]], scale=sc_ap, bias=sh_ap)
                else:
                    TS("dve", o_ap, i_ap, sc_ap, sh_ap, ALU.mult, ALU.add, [BK[b], gain.R(), modc.R()], [dst_res[k]])

        part1(0)
        for gi in range(len(groups)):
            if gi + 1 < len(groups):
                part1(gi + 1)
            if between is not None:
                between()
            part2(gi)
        for b_ in ssb + xnb[0] + xnb[1]:
            AR.release(b_)

    epsc = AR.alloc([128, 1], F32)
    MSET("dve", epsc.ap, EPS, [epsc.R()])

    hTs = AR.alloc([128, 8, LS + 2], BF16)
    hTp = AR.alloc([128, 8, 2 * (LP + 2)], BF16)
    for k in range(8):
        MSET("dve", hTs[:, k, 0:1], 0.0, [hTs.R(k)])
        MSET("dve", hTs[:, k, LS + 1:LS + 2], 0.0, [hTs.R(k)])
        for s in range(2):
            MSET("dve", hTp[:, k, s * 258:s * 258 + 1], 0.0, [hTp.R(k)])
            MSET("dve", hTp[:, k, s * 258 + 257:s * 258 + 258], 0.0, [hTp.R(k)])
    xring = [AR.alloc([128, D], F32) for _ in range(8)]
    xi = [0]

    def load_x(dram_rows):
        t = xring[xi[0] % len(xring)]
        xi[0] += 1
        DMA("sp", t.ap, dram_rows, [], [t.R()])
        return t

    def loader(row_aps):
        def f():
            tiles = [load_x(r) for r in row_aps]
            return [t.ap for t in tiles], [t.R() for t in tiles]
        return f

    groups = []
    for g in range(4):
        groups.append(dict(srcs=loader([xs_d[(4 * g + a) * 128:(4 * g + a + 1) * 128, :] for a in range(4)]),
                           dst_aps=[hTs[:, k, 1 + g * 512:1 + (g + 1) * 512] for k in range(8)],
                           dst_res=[hTs.R(k) for k in range(8)], n_idx=0, conds=[1] * 4))
    for s_ in range(2):
        groups.append(dict(srcs=loader([xp_d[(2 * s_ + a) * 128:(2 * s_ + a + 1) * 128, :] for a in range(2)]),
                           dst_aps=[hTp[:, k, s_ * 258 + 1:s_ * 258 + 257] for k in range(8)],
                           dst_res=[hTp.R(k) for k in range(8)], n_idx=0, conds=[0] * 2))
    prenorm_run(groups, between=lambda: pump(1))
    pump(100)
    for t in xring:
        AR.release(t)
    DBG("hTs", hTs, [hTs.R(k) for k in range(8)])
    DBG("hTp", hTp, [hTp.R(k) for k in range(8)])

    hs_all = [hTs.R(k) for k in range(8)]
    hp_all = [hTp.R(k) for k in range(8)]

    if stop == '1':
        return finish()
    ring2 = [AR.alloc([128, 8, 512], BF16) for _ in range(3)]
    ring.extend(ring2)
    QT = AR.alloc([128, NH, NM], BF16)
    KTs = AR.alloc([128, NKV, NKEY], BF16)
    KTp = AR.alloc([128, NKV, 2 * LP], BF16)
    KTc = AR.alloc([128, NKV, LP], BF16)
    Vs = AR.alloc([128, 9, NKV, 129], BF16)
    Vp = AR.alloc([128, 4, NKV, 129], BF16)
    Vc = AR.alloc([128, 2, NKV, 129], BF16)
    ropec = AR.alloc([128, NKEY], F32)
    ropes = AR.alloc([128, NKEY], F32)
    DMA("sp", ropec.ap, ropec_d, [], [ropec.R()])
    DMA("sp", ropes.ap, ropes_d, [], [ropes.R()])
    for vb, n_ in ((Vs, 9), (Vp, 4), (Vc, 2)):
        MSET("dve", vb.ap.rearrange("p a g c -> p (a g) c")[:, :, 128:129], 1.0, [vb.R()])

    kcv = AR.alloc([128, 2, 2, 256], F32)
    kcb = AR.alloc([128, 2, 256], BF16)
    DMA("sp", kcv[:, 0, :, :], kc_d.rearrange("(a p) c -> p a c", p=128), [], [kcv.R(0)])
    DMA("sp", kcv[:, 1, :, :], vc_d.rearrange("(a p) c -> p a c", p=128), [], [kcv.R(1)])
    CP("act", kcb.ap, kcv[:, 0, :, :], [kcv.R(0)], [kcb.R()])
    for a in range(2):
        CP("dve", Vc[:, a, :, 0:128], kcv[:, 1, a, :].rearrange("p (g c) -> p g c", g=2), [kcv.R(1)], [Vc.R()])
    pT7 = banks[7][:, :].bitcast(BF16)
    for g in range(2):
        for a in range(2):
            TR(pT7[:, (g * 2 + a) * 128:(g * 2 + a + 1) * 128], kcb[:, a, g * 128:(g + 1) * 128], [kcb.R()], [BK[7]],
               inc=(g == 1 and a == 1))
    CP("dve", KTc.ap, pT7[:, 0:512].rearrange("p (g t) -> p g t", g=2), [BK[7]], [KTc.R()])
    AR.release(kcv)
    AR.release(kcb)

    rr = [0]
    qraw = [AR.alloc([128, 512], BF16) for _ in range(3)]
    rtmp = [AR.alloc([128, 512], F32) for _ in range(3)]
    rtmp2 = [AR.alloc([128, 512], F32) for _ in range(3)]
    ri = [0]
    permb = AR.alloc([128, 128], BF16)
    CP("dve", permb.ap, permf.ap, [permf.R()], [permb.R()])

    def pbank():
        b = rr[0] % 6
        rr[0] += 1
        return b

    deferred = []

    def flush_deferred():
        while deferred:
            deferred.pop(0)()

    def rope_evac(b, n, t0, out_ap, out_res):
        i = ri[0] % 3
        ri[0] += 1
        q_ = qraw[i]
        t_ = rtmp[i]
        t2 = rtmp2[i]
        CP("act", q_[:, 0:n], banks[b][:, 0:n], [BK[b]], [q_.R()])
        TT("dve", t_[:, 0:n], banks[b][:, 0:n], ropec[:, t0:t0 + n], ALU.mult, [BK[b], ropec.R()], [t_.R()])

        def later():
            pb = 6 + (i % 2)
            MM(banks[pb][:, 0:n], permb.ap, q_[:, 0:n], True, True, [permb.R(), q_.R()], [BK[pb]])
            TT("dve", t2[:, 0:n], banks[pb][:, 0:n], ropes[:, t0:t0 + n], ALU.mult, [BK[pb], ropes.R()], [t2.R()])
            TT("dve", out_ap, t_[:, 0:n], t2[:, 0:n], ALU.add, [t_.R(), t2.R()], [out_res])
        deferred.append(later)

    def proj_fm(slot, c0, rhs_fn, n, rd):
        b = pbank()
        for k in range(8):
            MM(banks[b][:, 0:n], slot[:, k, c0:c0 + 128], rhs_fn(k), k == 0, k == 7, [slot.R()] + rd, [BK[b]], inc=(k == 7))
        flush_deferred()
        return b

    slot = wload(win_d[:, 1024:1536], 8, 512)
    for g in range(2):
        for (t0, n) in ((0, 512), (512, 512), (1024, 128)):
            b = proj_fm(slot, g * 128, lambda k, t0=t0, n=n: hTs[:, k, 1 + t0:1 + t0 + n], n, hs_all)
            rope_evac(b, n, t0, KTs[:, g, t0:t0 + n], KTs.R(g))
        for s in range(2):
            b = proj_fm(slot, g * 128, lambda k, s=s: hTp[:, k, s * 258 + 1:s * 258 + 257], 256, hp_all)
            CP("act", KTp[:, g, s * 256:(s + 1) * 256], banks[b][:, 0:256], [BK[b]], [KTp.R(g)])
    kvout = [AR.alloc([128, 512], F32) for _ in range(2)]
    for i in range(4):
        s, a = i // 2, i % 2
        b = pbank()
        for k in range(8):
            MM(banks[b][:, :], hTp[:, k, s * 258 + 1 + a * 128:s * 258 + 1 + (a + 1) * 128], slot[:, k, :], k == 0, k == 7,
               [slot.R()] + hp_all, [BK[b]], inc=(k == 7))
        ko = kvout[i % 2]
        CP("act", ko.ap, banks[b][:, :], [BK[b]], [ko.R()])
        CP("dve", Vp[:, i, :, 0:128], banks[b][:, 256:512].rearrange("p (g c) -> p g c", g=2), [BK[b]], [Vp.R()])
        out_toks.append(DMA("sp", nk_d[i * 128:(i + 1) * 128, :], ko[:, 0:256], [ko.R()], [DR("nk")]))
        out_toks.append(DMA("sp", nv_d[i * 128:(i + 1) * 128, :], ko[:, 256:512], [ko.R()], [DR("nv")]))
    for i in range(9):
        b = pbank()
        for k in range(8):
            MM(banks[b][:, 0:256], hTs[:, k, 1 + i * 128:1 + (i + 1) * 128], slot[:, k, 256:512], k == 0, k == 7,
               [slot.R()] + hs_all, [BK[b]], inc=(k == 7))
        CP("dve", Vs[:, i, :, 0:128], banks[b][:, 0:256].rearrange("p (g c) -> p g c", g=2), [BK[b]], [Vs.R()])
    for ko in kvout:
        AR.release(ko)
    for hb in range(2):
        slot = wload(win_d[:, hb * 512:(hb + 1) * 512], 8, 512)
        for h4 in range(4):
            h = hb * 4 + h4
            for (t0, n) in ((0, 512), (512, 512)):
                b = proj_fm(slot, h4 * 128, lambda k, t0=t0, n=n: hTs[:, k, 1 + t0:1 + t0 + n], n, hs_all)
                rope_evac(b, n, t0, QT[:, h, t0:t0 + n], QT.R(h))
            for s in range(2):
                b = proj_fm(slot, h4 * 128, lambda k, s=s: hTp[:, k, s * 258 + 1:s * 258 + 257], 256, hp_all)
                CP("act", QT[:, h, NOWN + s * 256:NOWN + (s + 1) * 256], banks[b][:, 0:256], [BK[b]], [QT.R(h)])
    flush_deferred()
    for b_ in qraw + rtmp + rtmp2 + [ropec, ropes, permb]:
        AR.release(b_)
    DBG("QT", QT, [QT.R(h) for h in range(NH)])
    DBG("KTs", KTs, [KTs.R(g) for g in range(2)])
    DBG("Vs", Vs, [Vs.R()])

    x0 = AR.alloc([128, 4, NM], BF16)
    zs = AR.alloc([128, 4, NOWN], BF16)
    zso = AR.alloc([128, 4, NOWN], BF16)
    zp = AR.alloc([128, 4, 2 * LP], BF16)
    cacc = [AR.alloc([128, 512], F32) for _ in range(4)]
    ci = [0]
    ocw, ocb = _CL["convw"][0], _CL["convb"][0]

    def conv_from_psum(b, n, chg, out_ap, out_rd_wr):
        a_ = cacc[ci[0] % 4]
        ci[0] += 1
        w = lambda tap: cols[:, ocw + chg * 3 + tap:ocw + chg * 3 + tap + 1]
        ACT(a_[:, 0:n], banks[b][:, 1:n + 1], AF.Identity, [BK[b], cols.R()], [a_.R()], scale=w(1),
            bias=cols[:, ocb + chg:ocb + chg + 1])
        STT(a_[:, 0:n], banks[b][:, 0:n], w(0), a_[:, 0:n], ALU.mult, ALU.add, [BK[b], cols.R(), a_.R()], [a_.R()])
        if out_ap is None:
            STT(a_[:, 0:n], banks[b][:, 2:n + 2], w(2), a_[:, 0:n], ALU.mult, ALU.add, [BK[b], cols.R(), a_.R()], [a_.R()])
            return a_
        STT(out_ap, banks[b][:, 2:n + 2], w(2), a_[:, 0:n], ALU.mult, ALU.add, [BK[b], cols.R(), a_.R()], out_rd_wr)
        return None

    def hy_blocks(T0, T1):
        out = []
        s = T0
        while s < T1:
            n = min(510, T1 - s)
            out.append((s, n))
            s += n
        return out

    slot1 = wload(win_d[:, 2048:2560], 8, 512)
    slot2 = wload(win_d[:, 2560:3072], 8, 512)
    slot_x0 = wload(win_d[:, 1536:2048], 8, 512)
    for ch in range(4):
        segs = [("s", s, n) for (s, n) in hy_blocks(0, NOWN) + hy_blocks(NOWN, LS)] + [("p", 0, 256), ("p", 1, 256)]
        for kind, s, n in segs:
            if kind == "s":
                rhs = lambda k, s=s, n=n: hTs[:, k, s:s + n + 2]
                rd = hs_all
            else:
                rhs = lambda k, s=s: hTp[:, k, s * 258:s * 258 + 258]
                rd = hp_all
            b1 = proj_fm(slot1, ch * 128, rhs, n + 2, rd)
            b2 = proj_fm(slot2, ch * 128, rhs, n + 2, rd)
            a1 = conv_from_psum(b1, n, 4 + ch, None, None)
            a2 = conv_from_psum(b2, n, 8 + ch, None, None)
            if kind == "s":
                zb_, s_ = (zs, s) if s < NOWN else (zso, s - NOWN)
                TT("pool", zb_[:, ch, s_:s_ + n], a1[:, 0:n], a2[:, 0:n], ALU.mult, [a1.R(), a2.R()], [zb_.R(ch)])
            else:
                TT("pool", zp[:, ch, s * 256:(s + 1) * 256], a1[:, 0:n], a2[:, 0:n], ALU.mult, [a1.R(), a2.R()], [zp.R(ch)])
    slot = slot_x0
    for ch in range(4):
        for (s, n) in hy_blocks(0, NOWN):
            b = proj_fm(slot, ch * 128, lambda k, s=s, n=n: hTs[:, k, s:s + n + 2], n + 2, hs_all)
            conv_from_psum(b, n, ch, x0[:, ch, s:s + n], [x0.R(ch)])
        for s in range(2):
            b = proj_fm(slot, ch * 128, lambda k, s=s: hTp[:, k, s * 258:s * 258 + 258], 258, hp_all)
            conv_from_psum(b, 256, ch, x0[:, ch, NOWN + s * 256:NOWN + (s + 1) * 256], [x0.R(ch)])
    for a_ in cacc:
        AR.release(a_)
    for b_ in ring2:
        ring.remove(b_)
        AR.release(b_)
    AR.release(hTs)
    AR.release(hTp)
    DBG("x0", x0, [x0.R(c) for c in range(4)])
    DBG("zs", zs, [zs.R(c) for c in range(4)])
    DBG("zp", zp, [zp.R(c) for c in range(4)])
    zTs = AR.alloc([128, 16, HW], BF16)
    zTp = AR.alloc([128, 4, HW], BF16)

    def z_transpose(j):
        b = 2 if j % 2 == 0 else 6
        pT = banks[b][:, :].bitcast(BF16)
        for hf in range(2):
            for c2 in range(2):
                ch = hf * 2 + c2
                if j < 8:
                    src, sr = zs[:, ch, j * 128:(j + 1) * 128], zs.R(ch)
                elif j < 16:
                    src, sr = zso[:, ch, (j - 8) * 128:(j - 7) * 128], zso.R(ch)
                else:
                    src, sr = zp[:, ch, (j - 16) * 128:(j - 15) * 128], zp.R(ch)
                TR(pT[:, 768 + c2 * 128:768 + (c2 + 1) * 128], src, [sr], [BK[b]], inc=(c2 == 1))
            dst = zTs[:, j, hf * 256:(hf + 1) * 256] if j < 16 else zTp[:, j - 16, hf * 256:(hf + 1) * 256]
            dres = zTs.R(j) if j < 16 else zTp.R(j - 16)
            CP("act" if hf else "dve", dst, pT[:, 768:1024], [BK[b]], [dres])

    zt_pending = list(range(20))
    attnT = AR.alloc([128, NH, NM], BF16)
    esink = AR.alloc([128, 8], F32)
    ACT(esink.ap, col("sink", 0, 8), AF.Exp, [cols.R()], [esink.R()])
    esrow = AR.alloc([1, 8, 128], BF16)
    eunit = AR.alloc([1, 129], BF16)
    MSET("dve", eunit[0:1, 0:128], 0.0, [eunit.R()])
    MSET("dve", eunit[0:1, 128:129], 1.0, [eunit.R()])
    for h in range(8):
        ACT(esrow[0:1, h, :], zeros[0:1, 0:128], AF.Identity, [zeros.R(), esink.R()], [esrow.R()],
            bias=esink[0:1, h:h + 1], scale=1.0)
    PT = [AR.alloc([128, 5, 256], BF16) for _ in range(2)]
    Otok = [AR.alloc([128, 256], BF16) for _ in range(3)]
    dent = [AR.alloc([128, 4], F32) for _ in range(3)]
    it = [0]

    iters = []

    def attn_iter(heads, qcol, slots, tokcol):
        iters.append((heads, qcol, slots, tokcol))

    def stage_A(i):
        heads, qcol, slots, tokcol = iters[i]
        p = i % 2
        sb = [4 * p, 4 * p + 1, 4 * p + 2]
        P_ = PT[p]
        ns = len(slots)
        for si, (kt, v, mask, rd) in enumerate(slots):
            b = sb[si // 2]
            for hh, h in enumerate(heads):
                o_ = banks[b][:, (si % 2) * 256 + hh * 128:(si % 2) * 256 + (hh + 1) * 128]
                MM(o_, kt, QT[:, h, qcol:qcol + 128], True, mask is None, rd + [QT.R(h)], [BK[b]],
                   inc=(mask is None and hh == 1))
                if mask is not None:
                    MM(o_, ident.ap, mask.ap, False, True, [ident.R(), mask.R()], [BK[b]], inc=(hh == 1))
        for bi in range((ns + 1) // 2):
            w = min(2, ns - 2 * bi) * 256
            ACT(P_[:, 2 * bi:2 * bi + w // 256, :].rearrange("p a c -> p (a c)"), banks[sb[bi]][:, 0:w], AF.Exp,
                [BK[sb[bi]]], [P_.R()], scale=SCALE)

    def stage_B(i):
        heads, qcol, slots, tokcol = iters[i]
        p = i % 2
        ob = 4 * p + 3
        P_ = PT[p]
        ns = len(slots)
        for hh, h in enumerate(heads):
            for si, (kt, v, mask, rd) in enumerate(slots):
                MM(banks[ob][:, hh * 129:(hh + 1) * 129], P_[:, si, hh * 128:(hh + 1) * 128], v, si == 0, si == ns - 1,
                   [P_.R()] + rd, [BK[ob]], inc=(si == ns - 1))
        d_ = dent[i % 3]
        O_ = Otok[i % 3]
        h0 = heads[0]
        TT("dve", d_[:, 0:2], banks[ob][:, 0:258].rearrange("p (h c) -> p h c", h=2)[:, :, 128],
           esink[:, h0:h0 + 2], ALU.add, [BK[ob], esink.R()], [d_.R()])
        RECIP(d_[:, 2:4], d_[:, 0:2], [d_.R()], [d_.R()])
        for hh in range(2):
            TS("dve", O_[:, hh * 128:(hh + 1) * 128], banks[ob][:, hh * 129:hh * 129 + 128], d_[:, 2 + hh:3 + hh], 0.0,
               ALU.mult, ALU.add, [BK[ob], d_.R()], [O_.R()])

    def stage_T(i):
        heads, qcol, slots, tokcol = iters[i]
        p = i % 2
        tb = 4 * p + 2
        O_ = Otok[i % 3]
        h0 = heads[0]
        pT = banks[tb][:, :].bitcast(BF16)
        for hh in range(2):
            TR(pT[:, 512 + hh * 128:512 + (hh + 1) * 128], O_[:, hh * 128:(hh + 1) * 128], [O_.R()], [BK[tb]], inc=(hh == 1))
        CP("dve", attnT[:, h0:h0 + 2, tokcol:tokcol + 128], pT[:, 512:768].rearrange("p (h t) -> p h t", h=2),
           [BK[tb]], [attnT.R(h0)])

    for g in range(2):
        for qb in range(8):
            slots = []
            for kb, mask in ((qb - 1, maskl), (qb, None), (qb + 1, maskr)):
                if kb < 0:
                    continue
                slots.append((KTs[:, g, kb * 128:(kb + 1) * 128], Vs[:, kb, g, :], mask, [KTs.R(g), Vs.R()]))
            for cb in range(2):
                slots.append((KTc[:, g, cb * 128:(cb + 1) * 128], Vc[:, cb, g, :], None, [KTc.R(), Vc.R()]))
            for hp_ in range(2):
                heads = [g * 4 + hp_ * 2, g * 4 + hp_ * 2 + 1]
                attn_iter(heads, qb * 128, slots, qb * 128)
    for s in range(2):
        for g in range(2):
            slots = [(KTp[:, g, s * 256 + kb * 128:s * 256 + (kb + 1) * 128], Vp[:, s * 2 + kb, g, :], None, [KTp.R(g), Vp.R()])
                     for kb in range(2)]
            for qb in range(2):
                for hp_ in range(2):
                    heads = [g * 4 + hp_ * 2, g * 4 + hp_ * 2 + 1]
                    c0 = NOWN + s * 256 + qb * 128
                    attn_iter(heads, c0, slots, c0)
    nit = len(iters)
    for step in range(nit + 3):
        if step < nit:
            stage_A(step)
        if 0 <= step - 3 < nit:
            stage_T(step - 3)
        if 0 <= step - 1 < nit:
            stage_B(step - 1)
        if zt_pending and step % 2 == 1:
            z_transpose(zt_pending.pop(0))
    while zt_pending:
        z_transpose(zt_pending.pop(0))
    AR.release(zso)
    for b_ in PT + Otok + dent + [QT, KTs, KTp, KTc, Vs, Vp, Vc, esink, esrow, eunit]:
        AR.release(b_)
    DBG("attnT", attnT, [attnT.R(h) for h in range(0, NH, 2)])

    if stop == '4':
        return finish()
    hyT = AR.alloc([128, 4, NM], BF16)
    osk = _CL["skip"][0]

    def hyena_dft(L, zT, zfm, zfm_c0, tok0, nown, after_fwd=None):
        nt = L // 128
        Ub = AR.alloc([128, nt, HW], BF16)
        Vb = AR.alloc([128, nt, HW], BF16)
        dsl = [AR.alloc([128, nt, 256], BF16) for _ in range(4)]
        pqs = [AR.alloc([128, 2, HW], BF16) for _ in range(2)]
        pv0 = AR.alloc([128, HW], BF16)
        mm_ = [AR.alloc([128, 4, HW], F32) for _ in range(1)]
        DMA("sp", pv0.ap, spec_d[L][nt][:, 0:HW], [DR(f"spec{L}")], [pv0.R()])
        di = 0
        nfp = L // 256
        loads = {}

        def issue(fp):
            nonlocal di
            cs, ms = dsl[di % 4], dsl[(di + 1) % 4]
            di += 2
            DMA("sp", cs.ap, cm_d[L][fp], [], [cs.R()])
            DMA("sp", ms.ap, m2f_d[L][fp], [], [ms.R()])
            loads[fp] = (cs, ms)

        issue(0)
        yield "fwd"
        for fp in range(nfp):
            if fp + 1 < nfp:
                issue(fp + 1)
            cs, ms = loads.pop(fp)
            for f2 in range(2):
                fi = fp * 2 + f2
                p_ = pqs[fi % 2]
                DMA("sp", p_.ap, spec_d[L][fi].rearrange("p (a c) -> p a c", a=2), [DR(f"spec{L}")], [p_.R()])
                bA = (fi % 2) * 2
                bB = bA + 1
                for k in range(nt):
                    MM(banks[bA][:, :], cs[:, k, f2 * 128:(f2 + 1) * 128], zT[:, k, :], k == 0, k == nt - 1,
                       [cs.R(), zT.R(k)], [BK[bA]], inc=(k == nt - 1))
                for k in range(nt):
                    MM(banks[bB][:, :], ms[:, k, f2 * 128:(f2 + 1) * 128], zT[:, k, :], k == 0, k == nt - 1,
                       [ms.R(), zT.R(k)], [BK[bB]], inc=(k == nt - 1))
                m_ = mm_[0]
                pv = pv0.ap if fi == 0 else p_[:, 0, :]
                pvr = [pv0.R()] if fi == 0 else []
                TT("dve", m_[:, 0, :], banks[bA][:, :], p_[:, 0, :], ALU.mult, [BK[bA], p_.R()], [m_.R(0)])
                TT("dve", m_[:, 3, :], banks[bA][:, :], p_[:, 1, :], ALU.mult, [BK[bA], p_.R()], [m_.R(3)])
                TT("dve", m_[:, 1, :], banks[bB][:, :], p_[:, 1, :], ALU.mult, [BK[bB], p_.R()], [m_.R(1)])
                TT("dve", m_[:, 2, :], banks[bB][:, :], pv, ALU.mult, [BK[bB], p_.R()] + pvr, [m_.R(2)])
                TT("dve", Ub[:, fi, :], m_[:, 0, :], m_[:, 1, :], ALU.subtract, [m_.R(0), m_.R(1)], [Ub.R(fi)])
                TT("dve", Vb[:, fi, :], m_[:, 2, :], m_[:, 3, :], ALU.add, [m_.R(2), m_.R(3)], [Vb.R(fi)])
                yield "fwd"
        for b_ in pqs + [pv0] + mm_:
            AR.release(b_)
        if after_fwd is not None:
            after_fwd()
        ytmp = [AR.alloc([128, 256], F32) for _ in range(2)]
        yi = 0
        ntq = nown // 256
        iloads = {}

        def issue_inv(tq):
            nonlocal di
            cs, ms = dsl[di % 4], dsl[(di + 1) % 4]
            di += 2
            DMA("sp", cs.ap, cm_d[L][tq], [], [cs.R()])
            DMA("sp", ms.ap, m2i_d[L][tq], [], [ms.R()])
            iloads[tq] = (cs, ms)

        issue_inv(0)
        yield "inv"
        for tq in range(ntq):
            if tq + 1 < ntq:
                issue_inv(tq + 1)
            cs, ms = iloads.pop(tq)
            for ct in range(4):
                b = 4 + (yi % 4)
                for k in range(2 * nt):
                    src = cs if k < nt else ms
                    uvb = Ub if k < nt else Vb
                    MM(banks[b][:, 0:256], uvb[:, k % nt, ct * 128:(ct + 1) * 128], src[:, k % nt, :], k == 0, k == 2 * nt - 1,
                       [uvb.R(k % nt), src.R()], [BK[b]], inc=(k == 2 * nt - 1))
                y_ = ytmp[yi % 2]
                yi += 1
                zsrc, zres = zfm(ct, tq * 256)
                STT(y_.ap, zsrc, cols[:, osk + ct:osk + ct + 1], banks[b][:, 0:256], ALU.mult, ALU.add,
                    [zres, cols.R(), BK[b]], [y_.R()])
                TT("dve", hyT[:, ct, tok0 + tq * 256:tok0 + (tq + 1) * 256], y_.ap,
                   x0[:, ct, tok0 + tq * 256:tok0 + (tq + 1) * 256], ALU.mult, [y_.R(), x0.R(ct)], [hyT.R(ct)])
                if ct % 2 == 1:
                    yield "inv"
        for b_ in ytmp + dsl + [Ub, Vb]:
            AR.release(b_)

    class _V:
        def __init__(self, s):
            self.s = s

        def __getitem__(self, idx):
            p, k, c = idx
            return zTp[p, 2 * self.s + k, c]

        def R(self, k):
            return zTp.R(2 * self.s + k)

    def prompt_gens():
        for s_ in range(2):
            yield from hyena_dft(LP, _V(s_), lambda ct, c0, s_=s_: (zp[:, ct, s_ * 256 + c0:s_ * 256 + c0 + 256], zp.R(ct)),
                                 0, NOWN + s_ * 256, LP)

    pg = prompt_gens()
    for tag in hyena_dft(LS, zTs, lambda ct, c0: (zs[:, ct, c0:c0 + 256], zs.R(ct)), 0, 0, NOWN,
                         after_fwd=lambda: AR.release(zTs)):
        if tag == "inv":
            try:
                next(pg)
                next(pg)
            except StopIteration:
                pass
    for _ in pg:
        pass
    for b_ in [zTp, zs, zp, x0]:
        AR.release(b_)
    DBG("hyT", hyT, [hyT.R(c) for c in range(4)])


    if stop == '5':
        return finish()
    hTm = AR.alloc([128, 8, NM], BF16)
    ring_extra = [AR.alloc([128, 8, 512], BF16) for _ in range(2)]
    ring.extend(ring_extra)
    xring = [AR.alloc([128, D], F32) for _ in range(8)]
    xi[0] = 0

    def main_rows(i):
        if i < 8:
            return xs_d[i * 128:(i + 1) * 128, :]
        return xp_d[(i - 8) * 128:(i - 7) * 128, :]

    groups = []
    for g in range(3):
        groups.append(dict(srcs=loader([main_rows(4 * g + a) for a in range(4)]),
                           dst_aps=[hTm[:, k, g * 512:(g + 1) * 512] for k in range(8)],
                           dst_res=[hTm.R(k) for k in range(8)], n_idx=0, conds=[1 if g < 2 else 0] * 4))
    hy_all = [hyT.R(c) for c in range(4)]
    phT = AR.alloc([128, 8, NM], BF16)
    ph_todo = [0, 1]

    def ph_block():
        if not ph_todo:
            return
        fb2 = ph_todo.pop(0)
        s_ph = wload(wph_d[:, fb2 * 512:(fb2 + 1) * 512], 4, 512)
        for f4 in range(4):
            fc = fb2 * 4 + f4
            for tb in range(3):
                tsl = slice(tb * 512, (tb + 1) * 512)
                b = 4 + pbank() % 4
                for k in range(4):
                    MM(banks[b][:, :], s_ph[:, k, f4 * 128:(f4 + 1) * 128], hyT[:, k, tsl], k == 0, k == 3,
                       [s_ph.R()] + hy_all, [BK[b]], inc=(k == 3))
                CP("act" if tb % 2 else "dve", phT[:, fc, tsl], banks[b][:, :], [BK[b]], [phT.R(fc)])

    prenorm_run(groups, between=ph_block)
    for t in xring:
        AR.release(t)
    hm_all = [hTm.R(k) for k in range(8)]
    at_all = [attnT.R(h) for h in range(0, NH, 2)]
    while ph_todo:
        ph_block()
    for fb2 in range(0):
        s_ph = wload(wph_d[:, fb2 * 512:(fb2 + 1) * 512], 4, 512)
        for f4 in range(4):
            fc = fb2 * 4 + f4
            for tb in range(3):
                tsl = slice(tb * 512, (tb + 1) * 512)
                b = pbank()
                for k in range(4):
                    MM(banks[b][:, :], s_ph[:, k, f4 * 128:(f4 + 1) * 128], hyT[:, k, tsl], k == 0, k == 3,
                       [s_ph.R()] + hy_all, [BK[b]], inc=(k == 3))
                CP("act" if tb % 2 else "dve", phT[:, fc, tsl], banks[b][:, :], [BK[b]], [phT.R(fc)])
    AR.release(hyT)
    mergedT = AR.alloc([128, 8, NM], BF16)
    sg = [AR.alloc([128, 4, 512], F32) for _ in range(2)]
    obg = _CL["bgate"][0]
    mi = 0
    for fb2 in range(2):
        s_pa = wload(wpa_d[:, fb2 * 512:(fb2 + 1) * 512], 8, 512)
        s_ga = wload(wgate_d[:, fb2 * 512:(fb2 + 1) * 512], 8, 512)
        s_gh = wload(wgate_d[:, D + fb2 * 512:D + (fb2 + 1) * 512], 8, 512)
        for f4 in range(4):
            fc = fb2 * 4 + f4
            for tb in range(3):
                tsl = slice(tb * 512, (tb + 1) * 512)
                base = 4 * (mi % 2)
                g_ = sg[mi % 2]
                mi += 1
                for k in range(8):
                    MM(banks[base][:, :], s_pa[:, k, f4 * 128:(f4 + 1) * 128], attnT[:, k, tsl], k == 0, k == 7,
                       [s_pa.R()] + at_all, [BK[base]], inc=(k == 7))
                for k in range(8):
                    MM(banks[base + 1][:, :], s_ga[:, k, f4 * 128:(f4 + 1) * 128], hTm[:, k, tsl], k == 0, k == 7,
                       [s_ga.R()] + hm_all, [BK[base + 1]], inc=(k == 7))
                for k in range(8):
                    MM(banks[base + 2][:, :], s_gh[:, k, f4 * 128:(f4 + 1) * 128], hTm[:, k, tsl], k == 0, k == 7,
                       [s_gh.R()] + hm_all, [BK[base + 2]], inc=(k == 7))
                ACT(g_[:, 0, :], banks[base + 1][:, :], AF.Sigmoid, [BK[base + 1], cols.R()], [g_.R(0)],
                    bias=cols[:, obg + fc:obg + fc + 1])
                ACT(g_[:, 1, :], banks[base + 2][:, :], AF.Sigmoid, [BK[base + 2], cols.R()], [g_.R(1)],
                    bias=cols[:, obg + 8 + fc:obg + 8 + fc + 1])
                TT("dve", g_[:, 2, :], banks[base][:, :], g_[:, 0, :], ALU.mult, [BK[base], g_.R(0)], [g_.R(2)])
                TT("dve", g_[:, 3, :], phT[:, fc, tsl], g_[:, 1, :], ALU.mult, [phT.R(fc), g_.R(1)], [g_.R(3)])
                TT("dve", mergedT[:, fc, tsl], g_[:, 2, :], g_[:, 3, :], ALU.add, [g_.R(2), g_.R(3)], [mergedT.R(fc)])
    for b_ in sg + [phT, hTm, attnT]:
        AR.release(b_)
    mg_all = [mergedT.R(fc) for fc in range(8)]
    DBG("mergedT", mergedT, mg_all)

    if stop == '6':
        return finish()
    def gate_rows(part, brow_i, nrow_i):
        G = AR.alloc([128, 2, D], F32)
        rw = AR.alloc([128, 2, D], F32)
        DMA("sp", rw[:, 0, :], rows_d[:, brow_i, :], [], [rw.R(0)])
        DMA("sp", rw[:, 1, :], rows_d[:, nrow_i, :], [], [rw.R(1)])
        for half in range(2):
            slot = wload(wmod_d[:, part * D + half * 512:part * D + (half + 1) * 512], 8, 512)
            for c in range(2):
                b = pbank()
                for k in range(8):
                    MM(banks[b][:, :], srep[:, k, c, :], slot[:, k, :], k == 0, k == 7, [srep.R(), slot.R()], [BK[b]],
                       inc=(k == 7))
                hs = slice(half * 512, (half + 1) * 512)
                TT("dve", G[:, c, hs], banks[b][:, :], rw[:, 0, hs], ALU.add, [BK[b], rw.R(0)], [G.R(c)])
                TT("dve", G[:, c, hs], G[:, c, hs], rw[:, 1, hs], ALU.mult, [G.R(c), rw.R(1)], [G.R(c)])
        AR.release(rw)
        return G

    pn_sets = []
    pn_i = [0]

    def postnorm_alloc(n=3):
        for _ in range(n):
            pn_sets.append((AR.alloc([128, 4], F32), AR.alloc([128, D], F32)))

    def postnorm_free():
        for ss_, tmp_ in pn_sets:
            AR.release(ss_)
            AR.release(tmp_)
        pn_sets.clear()

    def postnorm_residual(bk0, bk1, G, cond, xtile, xres):
        ss, tmp = pn_sets[pn_i[0] % len(pn_sets)]
        pn_i[0] += 1
        ACT(junk[:, 0:512], banks[bk0][:, :], AF.Square, [BK[bk0]], [junk.R(), ss.R()], accum=ss[:, 0:1])
        ACT(junk[:, 512:1024], banks[bk1][:, :], AF.Square, [BK[bk1]], [junk.R(), ss.R()], accum=ss[:, 1:2])
        TT("dve", ss[:, 2:3], ss[:, 0:1], ss[:, 1:2], ALU.add, [ss.R()], [ss.R()])
        ACT(ss[:, 2:3], ss[:, 2:3], AF.Sqrt, [ss.R(), epsc.R()], [ss.R()], scale=1.0 / D, bias=epsc[:, 0:1])
        RECIP(ss[:, 3:4], ss[:, 2:3], [ss.R()], [ss.R()])
        STT(tmp[:, 0:512], banks[bk0][:, :], ss[:, 3:4], G[:, cond, 0:512], ALU.mult, ALU.mult, [BK[bk0], ss.R(), G.R(cond)], [tmp.R()])
        STT(tmp[:, 512:1024], banks[bk1][:, :], ss[:, 3:4], G[:, cond, 512:1024], ALU.mult, ALU.mult, [BK[bk1], ss.R(), G.R(cond)], [tmp.R()])
        TT("dve", xtile, xtile, tmp.ap, ALU.add, [tmp.R(), xres], [xres])

    G1 = gate_rows(2, 2, 0)
    x1 = AR.alloc([128, 12, D], F32)
    for i in range(12):
        DMA("sp", x1[:, i, :], main_rows(i), [], [x1.R(i)])
    postnorm_alloc(3)
    wo = [wload(wo_d[:, half * 512:(half + 1) * 512], 8, 512) for half in range(2)]
    for i in range(12):
        bb = [(2 * i) % 8, (2 * i + 1) % 8]
        for half in range(2):
            for k in range(8):
                MM(banks[bb[half]][:, :], mergedT[:, k, i * 128:(i + 1) * 128], wo[half][:, k, :], k == 0, k == 7,
                   [wo[half].R()] + mg_all, [BK[bb[half]]], inc=(k == 7))
        postnorm_residual(bb[0], bb[1], G1, 1 if i < 8 else 0, x1[:, i, :], x1.R(i))
    postnorm_free()
    AR.release(mergedT)
    AR.release(G1)
    DBG("x1", x1, [x1.R(i) for i in range(12)])

    if stop == '7':
        return finish()
    h2T = AR.alloc([128, 8, NM], BF16)
    groups = []
    for g in range(3):
        groups.append(dict(srcs=(lambda g=g: ([x1[:, 4 * g + a, :] for a in range(4)], [x1.R(4 * g + a) for a in range(4)])),
                           dst_aps=[h2T[:, k, g * 512:(g + 1) * 512] for k in range(8)],
                           dst_res=[h2T.R(k) for k in range(8)], n_idx=1, conds=[1 if g < 2 else 0] * 4))
    g2box = []

    def g2_once():
        if not g2box:
            g2box.append(gate_rows(5, 3, 1))

    prenorm_run(groups, between=g2_once)
    h2_all = [h2T.R(k) for k in range(8)]

    for b_ in ring_extra:
        ring.remove(b_)
        AR.release(b_)
    aT_bufs = []
    aT_map = []
    need = 22
    while need > 0:
        best = max(AR.free, key=lambda se: se[1] - se[0])
        can = min(need, (best[1] - best[0]) // (NM * 2))
        assert can > 0, ("arena too fragmented for aT", AR.free)
        b_ = AR.alloc([128, can, NM], BF16)
        for j_ in range(can):
            aT_map.append((b_, j_))
        aT_bufs.append(b_)
        need -= can
    sil = [AR.alloc([128, 512], F32) for _ in range(2)]
    ui = 0
    halves = []
    hres_all = []
    for sl_ in ring:
        full = sl_.R()
        toks = _reduce([full.w] + full.r)
        hr = [Res("h0", toks), Res("h1", toks)]
        hres_all.append((sl_, hr))
        halves += [(sl_[:, :, 0:256], hr[0]), (sl_[:, :, 256:512], hr[1])]
    hi_ = [0]

    def hload(dram_ap):
        ap_, r_ = halves[hi_[0] % len(halves)]
        hi_[0] += 1
        DMA("pool", ap_, dram_ap.rearrange("(k p) n -> p k n", p=128), [], [r_])
        return ap_, r_

    for g2 in range(11):
        ga, gr = hload(wup_d[:, g2 * 256:(g2 + 1) * 256])
        ua, ur = hload(wup_d[:, DFF + g2 * 256:DFF + (g2 + 1) * 256])
        for f2 in range(2):
            fc = g2 * 2 + f2
            for tb in range(3):
                tsl = slice(tb * 512, (tb + 1) * 512)
                base = 2 * (ui % 4)
                s_ = sil[ui % 2]
                ui += 1
                for k in range(8):
                    MM(banks[base][:, :], ga[:, k, f2 * 128:(f2 + 1) * 128], h2T[:, k, tsl], k == 0, k == 7,
                       [gr] + h2_all, [BK[base]], inc=(k == 7))
                for k in range(8):
                    MM(banks[base + 1][:, :], ua[:, k, f2 * 128:(f2 + 1) * 128], h2T[:, k, tsl], k == 0, k == 7,
                       [ur] + h2_all, [BK[base + 1]], inc=(k == 7))
                ACT(s_.ap, banks[base][:, :], AF.Silu, [BK[base]], [s_.R()])
                TT("dve", aT_map[fc][0][:, aT_map[fc][1], tsl], banks[base + 1][:, :], s_.ap, ALU.mult, [BK[base + 1], s_.R()],
                   [aT_map[fc][0].R(aT_map[fc][1])])
    for sl_, hr in hres_all:
        full = sl_.R()
        toks = list(full.r) + [full.w]
        for h_ in hr:
            toks += [h_.w] + h_.r
        full.r = _reduce(toks)
    for b_ in sil + [h2T]:
        AR.release(b_)
    a_all = [aT_map[fc][0].R(aT_map[fc][1]) for fc in range(22)]

    if stop == '9':
        return finish()
    G2 = g2box[0]
    postnorm_alloc(2)
    xs_ = [AR.alloc([128, 8, 512], BF16) for _ in range(3)]
    wd = {}
    for kg in range(3):
        kc = 8 if kg < 2 else 6
        for half in range(2):
            if half == 0:
                wd[(kg, half)] = wload(wdown_d[kg * 1024:kg * 1024 + kc * 128, 0:512], kc, 512)
            else:
                sl_ = xs_[kg]
                DMA("pool", sl_[:, 0:kc, :], wdown_d[kg * 1024:kg * 1024 + kc * 128, 512:1024].rearrange("(k p) n -> p k n", p=128),
                    [], [sl_.R()])
                wd[(kg, half)] = sl_
    for i in range(12):
        bb = [(2 * i) % 8, (2 * i + 1) % 8]
        for half in range(2):
            for k in range(22):
                w_ = wd[(k // 8, half)]
                MM(banks[bb[half]][:, :], aT_map[k][0][:, aT_map[k][1], i * 128:(i + 1) * 128], w_[:, k % 8, :], k == 0, k == 21,
                   [w_.R()] + a_all, [BK[bb[half]]], inc=(k == 21))
        postnorm_residual(bb[0], bb[1], G2, 1 if i < 8 else 0, x1[:, i, :], x1.R(i))
        if i < 8:
            out_toks.append(DMA("sp", ys_d[i * 128:(i + 1) * 128, :], x1[:, i, :], [x1.R(i)], [DR("ys")]))
        else:
            out_toks.append(DMA("sp", yp_d[(i - 8) * 128:(i - 7) * 128, :], x1[:, i, :], [x1.R(i)], [DR("yp")]))

    return finish()


def _colform(v):
    v = np.asarray(v, np.float32).reshape(-1, 128)
    return np.ascontiguousarray(v.T)


def _consts():
    c = {}
    c["ident"] = np.eye(128, dtype=np.float32).astype(BF)
    perm = np.zeros((128, 128), np.float32)
    for m in range(128):
        blk, i = divmod(m, 64)
        src = blk * 64 + (i + 32) % 64
        perm[src, m] = 1.0
    c["permf"] = perm
    j = np.arange(128)[:, None]
    i = np.arange(128)[None, :]
    c["maskl"] = np.where(j >= i, 0.0, NEG).astype(np.float32).astype(BF)
    c["maskr"] = np.where(j <= i, 0.0, NEG).astype(np.float32).astype(BF)
    for L in (LS, LP):
        t = np.arange(L, dtype=np.float32)
        bands = np.linspace(1e-4, 7, 8).astype(np.float32)
        ang = (2.0 * np.float32(math.pi) * t / np.float32(L))[:, None] * bands[None, :]
        feat = np.concatenate([(t / np.float32(L))[:, None], np.cos(ang), np.sin(ang)], axis=-1).astype(np.float32)
        c[f"feat{L}"] = np.ascontiguousarray(feat.T)
        ii = np.arange(L, dtype=np.int64)
        prod = np.outer(ii, ii) % (2 * L)
        th = prod.astype(np.float64) * (math.pi / L)
        cm = np.cos(th)
        m2 = np.sin(th)
        m2[:, 0] = (-1.0) ** ii
        nown = NOWN if L == LS else LP

        def tile_(mat):
            Lr, W = mat.shape
            return np.ascontiguousarray(mat.reshape(Lr // 128, 128, W // 256, 256).transpose(2, 1, 0, 3)).astype(np.float32).astype(BF)

        c[f"cm{L}"] = tile_(cm)
        c[f"m2f{L}"] = tile_(m2)
        c[f"m2i{L}"] = tile_(np.ascontiguousarray(m2.T[:, :nown]))
    return c


_CONSTS = None


def _rope_tables(tglob):
    half = 64
    inv = (10000.0 ** (-np.arange(0, half, 2, dtype=np.float32) / half)).astype(np.float32)
    row = (tglob // 64).astype(np.float32)
    colp = (tglob % 64).astype(np.float32)
    cosT = np.zeros((128, len(tglob)), np.float32)
    sinT = np.zeros((128, len(tglob)), np.float32)
    for i in range(128):
        pos = row if i < 64 else colp
        a = pos * inv[i % 32]
        cosT[i] = np.cos(a)
        sinT[i] = np.sin(a) * (-1.0 if (i % 64) < 32 else 1.0)
    return cosT, sinT


def prep_inputs(inp):
    global _CONSTS
    if _CONSTS is None:
        _CONSTS = _consts()
    f = lambda k: np.asarray(inp[k], np.float32)
    maps = []
    shared = dict(_CONSTS)
    shared["fw1"] = np.ascontiguousarray(f("filt_w1")[0])
    shared["fw2"] = np.ascontiguousarray(f("filt_w2")[0])
    shared["fw3"] = np.ascontiguousarray(np.concatenate([f("filt_w3")[0], f("filt_b3")[0][None, :]], axis=0))
    for k in ("w_mod", "w_in", "w_gate", "w_pa", "w_ph", "w_o", "w_up", "w_down"):
        shared[k] = np.ascontiguousarray(f(k)[0])
    bmod = f("b_mod")[0]
    rows_common = [f("norm_mix_post")[0], f("norm_ffn_post")[0], bmod[2 * D:3 * D], bmod[5 * D:6 * D], f("filt_decay")[0]]
    rows = np.ascontiguousarray(np.broadcast_to(np.stack(rows_common)[None], (128, 5, D))).astype(np.float32)
    for r in range(8):
        b, half = divmod(r, 2)
        m = dict(shared)
        xs = f("x_sample")[b]
        tglob = np.arange(LS)
        if half:
            xs = xs[::-1]
            tglob = tglob[::-1]
        m["xs"] = np.ascontiguousarray(xs)
        xpr = f("x_prompt")[2 * r:2 * r + 2]
        if half:
            xpr = xpr[:, ::-1]
        m["xp"] = np.ascontiguousarray(xpr).reshape(2 * LP, D)
        m["rows"] = rows
        m["kc"] = np.ascontiguousarray(f("cache_k")[b, 0].reshape(LP, 256))
        m["vc"] = np.ascontiguousarray(f("cache_v")[b, 0].reshape(LP, 256))
        cT, sT = _rope_tables(tglob[:NKEY])
        m["ropec"], m["ropes"] = cT, sT
        cols = np.zeros((128, NCOL), np.float32)

        def put(name, arr):
            o, w = _CL[name]
            assert arr.shape == (128, w), (name, arr.shape, w)
            cols[:, o:o + w] = arr

        cc = np.stack([_colform(f("c_ctx")), _colform(f("c")[b])], axis=-1)
        put("ccond", cc.reshape(128, 16))
        put("gpre1", _colform(f("norm_mix_pre")[0]))
        put("gpre2", _colform(f("norm_ffn_pre")[0]))
        put("bmodc", np.concatenate([_colform(bmod[p * D:(p + 1) * D]) for p in (0, 1, 3, 4)], axis=1))
        cw = f("conv_w")[0]
        if half:
            cw = cw[::-1]
        put("convw", np.stack([_colform(cw[t]) for t in range(3)], axis=-1).reshape(128, 36))
        put("convb", _colform(f("conv_b")[0]))
        put("bgate", _colform(f("b_gate")[0]))
        put("skip", _colform(f("hyena_skip")[0]))
        fv = np.zeros((128, 4), np.float32)
        fv[:64, 0] = f("filt_b1")[0]
        fv[:64, 1] = f("filt_freq1")[0]
        fv[:64, 2] = f("filt_b2")[0]
        fv[:64, 3] = f("filt_freq2")[0]
        put("fvec", fv)
        for L, nm in ((LS, "2048"), (LP, "256")):
            t = np.arange(L, dtype=np.float32)
            put("negt" + nm, _colform(-(t / np.float32(L))))
            wp = np.full(L, 1.0 / L, np.float32)
            wp[0] = 1.0 / (2 * L)
            sign = -1.0 if half else 1.0
            wq = np.full(L, sign / L, np.float32)
            wq[0] = 0.0
            put("wP" + nm, _colform(wp))
            put("wQ" + nm, _colform(wq))
        put("sink", np.ascontiguousarray(np.broadcast_to(f("attn_sink")[0][None, :], (128, 8))))
        m["cols"] = cols
        maps.append(m)
    return maps


_NC = None


def kernel(**inputs):
    global _NC
    if _NC is None:
        _NC = build()[0]
    maps = prep_inputs(inputs)
    res = run_bass_kernel_spmd(_NC, maps, core_ids=list(range(8)))
    B, Bd = 16, 4
    y_prompt = np.zeros((B, LP, D), np.float32)
    y_sample = np.zeros((Bd, LS, D), np.float32)
    new_k = np.zeros((B, 1, LP, NKV, HD), np.float32)
    new_v = np.zeros((B, 1, LP, NKV, HD), np.float32)
    for r in range(8):
        o = res.results[r]
        b, half = divmod(r, 2)
        rv = (lambda a: a[:, ::-1]) if half else (lambda a: a)
        y_prompt[2 * r:2 * r + 2] = rv(np.asarray(o["yp"]).reshape(2, LP, D))
        ys = np.asarray(o["ys"])
        if half:
            y_sample[b, NOWN:] = ys[::-1]
        else:
            y_sample[b, :NOWN] = ys
        new_k[2 * r:2 * r + 2, 0] = rv(np.asarray(o["nk"]).reshape(2, LP, NKV, HD))
        new_v[2 * r:2 * r + 2, 0] = rv(np.asarray(o["nv"]).reshape(2, LP, NKV, HD))
    return (y_prompt, y_sample, new_k, new_v)
```

```python
import contextlib
import math
import numpy as np
import ml_dtypes
import concourse.bass as bass
import concourse.mybir as mybir
from concourse.bass_utils import run_bass_kernel_spmd

F32 = mybir.dt.float32
BF16 = mybir.dt.bfloat16
AF = mybir.ActivationFunctionType
ALU = mybir.AluOpType
BF = ml_dtypes.bfloat16

D = 1024
NH, NKV, HD, GRP = 8, 2, 128, 4
LS, LP = 2048, 256
NOWN = 1024
NKEY = 1152
NM = 1536
HW = 512
DFF = 2816
EPS = 1e-6
SCALE = HD ** -0.5
NEG = -30000.0

_CL = {}
_off = 0
for _n, _w in [("ccond", 16), ("gpre1", 8), ("gpre2", 8), ("bmodc", 32), ("convw", 36), ("convb", 12),
               ("bgate", 16), ("skip", 4), ("fvec", 4), ("negt2048", 16), ("negt256", 2),
               ("wP2048", 16), ("wQ2048", 16), ("wP256", 2), ("wQ256", 2), ("sink", 8)]:
    _CL[_n] = (_off, _w)
    _off += _w
NCOL = _off


class Res:
    __slots__ = ("name", "w", "r", "excl")

    def __init__(self, name, inherit=(), excl=False):
        self.name = name
        self.w = None
        self.r = list(inherit)
        self.excl = excl


class Tok:
    __slots__ = ("kind", "key", "val")

    def __init__(self, kind, key, val):
        self.kind, self.key, self.val = kind, key, val


class Sched:
    ENGS = ("pe", "act", "dve", "pool", "sp")

    def __init__(self, nc, ndma=16):
        self.nc = nc
        self.q = {e: [] for e in self.ENGS}
        self.n = {e: 0 for e in self.ENGS}
        self.pending = {e: [] for e in self.ENGS}
        self.seen = {e: {} for e in self.ENGS}
        self.ndma = ndma
        self.dma_i = {e: 0 for e in self.ENGS}
        self.dma_cnt = {}
        self.ninstr = 0

    def _need_waits(self, eng, toks):
        best = {}
        for t in toks:
            if t is None:
                continue
            if t.kind == "e":
                if t.key == eng and eng == "pe":
                    continue
                if t.val is None:
                    raise RuntimeError(f"dependency on pending token of {t.key} from {eng}")
            key = (t.kind, t.key)
            if self.seen[eng].get(key, 0) >= t.val:
                continue
            best[key] = max(best.get(key, 0), t.val)
        for k, v in best.items():
            self.seen[eng][k] = v
        return list(best.items())

    def _deps(self, eng, reads, writes):
        toks = []
        for r in reads:
            toks.append(r.w)
        for w in writes:
            if w.w is not None and not (w.w.kind == "e" and w.w.key == eng):
                toks.append(w.w)
            for t in w.r:
                if t.kind == "e" and t.key == eng:
                    continue
                toks.append(t)
        return toks

    def _finish(self, tok, reads, writes):
        for r in reads:
            r.r.append(tok)
            if len(r.r) > 24:
                r.r = _reduce(r.r)
        for w in writes:
            w.w = tok
            w.r = []
        self.ninstr += 1
        return tok

    def op(self, eng, fn, reads=(), writes=(), inc=True, extra=()):
        ex = [r for r in reads if r.excl]
        if ex:
            reads = [r for r in reads if not r.excl]
            writes = list(writes) + [r for r in ex if r not in writes]
        toks = self._deps(eng, reads, writes) + list(extra)
        waits = self._need_waits(eng, toks)
        if inc:
            self.n[eng] += 1
            tok = Tok("e", eng, self.n[eng])
            for p in self.pending[eng]:
                p.val = self.n[eng]
            self.pending[eng] = []
        else:
            tok = Tok("e", eng, None)
            self.pending[eng].append(tok)
        self.q[eng].append((waits, fn, ("e", eng) if inc else None))
        return self._finish(tok, reads, writes)

    def dma(self, eng, fn, reads=(), writes=(), extra=()):
        toks = self._deps(eng, reads, writes) + list(extra)
        slot = self.dma_i[eng] % self.ndma
        self.dma_i[eng] += 1
        key = (eng, slot)
        prev = self.dma_cnt.get(key, 0)
        if prev:
            toks.append(Tok("d", key, prev))
        waits = self._need_waits(eng, toks)
        self.dma_cnt[key] = prev + 16
        tok = Tok("d", key, prev + 16)
        self.q[eng].append((waits, fn, ("d", key)))
        return self._finish(tok, reads, writes)

    def wait_all(self, eng, toks):
        waits = self._need_waits(eng, [t for t in toks if t is not None])
        self.q[eng].append((waits, None, None))

    def run(self, stack):
        nc = self.nc
        semobj = {}
        for e in self.ENGS:
            if self.n[e] > 0:
                semobj[("e", e)] = stack.enter_context(nc.semaphore(f"s_{e}"))
        for key in self.dma_cnt:
            semobj[("d", key)] = stack.enter_context(nc.semaphore(f"d_{key[0]}_{key[1]}"))
        block = stack.enter_context(nc.Block())
        names = {"pe": "tensor", "act": "scalar", "dve": "vector", "pool": "gpsimd", "sp": "sync"}

        def make(e):
            q = self.q[e]

            def body(h):
                for waits, fn, inc in q:
                    for key, val in waits:
                        h.wait_ge(semobj[key], val)
                    if fn is None:
                        continue
                    ins = fn()
                    if inc is not None:
                        ins.then_inc(semobj[inc], 16 if inc[0] == "d" else 1)
            return body

        for e in self.ENGS:
            if self.q[e]:
                getattr(block, names[e])(make(e))


def _reduce(toks):
    best = {}
    out = []
    for t in toks:
        if t is None:
            continue
        if t.val is None:
            out.append(t)
            continue
        k = (t.kind, t.key)
        if k not in best or best[k].val < t.val:
            best[k] = t
    return out + list(best.values())


class Buf:
    def __init__(self, ap, off, nbytes, inherit):
        self.ap = ap
        self.off = off
        self.nbytes = nbytes
        self.inherit = inherit
        self.res = {}

    def R(self, key=0):
        r = self.res.get(key)
        if r is None:
            r = Res(key, self.inherit)
            self.res[key] = r
        return r

    def __getitem__(self, idx):
        return self.ap[idx]


class Arena:
    def __init__(self, base_ap, nbytes):
        self.base = base_ap
        self.free = [(0, nbytes)]
        self.hist = []

    def alloc(self, shape, dtype):
        esz = 4 if dtype == F32 else 2
        n = esz
        for s in shape[1:]:
            n *= s
        n = (n + 63) // 64 * 64
        for i, (s, e) in enumerate(self.free):
            if e - s >= n:
                off = s
                if e - s == n:
                    self.free.pop(i)
                else:
                    self.free[i] = (s + n, e)
                break
        else:
            raise RuntimeError(f"arena out of memory for {shape} ({n} B); free={self.free}")
        inherit = []
        keep = []
        for (hs, he, toks) in self.hist:
            if hs < off + n and off < he:
                inherit += toks
                if hs < off:
                    keep.append((hs, off, toks))
                if he > off + n:
                    keep.append((off + n, he, toks))
            else:
                keep.append((hs, he, toks))
        self.hist = keep
        ap = self.base[0:shape[0], off // 2:(off + n) // 2]
        if dtype == F32:
            ap = ap.bitcast(F32)
        cnt = 1
        for s in shape[1:]:
            cnt *= s
        ap = ap[:, 0:cnt]
        if len(shape) == 3:
            ap = ap.rearrange("p (a b) -> p a b", a=shape[1])
        elif len(shape) == 4:
            ap = ap.rearrange("p (a b c) -> p a b c", a=shape[1], b=shape[2])
        return Buf(ap, off, n, _reduce(inherit))

    def release(self, buf):
        toks = list(buf.inherit)
        for r in buf.res.values():
            toks.append(r.w)
            toks += r.r
        toks = _reduce(toks)
        self.hist.append((buf.off, buf.off + buf.nbytes, toks))
        self.free.append((buf.off, buf.off + buf.nbytes))
        self.free.sort()
        merged = []
        for s, e in self.free:
            if merged and merged[-1][1] == s:
                merged[-1] = (merged[-1][0], e)
            else:
                merged.append((s, e))
        self.free = merged


def build(debug=(), stop=None):
    nc = bass.Bass("TRN2", target_bir_lowering=False)
    st = contextlib.ExitStack()
    dbg_outs = {}

    def din(name, shape, dt=F32):
        return nc.dram_tensor(name, list(shape), dt, kind="ExternalInput").ap()

    def dout(name, shape, dt=F32):
        return nc.dram_tensor(name, list(shape), dt, kind="ExternalOutput").ap()

    xs_d = din("xs", [LS, D])
    xp_d = din("xp", [2 * LP, D])
    cols_d = din("cols", [128, NCOL])
    rows_d = din("rows", [128, 5, D])
    kc_d = din("kc", [LP, 256])
    vc_d = din("vc", [LP, 256])
    ropec_d = din("ropec", [128, NKEY])
    ropes_d = din("ropes", [128, NKEY])
    ident_d = din("ident", [128, 128], BF16)
    permf_d = din("permf", [128, 128])
    maskl_d = din("maskl", [128, 128], BF16)
    maskr_d = din("maskr", [128, 128], BF16)
    feat_d = {LS: din("feat2048", [17, LS]), LP: din("feat256", [17, LP])}
    cm_d = {LS: din("cm2048", [LS // 256, 128, LS // 128, 256], BF16), LP: din("cm256", [1, 128, 2, 256], BF16)}
    m2f_d = {LS: din("m2f2048", [LS // 256, 128, LS // 128, 256], BF16), LP: din("m2f256", [1, 128, 2, 256], BF16)}
    m2i_d = {LS: din("m2i2048", [NOWN // 256, 128, LS // 128, 256], BF16), LP: din("m2i256", [1, 128, 2, 256], BF16)}
    fw1_d = din("fw1", [17, 64])
    fw2_d = din("fw2", [64, 64])
    fw3_d = din("fw3", [65, 2 * HW])
    wmod_d = din("w_mod", [D, 6 * D])
    win_d = din("w_in", [D, 3072])
    wgate_d = din("w_gate", [D, 2 * D])
    wpa_d = din("w_pa", [D, D])
    wph_d = din("w_ph", [HW, D])
    wo_d = din("w_o", [D, D])
    wup_d = din("w_up", [D, 2 * DFF])
    wdown_d = din("w_down", [DFF, D])
    ys_d = dout("ys", [NOWN, D])
    yp_d = dout("yp", [2 * LP, D])
    nk_d = dout("nk", [2 * LP, 256])
    nv_d = dout("nv", [2 * LP, 256])
    spec_d = {LS: nc.dram_tensor("spec2048", [17, 128, 1024], BF16).ap(),
              LP: nc.dram_tensor("spec256", [3, 128, 1024], BF16).ap()}

    ARENA_BYTES = 207 * 1024
    arena_t = st.enter_context(nc.sbuf_tensor("arena", [128, ARENA_BYTES // 2], BF16))
    AR = Arena(arena_t[:, :], ARENA_BYTES)
    banks = [st.enter_context(nc.psum_tensor(f"bank{i}", [128, 512], F32)) for i in range(8)]
    BK = [Res(f"bank{i}", excl=True) for i in range(8)]
    S = Sched(nc)
    out_toks = []
    dram_res = {}

    def DR(name):
        if name not in dram_res:
            dram_res[name] = Res(name)
        return dram_res[name]

    def MM(out, lhsT, rhs, st_, sp_, rd, wr, inc=True):
        return S.op("pe", lambda: nc.tensor.matmul(out, lhsT=lhsT, rhs=rhs, start=st_, stop=sp_,
                                                   skip_group_check=True), reads=rd, writes=wr, inc=inc)

    def TR(out, in_, rd, wr, inc=True):
        return S.op("pe", lambda: nc.tensor.transpose(out, in_, ident[:, :]), reads=rd + [ident.R()], writes=wr, inc=inc)

    def ACT(out, in_, func, rd, wr, scale=None, bias=None, accum=None):
        kw = {}
        if scale is not None:
            kw["scale"] = scale
        if bias is not None:
            kw["bias"] = bias
        if accum is not None:
            kw["accum_out"] = accum
        return S.op("act", lambda: nc.scalar.activation(out=out, in_=in_, func=func, **kw), reads=rd, writes=wr)

    def ENG(e):
        return nc.vector if e == "dve" else nc.gpsimd

    def TT(e, out, a, b, op, rd, wr):
        return S.op(e, lambda: ENG(e).tensor_tensor(out=out, in0=a, in1=b, op=op), reads=rd, writes=wr)

    def TS(e, out, a, s1, s2, op0, op1, rd, wr):
        if op1 is None:
            return S.op(e, lambda: ENG(e).tensor_scalar(out=out, in0=a, scalar1=s1, scalar2=None, op0=op0), reads=rd, writes=wr)
        return S.op(e, lambda: ENG(e).tensor_scalar(out=out, in0=a, scalar1=s1, scalar2=s2, op0=op0, op1=op1), reads=rd, writes=wr)

    def STT(out, in0, scalar, in1, op0, op1, rd, wr):
        return S.op("dve", lambda: nc.vector.scalar_tensor_tensor(out=out, in0=in0, scalar=scalar, in1=in1, op0=op0, op1=op1),
                    reads=rd, writes=wr)

    def CP(e, out, in_, rd, wr):
        if e == "act":
            return S.op("act", lambda: nc.scalar.copy(out=out, in_=in_), reads=rd, writes=wr)
        return S.op(e, lambda: ENG(e).tensor_copy(out=out, in_=in_), reads=rd, writes=wr)

    def MSET(e, ap, val, wr):
        return S.op(e, lambda: ENG(e).memset(ap, val), writes=wr)

    def RECIP(out, in_, rd, wr):
        return S.op("dve", lambda: nc.vector.reciprocal(out=out, in_=in_), reads=rd, writes=wr)

    def DMA(e, out, in_, rd, wr):
        h = {"sp": nc.sync, "pool": nc.gpsimd, "act": nc.scalar}[e]
        return S.dma(e, lambda: h.dma_start(out=out, in_=in_), reads=rd, writes=wr)

    def DBG(name, buf, rd):
        if name not in debug:
            return
        shape = list(buf.ap.shape)
        dt = buf.ap.dtype
        d = nc.dram_tensor("dbg_" + name, shape, dt, kind="ExternalOutput").ap()
        dbg_outs[name] = d
        out_toks.append(DMA("sp", d, buf.ap, rd, [DR("dbg_" + name)]))

    def finish():
        S.wait_all("sp", out_toks)
        S.run(st)
        st.close()
        return nc, dbg_outs, S

    ident = AR.alloc([128, 128], BF16)
    permf = AR.alloc([128, 128], F32)
    maskl = AR.alloc([128, 128], BF16)
    maskr = AR.alloc([128, 128], BF16)
    cols = AR.alloc([128, NCOL], F32)
    zeros = AR.alloc([128, 128], F32)
    junk = AR.alloc([128, 1024], BF16)
    DMA("sp", ident.ap, ident_d, [], [ident.R()])
    DMA("sp", permf.ap, permf_d, [], [permf.R()])
    DMA("sp", maskl.ap, maskl_d, [], [maskl.R()])
    DMA("sp", maskr.ap, maskr_d, [], [maskr.R()])
    DMA("sp", cols.ap, cols_d, [], [cols.R()])
    MSET("dve", zeros.ap, 0.0, [zeros.R()])

    def col(name, j=0, n=1):
        o, w = _CL[name]
        return cols[:, o + j:o + j + n]

    NRING = 4
    ring = [AR.alloc([128, 8, 512], BF16) for _ in range(NRING)]
    ring_i = [0]

    def wload(dram_ap, kc, ncols):
        slot = ring[ring_i[0] % len(ring)]
        ring_i[0] += 1
        DMA("pool", slot[:, 0:kc, 0:ncols], dram_ap.rearrange("(k p) n -> p k n", p=128), [], [slot.R()])
        return slot

    wmod_pre = []
    for pi_, part_ in enumerate([0, 1]):
        for half_ in range(2):
            if len(wmod_pre) < 3:
                wmod_pre.append(wload(wmod_d[:, part_ * D + half_ * 512: part_ * D + (half_ + 1) * 512], 8, 512))
    fw1 = AR.alloc([17, 64], F32)
    fw2 = AR.alloc([64, 64], F32)
    fw3 = AR.alloc([65, 2 * HW], F32)
    adec = AR.alloc([128, 2 * HW], F32)
    fb = AR.alloc([64, 2], F32)
    DMA("sp", fw1.ap, fw1_d, [], [fw1.R()])
    DMA("sp", fw2.ap, fw2_d, [], [fw2.R()])
    DMA("sp", fw3.ap, fw3_d, [], [fw3.R()])
    fw3b = AR.alloc([65, 2 * HW], BF16)
    CP("dve", fw3b.ap, fw3.ap, [fw3.R()], [fw3b.R()])
    DMA("sp", adec.ap, rows_d[:, 4, :], [], [adec.R()])
    ACT(adec.ap, adec.ap, AF.Abs, [adec.R()], [adec.R()])
    fo = _CL["fvec"][0]
    fv = cols.ap
    TT("dve", fb[:, 0:1], fv[0:64, fo + 0:fo + 1], fv[0:64, fo + 1:fo + 2], ALU.mult, [cols.R()], [fb.R()])
    TT("dve", fb[:, 1:2], fv[0:64, fo + 2:fo + 3], fv[0:64, fo + 3:fo + 4], ALU.mult, [cols.R()], [fb.R()])

    def wrap_pi(a, ares, t, tres):
        TS("dve", t, a, -math.pi, 2 * math.pi, ALU.is_lt, ALU.mult, [ares], [tres])
        TT("dve", a, a, t, ALU.add, [ares, tres], [ares])
        TS("dve", t, a, math.pi, -2 * math.pi, ALU.is_gt, ALU.mult, [ares], [tres])
        TT("dve", a, a, t, ALU.add, [ares, tres], [ares])

    def filter_spectrum(L):
        nt = L // 128
        feat = AR.alloc([17, L], F32)
        nb_ = max(1, L // 512)
        h1s = [AR.alloc([64, 512], F32) for _ in range(nb_)]
        args = [AR.alloc([64, 512], F32) for _ in range(nb_)]
        h2 = AR.alloc([65, L], BF16)
        wtmps = [AR.alloc([64, 512], F32) for _ in range(nb_)]
        DMA("sp", feat.ap, feat_d[L], [], [feat.R()])
        MSET("dve", h2[64:65, :], 1.0, [h2.R()])
        nb = max(1, L // 512)
        bw = min(512, L)
        def blk(tb):
            return slice(tb * bw, (tb + 1) * bw), h1s[tb], args[tb], wtmps[tb], (2 * tb) % 8, (2 * tb + 1) % 8

        for tb in range(nb):
            sl, h1, arg, wtmp, b0, b1 = blk(tb)
            MM(banks[b0][0:64, 0:bw], fw1.ap, feat[:, sl], True, True, [fw1.R(), feat.R()], [BK[b0]])
            ACT(arg[:, 0:bw], banks[b0][0:64, 0:bw], AF.Identity, [BK[b0], cols.R(), fb.R()], [arg.R()],
                scale=fv[0:64, fo + 1:fo + 2], bias=fb[:, 0:1])
        yield "mlp"
        for tb in range(nb):
            sl, h1, arg, wtmp, b0, b1 = blk(tb)
            wrap_pi(arg[:, 0:bw], arg.R(), wtmp[:, 0:bw], wtmp.R())
        yield "mlp"
        for tb in range(nb):
            sl, h1, arg, wtmp, b0, b1 = blk(tb)
            ACT(h1[:, 0:bw], arg[:, 0:bw], AF.Sin, [arg.R()], [h1.R()])
            MM(banks[b1][0:64, 0:bw], fw2.ap, h1[:, 0:bw], True, True, [fw2.R(), h1.R()], [BK[b1]])
            ACT(arg[:, 0:bw], banks[b1][0:64, 0:bw], AF.Identity, [BK[b1], cols.R(), fb.R()], [arg.R()],
                scale=fv[0:64, fo + 3:fo + 4], bias=fb[:, 1:2])
        yield "mlp"
        for tb in range(nb):
            sl, h1, arg, wtmp, b0, b1 = blk(tb)
            wrap_pi(arg[:, 0:bw], arg.R(), wtmp[:, 0:bw], wtmp.R())
        yield "mlp"
        for tb in range(nb):
            sl, h1, arg, wtmp, b0, b1 = blk(tb)
            ACT(h2[0:64, sl], arg[:, 0:bw], AF.Sin, [arg.R()], [h2.R()])
        for b_ in h1s + args + wtmps:
            AR.release(b_)
        AR.release(feat)
        eT = AR.alloc([128, nt, HW], BF16)
        oT = AR.alloc([128, nt, HW], BF16)
        edec = [AR.alloc([128, 2 * HW], F32) for _ in range(2)]
        ftap = [AR.alloc([128, 2 * HW], F32) for _ in range(2)]
        ngt = "negt2048" if L == LS else "negt256"
        for j in range(nt):
            ed = edec[j % 2]
            ft = ftap[j % 2]
            ACT(ed.ap, adec.ap, AF.Exp, [adec.R(), cols.R()], [ed.R()], scale=col(ngt, j))
            for hlf in range(2):
                b = 2 + (2 * j + hlf) % 4
                MM(banks[b][:, :], h2[0:65, j * 128:(j + 1) * 128], fw3b[0:65, hlf * HW:(hlf + 1) * HW], True, True,
                   [h2.R(), fw3b.R()], [BK[b]])
                TT("dve", ft[:, hlf * HW:(hlf + 1) * HW], banks[b][:, :], ed[:, hlf * HW:(hlf + 1) * HW], ALU.mult,
                   [BK[b], ed.R()], [ft.R()])
            if j == 0:
                MSET("dve", ft[0:1, HW:2 * HW], 0.0, [ft.R()])
            TT("dve", eT[:, j, :], ft[:, 0:HW], ft[:, HW:2 * HW], ALU.add, [ft.R()], [eT.R(j)])
            TT("pool", oT[:, j, :], ft[:, 0:HW], ft[:, HW:2 * HW], ALU.subtract, [ft.R()], [oT.R(j)])
            if j % 4 == 3:
                yield "taps"
        for b_ in edec + ftap:
            AR.release(b_)
        AR.release(h2)
        yield "spec"
        wPn, wQn = ("wP2048", "wQ2048") if L == LS else ("wP256", "wQ256")
        nfp = L // 256
        dslots = [AR.alloc([128, nt, 256], BF16) for _ in range(4)]
        pq = [AR.alloc([128, 2, HW], BF16) for _ in range(2)]
        pl = AR.alloc([1, HW], F32)
        pv0 = AR.alloc([128, HW], BF16)
        di = 0
        for fp in range(nfp):
            cs = dslots[di % 4]
            ms = dslots[(di + 1) % 4]
            di += 2
            DMA("sp", cs.ap, cm_d[L][fp], [], [cs.R()])
            DMA("sp", ms.ap, m2f_d[L][fp], [], [ms.R()])
            for f2 in range(2):
                ft_i = fp * 2 + f2
                bP = 4 + (ft_i % 2) * 2
                bQ = bP + 1
                for k in range(nt):
                    MM(banks[bP][:, :], cs[:, k, f2 * 128:(f2 + 1) * 128], eT[:, k, :], k == 0, k == nt - 1,
                       [cs.R(), eT.R(k)], [BK[bP]], inc=(k == nt - 1))
                for k in range(nt):
                    MM(banks[bQ][:, :], ms[:, k, f2 * 128:(f2 + 1) * 128], oT[:, k, :], k == 0, k == nt - 1,
                       [ms.R(), oT.R(k)], [BK[bQ]], inc=(k == nt - 1))
                p_ = pq[ft_i % 2]
                ACT(p_[:, 0, :], banks[bP][:, :], AF.Copy, [BK[bP], cols.R()], [p_.R()], scale=col(wPn, ft_i))
                ACT(p_[:, 1, :], banks[bQ][:, :], AF.Copy, [BK[bQ], cols.R()], [p_.R()], scale=col(wQn, ft_i))
                DMA("act", spec_d[L][ft_i].rearrange("p (a c) -> p a c", a=2), p_.ap, [p_.R()], [DR(f"spec{L}")])
                if ft_i == 0:
                    for k in range(nt):
                        MM(banks[3][0:1, :], ms[:, k, 0:1], eT[:, k, :], k == 0, k == nt - 1,
                           [ms.R(), eT.R(k)], [BK[3]], inc=(k == nt - 1))
                    CP("dve", pv0.ap, p_[:, 0, :], [p_.R()], [pv0.R()])
                    ACT(pv0[0:1, :], banks[3][0:1, :], AF.Copy, [BK[3]], [pv0.R()], scale=1.0 / (2 * L))
                    DMA("act", spec_d[L][nt][:, 0:HW], pv0.ap, [pv0.R()], [DR(f"spec{L}")])
                yield "ft"
        for b_ in dslots + pq + [pl, pv0, eT, oT]:
            AR.release(b_)

    genS = filter_spectrum(LS)
    genP = filter_spectrum(LP)
    tS = tP = None
    while tS != "spec" or tP != "spec":
        if tS != "spec":
            tS = next(genS)
        if tP != "spec":
            tP = next(genP)
    for b_ in [fw1, fw2, fw3, fw3b, adec, fb]:
        AR.release(b_)

    pstate = {"lp": True}

    def pump(n=1):
        for _ in range(n):
            if pstate["lp"]:
                try:
                    next(genP)
                except StopIteration:
                    pstate["lp"] = False
            try:
                next(genS)
            except StopIteration:
                return
    pump(2)

    if stop == 'F':
        pump(100)
        return finish()
    scol = AR.alloc([128, 8, 2], BF16)
    srep = AR.alloc([128, 8, 2, 128], BF16)
    modc = AR.alloc([128, 4, 8, 2], F32)
    gain = AR.alloc([128, 2, 8, 2], F32)
    oc = _CL["ccond"][0]
    ACT(scol.ap.rearrange("p k c -> p (k c)"), cols[:, oc:oc + 16], AF.Silu, [cols.R()], [scol.R()])
    scolf = AR.alloc([128, 16], F32)
    ACT(scolf.ap, cols[:, oc:oc + 16], AF.Silu, [cols.R()], [scolf.R()])
    for k in range(8):
        for c in range(2):
            ACT(srep[:, k, c, :], zeros[:, 0:128], AF.Identity, [zeros.R(), scolf.R()], [srep.R()],
                bias=scolf[:, 2 * k + c:2 * k + c + 1], scale=1.0)
    parts = [0, 1, 3, 4]
    for pi, part in enumerate(parts):
        for half in range(2):
            slot = wmod_pre.pop(0) if wmod_pre else wload(wmod_d[:, part * D + half * 512: part * D + (half + 1) * 512], 8, 512)
            for f4 in range(4):
                fc = half * 4 + f4
                o_ = (pi * 8 + fc) * 2
                for k in range(8):
                    MM(banks[0][:, o_:o_ + 2], slot[:, k, f4 * 128:(f4 + 1) * 128], scol[:, k, :], k == 0, k == 7,
                       [slot.R(), scol.R()], [BK[0]], inc=(k == 7))
            pump(1)
    ob = _CL["bmodc"][0]
    for c in range(2):
        TT("dve", modc.ap.rearrange("p a k c -> p (a k) c")[:, :, c],
           banks[0][:, 0:64].rearrange("p (a c) -> p a c", c=2)[:, :, c],
           cols[:, ob:ob + 32], ALU.add, [BK[0], cols.R()], [modc.R()])
    for n_, (gname, sc_i) in enumerate([("gpre1", 1), ("gpre2", 3)]):
        og = _CL[gname][0]
        for c in range(2):
            STT(gain[:, n_, :, c], modc[:, sc_i, :, c], 1.0, cols[:, og:og + 8], ALU.add, ALU.mult,
                [modc.R(), cols.R()], [gain.R()])
    AR.release(scolf)

    if stop == '0':
        pump(100)
        return finish()
    def prenorm_run(groups, between=None):
        ssb = [AR.alloc([128, 8], F32) for _ in range(2)]
        xnb = [[AR.alloc([128, D], BF16) for _ in range(4)] for _ in range(2)]

        def part1(gi):
            G = groups[gi]
            srcs, src_res = G["srcs"]()
            G["n"] = len(srcs)
            ss = ssb[gi % 2]
            xn = xnb[gi % 2]
            for a, src in enumerate(srcs):
                S.op("dve", lambda src=src, acc=ss[:, a:a + 1]: nc.vector.scalar_tensor_tensor(
                    out=junk.ap, in0=src, scalar=1.0, in1=src, op0=ALU.mult, op1=ALU.mult, accum_out=acc),
                    reads=[src_res[a]], writes=[junk.R(), ss.R(a)])
            for a, src in enumerate(srcs):
                ACT(ss[:, a:a + 1], ss[:, a:a + 1], AF.Sqrt, [ss.R(a), epsc.R()], [ss.R(a)], scale=1.0 / D, bias=epsc[:, 0:1])
            for a, src in enumerate(srcs):
                RECIP(ss[:, 4 + a:5 + a], ss[:, a:a + 1], [ss.R(a)], [ss.R(a)])
            for a, src in enumerate(srcs):
                if a == 0:
                    TS("dve", xn[a].ap, src, ss[:, 4 + a:5 + a], 0.0, ALU.mult, ALU.add, [src_res[a], ss.R(a)], [xn[a].R()])
                else:
                    ACT(xn[a].ap, src, AF.Copy, [src_res[a], ss.R(a)], [xn[a].R()], scale=ss[:, 4 + a:5 + a])

        def part2(gi):
            G = groups[gi]
            nt_ = G["n"]
            xn = xnb[gi % 2]
            dst_aps, dst_res, n_idx, conds = G["dst_aps"], G["dst_res"], G["n_idx"], G["conds"]
            for a in range(nt_):
                for k in range(8):
                    b = k // 2
                    pT = banks[b][:, :].bitcast(BF16)
                    TR(pT[:, (k % 2) * 512 + a * 128:(k % 2) * 512 + (a + 1) * 128], xn[a][:, k * 128:(k + 1) * 128],
                       [xn[a].R()], [BK[b]], inc=(k % 2 == 1))
            c = conds[0]
            for k in range(8):
                b = k // 2
                pT = banks[b][:, :].bitcast(BF16)
                i_ap = pT[:, (k % 2) * 512:(k % 2) * 512 + nt_ * 128]
                sc_ap = gain[:, n_idx, k, c:c + 1]
                sh_ap = modc[:, 0 if n_idx == 0 else 2, k, c:c + 1]
                o_ap = dst_aps[k]
                if k % 2 == 0:
                    ACT(o_ap, i_ap, AF.Identity, [BK[b], gain.R(), modc.R()], [dst_res[k]], scale=sc_ap, bias=sh_ap)
                else:
                    TS("dve", o_ap, i_ap, sc_ap, sh_ap, ALU.mult, ALU.add, [BK[b], gain.R(), modc.R()], [dst_res[k]])

        part1(0)
        for gi in range(len(groups)):
            if gi + 1 < len(groups):
                part1(gi + 1)
            if between is not None:
                between()
            part2(gi)
        for b_ in ssb + xnb[0] + xnb[1]:
            AR.release(b_)

    epsc = AR.alloc([128, 1], F32)
    MSET("dve", epsc.ap, EPS, [epsc.R()])

    hTs = AR.alloc([128, 8, LS + 2], BF16)
    hTp = AR.alloc([128, 8, 2 * (LP + 2)], BF16)
    for k in range(8):
        MSET("dve", hTs[:, k, 0:1], 0.0, [hTs.R(k)])
        MSET("dve", hTs[:, k, LS + 1:LS + 2], 0.0, [hTs.R(k)])
        for s in range(2):
            MSET("dve", hTp[:, k, s * 258:s * 258 + 1], 0.0, [hTp.R(k)])
            MSET("dve", hTp[:, k, s * 258 + 257:s * 258 + 258], 0.0, [hTp.R(k)])
    xring = [AR.alloc([128, D], F32) for _ in range(8)]
    xi = [0]

    def load_x(dram_rows):
        t = xring[xi[0] % len(xring)]
        xi[0] += 1
        DMA("sp", t.ap, dram_rows, [], [t.R()])
        return t

    def loader(row_aps):
        def f():
            tiles = [load_x(r) for r in row_aps]
            return [t.ap for t in tiles], [t.R() for t in tiles]
        return f

    groups = []
    for g in range(4):
        groups.append(dict(srcs=loader([xs_d[(4 * g + a) * 128:(4 * g + a + 1) * 128, :] for a in range(4)]),
                           dst_aps=[hTs[:, k, 1 + g * 512:1 + (g + 1) * 512] for k in range(8)],
                           dst_res=[hTs.R(k) for k in range(8)], n_idx=0, conds=[1] * 4))
    for s_ in range(2):
        groups.append(dict(srcs=loader([xp_d[(2 * s_ + a) * 128:(2 * s_ + a + 1) * 128, :] for a in range(2)]),
                           dst_aps=[hTp[:, k, s_ * 258 + 1:s_ * 258 + 257] for k in range(8)],
                           dst_res=[hTp.R(k) for k in range(8)], n_idx=0, conds=[0] * 2))
    prenorm_run(groups, between=lambda: pump(1))
    pump(100)
    for t in xring:
        AR.release(t)
    DBG("hTs", hTs, [hTs.R(k) for k in range(8)])
    DBG("hTp", hTp, [hTp.R(k) for k in range(8)])

    hs_all = [hTs.R(k) for k in range(8)]
    hp_all = [hTp.R(k) for k in range(8)]

    if stop == '1':
        return finish()
    ring2 = [AR.alloc([128, 8, 512], BF16) for _ in range(3)]
    ring.extend(ring2)
    QT = AR.alloc([128, NH, NM], BF16)
    KTs = AR.alloc([128, NKV, NKEY], BF16)
    KTp = AR.alloc([128, NKV, 2 * LP], BF16)
    KTc = AR.alloc([128, NKV, LP], BF16)
    Vs = AR.alloc([128, 9, NKV, 129], BF16)
    Vp = AR.alloc([128, 4, NKV, 129], BF16)
    Vc = AR.alloc([128, 2, NKV, 129], BF16)
    ropec = AR.alloc([128, NKEY], F32)
    ropes = AR.alloc([128, NKEY], F32)
    DMA("sp", ropec.ap, ropec_d, [], [ropec.R()])
    DMA("sp", ropes.ap, ropes_d, [], [ropes.R()])
    for vb, n_ in ((Vs, 9), (Vp, 4), (Vc, 2)):
        MSET("dve", vb.ap.rearrange("p a g c -> p (a g) c")[:, :, 128:129], 1.0, [vb.R()])

    kcv = AR.alloc([128, 2, 2, 256], F32)
    kcb = AR.alloc([128, 2, 256], BF16)
    DMA("sp", kcv[:, 0, :, :], kc_d.rearrange("(a p) c -> p a c", p=128), [], [kcv.R(0)])
    DMA("sp", kcv[:, 1, :, :], vc_d.rearrange("(a p) c -> p a c", p=128), [], [kcv.R(1)])
    CP("act", kcb.ap, kcv[:, 0, :, :], [kcv.R(0)], [kcb.R()])
    for a in range(2):
        CP("dve", Vc[:, a, :, 0:128], kcv[:, 1, a, :].rearrange("p (g c) -> p g c", g=2), [kcv.R(1)], [Vc.R()])
    pT7 = banks[7][:, :].bitcast(BF16)
    for g in range(2):
        for a in range(2):
            TR(pT7[:, (g * 2 + a) * 128:(g * 2 + a + 1) * 128], kcb[:, a, g * 128:(g + 1) * 128], [kcb.R()], [BK[7]],
               inc=(g == 1 and a == 1))
    CP("dve", KTc.ap, pT7[:, 0:512].rearrange("p (g t) -> p g t", g=2), [BK[7]], [KTc.R()])
    AR.release(kcv)
    AR.release(kcb)

    rr = [0]
    qraw = [AR.alloc([128, 512], BF16) for _ in range(3)]
    rtmp = [AR.alloc([128, 512], F32) for _ in range(3)]
    rtmp2 = [AR.alloc([128, 512], F32) for _ in range(3)]
    ri = [0]
    permb = AR.alloc([128, 128], BF16)
    CP("dve", permb.ap, permf.ap, [permf.R()], [permb.R()])

    def pbank():
        b = rr[0] % 6
        rr[0] += 1
        return b

    deferred = []

    def flush_deferred():
        while deferred:
            deferred.pop(0)()

    def rope_evac(b, n, t0, out_ap, out_res):
        i = ri[0] % 3
        ri[0] += 1
        q_ = qraw[i]
        t_ = rtmp[i]
        t2 = rtmp2[i]
        CP("act", q_[:, 0:n], banks[b][:, 0:n], [BK[b]], [q_.R()])
        TT("dve", t_[:, 0:n], banks[b][:, 0:n], ropec[:, t0:t0 + n], ALU.mult, [BK[b], ropec.R()], [t_.R()])

        def later():
            pb = 6 + (i % 2)
            MM(banks[pb][:, 0:n], permb.ap, q_[:, 0:n], True, True, [permb.R(), q_.R()], [BK[pb]])
            TT("dve", t2[:, 0:n], banks[pb][:, 0:n], ropes[:, t0:t0 + n], ALU.mult, [BK[pb], ropes.R()], [t2.R()])
            TT("dve", out_ap, t_[:, 0:n], t2[:, 0:n], ALU.add, [t_.R(), t2.R()], [out_res])
        deferred.append(later)

    def proj_fm(slot, c0, rhs_fn, n, rd):
        b = pbank()
        for k in range(8):
            MM(banks[b][:, 0:n], slot[:, k, c0:c0 + 128], rhs_fn(k), k == 0, k == 7, [slot.R()] + rd, [BK[b]], inc=(k == 7))
        flush_deferred()
        return b

    slot = wload(win_d[:, 1024:1536], 8, 512)
    for g in range(2):
        for (t0, n) in ((0, 512), (512, 512), (1024, 128)):
            b = proj_fm(slot, g * 128, lambda k, t0=t0, n=n: hTs[:, k, 1 + t0:1 + t0 + n], n, hs_all)
            rope_evac(b, n, t0, KTs[:, g, t0:t0 + n], KTs.R(g))
        for s in range(2):
            b = proj_fm(slot, g * 128, lambda k, s=s: hTp[:, k, s * 258 + 1:s * 258 + 257], 256, hp_all)
            CP("act", KTp[:, g, s * 256:(s + 1) * 256], banks[b][:, 0:256], [BK[b]], [KTp.R(g)])
    kvout = [AR.alloc([128, 512], F32) for _ in range(2)]
    for i in range(4):
        s, a = i // 2, i % 2
        b = pbank()
        for k in range(8):
            MM(banks[b][:, :], hTp[:, k, s * 258 + 1 + a * 128:s * 258 + 1 + (a + 1) * 128], slot[:, k, :], k == 0, k == 7,
               [slot.R()] + hp_all, [BK[b]], inc=(k == 7))
        ko = kvout[i % 2]
        CP("act", ko.ap, banks[b][:, :], [BK[b]], [ko.R()])
        CP("dve", Vp[:, i, :, 0:128], banks[b][:, 256:512].rearrange("p (g c) -> p g c", g=2), [BK[b]], [Vp.R()])
        out_toks.append(DMA("sp", nk_d[i * 128:(i + 1) * 128, :], ko[:, 0:256], [ko.R()], [DR("nk")]))
        out_toks.append(DMA("sp", nv_d[i * 128:(i + 1) * 128, :], ko[:, 256:512], [ko.R()], [DR("nv")]))
    for i in range(9):
        b = pbank()
        for k in range(8):
            MM(banks[b][:, 0:256], hTs[:, k, 1 + i * 128:1 + (i + 1) * 128], slot[:, k, 256:512], k == 0, k == 7,
               [slot.R()] + hs_all, [BK[b]], inc=(k == 7))
        CP("dve", Vs[:, i, :, 0:128], banks[b][:, 0:256].rearrange("p (g c) -> p g c", g=2), [BK[b]], [Vs.R()])
    for ko in kvout:
        AR.release(ko)
    for hb in range(2):
        slot = wload(win_d[:, hb * 512:(hb + 1) * 512], 8, 512)
        for h4 in range(4):
            h = hb * 4 + h4
            for (t0, n) in ((0, 512), (512, 512)):
                b = proj_fm(slot, h4 * 128, lambda k, t0=t0, n=n: hTs[:, k, 1 + t0:1 + t0 + n], n, hs_all)
                rope_evac(b, n, t0, QT[:, h, t0:t0 + n], QT.R(h))
            for s in range(2):
                b = proj_fm(slot, h4 * 128, lambda k, s=s: hTp[:, k, s * 258 + 1:s * 258 + 257], 256, hp_all)
                CP("act", QT[:, h, NOWN + s * 256:NOWN + (s + 1) * 256], banks[b][:, 0:256], [BK[b]], [QT.R(h)])
    flush_deferred()
    for b_ in qraw + rtmp + rtmp2 + [ropec, ropes, permb]:
        AR.release(b_)
    DBG("QT", QT, [QT.R(h) for h in range(NH)])
    DBG("KTs", KTs, [KTs.R(g) for g in range(2)])
    DBG("Vs", Vs, [Vs.R()])

    x0 = AR.alloc([128, 4, NM], BF16)
    zs = AR.alloc([128, 4, NOWN], BF16)
    zso = AR.alloc([128, 4, NOWN], BF16)
    zp = AR.alloc([128, 4, 2 * LP], BF16)
    cacc = [AR.alloc([128, 512], F32) for _ in range(4)]
    ci = [0]
    ocw, ocb = _CL["convw"][0], _CL["convb"][0]

    def conv_from_psum(b, n, chg, out_ap, out_rd_wr):
        a_ = cacc[ci[0] % 4]
        ci[0] += 1
        w = lambda tap: cols[:, ocw + chg * 3 + tap:ocw + chg * 3 + tap + 1]
        ACT(a_[:, 0:n], banks[b][:, 1:n + 1], AF.Identity, [BK[b], cols.R()], [a_.R()], scale=w(1),
            bias=cols[:, ocb + chg:ocb + chg + 1])
        STT(a_[:, 0:n], banks[b][:, 0:n], w(0), a_[:, 0:n], ALU.mult, ALU.add, [BK[b], cols.R(), a_.R()], [a_.R()])
        if out_ap is None:
            STT(a_[:, 0:n], banks[b][:, 2:n + 2], w(2), a_[:, 0:n], ALU.mult, ALU.add, [BK[b], cols.R(), a_.R()], [a_.R()])
            return a_
        STT(out_ap, banks[b][:, 2:n + 2], w(2), a_[:, 0:n], ALU.mult, ALU.add, [BK[b], cols.R(), a_.R()], out_rd_wr)
        return None

    def hy_blocks(T0, T1):
        out = []
        s = T0
        while s < T1:
            n = min(510, T1 - s)
            out.append((s, n))
            s += n
        return out

    slot1 = wload(win_d[:, 2048:2560], 8, 512)
    slot2 = wload(win_d[:, 2560:3072], 8, 512)
    slot_x0 = wload(win_d[:, 1536:2048], 8, 512)
    for ch in range(4):
        segs = [("s", s, n) for (s, n) in hy_blocks(0, NOWN) + hy_blocks(NOWN, LS)] + [("p", 0, 256), ("p", 1, 256)]
        for kind, s, n in segs:
            if kind == "s":
                rhs = lambda k, s=s, n=n: hTs[:, k, s:s + n + 2]
                rd = hs_all
            else:
                rhs = lambda k, s=s: hTp[:, k, s * 258:s * 258 + 258]
                rd = hp_all
            b1 = proj_fm(slot1, ch * 128, rhs, n + 2, rd)
            b2 = proj_fm(slot2, ch * 128, rhs, n + 2, rd)
            a1 = conv_from_psum(b1, n, 4 + ch, None, None)
            a2 = conv_from_psum(b2, n, 8 + ch, None, None)
            if kind == "s":
                zb_, s_ = (zs, s) if s < NOWN else (zso, s - NOWN)
                TT("pool", zb_[:, ch, s_:s_ + n], a1[:, 0:n], a2[:, 0:n], ALU.mult, [a1.R(), a2.R()], [zb_.R(ch)])
            else:
                TT("pool", zp[:, ch, s * 256:(s + 1) * 256], a1[:, 0:n], a2[:, 0:n], ALU.mult, [a1.R(), a2.R()], [zp.R(ch)])
    slot = slot_x0
    for ch in range(4):
        for (s, n) in hy_blocks(0, NOWN):
            b = proj_fm(slot, ch * 128, lambda k, s=s, n=n: hTs[:, k, s:s + n + 2], n + 2, hs_all)
            conv_from_psum(b, n, ch, x0[:, ch, s:s + n], [x0.R(ch)])
        for s in range(2):
            b = proj_fm(slot, ch * 128, lambda k, s=s: hTp[:, k, s * 258:s * 258 + 258], 258, hp_all)
            conv_from_psum(b, 256, ch, x0[:, ch, NOWN + s * 256:NOWN + (s + 1) * 256], [x0.R(ch)])
    for a_ in cacc:
        AR.release(a_)
    for b_ in ring2:
        ring.remove(b_)
        AR.release(b_)
    AR.release(hTs)
    AR.release(hTp)
    DBG("x0", x0, [x0.R(c) for c in range(4)])
    DBG("zs", zs, [zs.R(c) for c in range(4)])
    DBG("zp", zp, [zp.R(c) for c in range(4)])
    zTs = AR.alloc([128, 16, HW], BF16)
    zTp = AR.alloc([128, 4, HW], BF16)

    def z_transpose(j):
        b = 2 if j % 2 == 0 else 6
        pT = banks[b][:, :].bitcast(BF16)
        for hf in range(2):
            for c2 in range(2):
                ch = hf * 2 + c2
                if j < 8:
                    src, sr = zs[:, ch, j * 128:(j + 1) * 128], zs.R(ch)
                elif j < 16:
                    src, sr = zso[:, ch, (j - 8) * 128:(j - 7) * 128], zso.R(ch)
                else:
                    src, sr = zp[:, ch, (j - 16) * 128:(j - 15) * 128], zp.R(ch)
                TR(pT[:, 768 + c2 * 128:768 + (c2 + 1) * 128], src, [sr], [BK[b]], inc=(c2 == 1))
            dst = zTs[:, j, hf * 256:(hf + 1) * 256] if j < 16 else zTp[:, j - 16, hf * 256:(hf + 1) * 256]
            dres = zTs.R(j) if j < 16 else zTp.R(j - 16)
            CP("act" if hf else "dve", dst, pT[:, 768:1024], [BK[b]], [dres])

    zt_pending = list(range(20))
    attnT = AR.alloc([128, NH, NM], BF16)
    esink = AR.alloc([128, 8], F32)
    ACT(esink.ap, col("sink", 0, 8), AF.Exp, [cols.R()], [esink.R()])
    esrow = AR.alloc([1, 8, 128], BF16)
    eunit = AR.alloc([1, 129], BF16)
    MSET("dve", eunit[0:1, 0:128], 0.0, [eunit.R()])
    MSET("dve", eunit[0:1, 128:129], 1.0, [eunit.R()])
    for h in range(8):
        ACT(esrow[0:1, h, :], zeros[0:1, 0:128], AF.Identity, [zeros.R(), esink.R()], [esrow.R()],
            bias=esink[0:1, h:h + 1], scale=1.0)
    PT = [AR.alloc([128, 5, 256], BF16) for _ in range(2)]
    Otok = [AR.alloc([128, 256], BF16) for _ in range(3)]
    dent = [AR.alloc([128, 4], F32) for _ in range(3)]
    it = [0]

    iters = []

    def attn_iter(heads, qcol, slots, tokcol):
        iters.append((heads, qcol, slots, tokcol))

    def stage_A(i):
        heads, qcol, slots, tokcol = iters[i]
        p = i % 2
        sb = [4 * p, 4 * p + 1, 4 * p + 2]
        P_ = PT[p]
        ns = len(slots)
        for si, (kt, v, mask, rd) in enumerate(slots):
            b = sb[si // 2]
            for hh, h in enumerate(heads):
                o_ = banks[b][:, (si % 2) * 256 + hh * 128:(si % 2) * 256 + (hh + 1) * 128]
                MM(o_, kt, QT[:, h, qcol:qcol + 128], True, mask is None, rd + [QT.R(h)], [BK[b]],
                   inc=(mask is None and hh == 1))
                if mask is not None:
                    MM(o_, ident.ap, mask.ap, False, True, [ident.R(), mask.R()], [BK[b]], inc=(hh == 1))
        for bi in range((ns + 1) // 2):
            w = min(2, ns - 2 * bi) * 256
            ACT(P_[:, 2 * bi:2 * bi + w // 256, :].rearrange("p a c -> p (a c)"), banks[sb[bi]][:, 0:w], AF.Exp,
                [BK[sb[bi]]], [P_.R()], scale=SCALE)

    def stage_B(i):
        heads, qcol, slots, tokcol = iters[i]
        p = i % 2
        ob = 4 * p + 3
        P_ = PT[p]
        ns = len(slots)
        for hh, h in enumerate(heads):
            for si, (kt, v, mask, rd) in enumerate(slots):
                MM(banks[ob][:, hh * 129:(hh + 1) * 129], P_[:, si, hh * 128:(hh + 1) * 128], v, si == 0, si == ns - 1,
                   [P_.R()] + rd, [BK[ob]], inc=(si == ns - 1))
        d_ = dent[i % 3]
        O_ = Otok[i % 3]
        h0 = heads[0]
        TT("dve", d_[:, 0:2], banks[ob][:, 0:258].rearrange("p (h c) -> p h c", h=2)[:, :, 128],
           esink[:, h0:h0 + 2], ALU.add, [BK[ob], esink.R()], [d_.R()])
        RECIP(d_[:, 2:4], d_[:, 0:2], [d_.R()], [d_.R()])
        for hh in range(2):
            TS("dve", O_[:, hh * 128:(hh + 1) * 128], banks[ob][:, hh * 129:hh * 129 + 128], d_[:, 2 + hh:3 + hh], 0.0,
               ALU.mult, ALU.add, [BK[ob], d_.R()], [O_.R()])

    def stage_T(i):
        heads, qcol, slots, tokcol = iters[i]
        p = i % 2
        tb = 4 * p + 2
        O_ = Otok[i % 3]
        h0 = heads[0]
        pT = banks[tb][:, :].bitcast(BF16)
        for hh in range(2):
            TR(pT[:, 512 + hh * 128:512 + (hh + 1) * 128], O_[:, hh * 128:(hh + 1) * 128], [O_.R()], [BK[tb]], inc=(hh == 1))
        CP("dve", attnT[:, h0:h0 + 2, tokcol:tokcol + 128], pT[:, 512:768].rearrange("p (h t) -> p h t", h=2),
           [BK[tb]], [attnT.R(h0)])

    for g in range(2):
        for qb in range(8):
            slots = []
            for kb, mask in ((qb - 1, maskl), (qb, None), (qb + 1, maskr)):
                if kb < 0:
                    continue
                slots.append((KTs[:, g, kb * 128:(kb + 1) * 128], Vs[:, kb, g, :], mask, [KTs.R(g), Vs.R()]))
            for cb in range(2):
                slots.append((KTc[:, g, cb * 128:(cb + 1) * 128], Vc[:, cb, g, :], None, [KTc.R(), Vc.R()]))
            for hp_ in range(2):
                heads = [g * 4 + hp_ * 2, g * 4 + hp_ * 2 + 1]
                attn_iter(heads, qb * 128, slots, qb * 128)
    for s in range(2):
        for g in range(2):
            slots = [(KTp[:, g, s * 256 + kb * 128:s * 256 + (kb + 1) * 128], Vp[:, s * 2 + kb, g, :], None, [KTp.R(g), Vp.R()])
                     for kb in range(2)]
            for qb in range(2):
                for hp_ in range(2):
                    heads = [g * 4 + hp_ * 2, g * 4 + hp_ * 2 + 1]
                    c0 = NOWN + s * 256 + qb * 128
                    attn_iter(heads, c0, slots, c0)
    nit = len(iters)
    for step in range(nit + 3):
        if step < nit:
            stage_A(step)
        if 0 <= step - 3 < nit:
            stage_T(step - 3)
        if 0 <= step - 1 < nit:
            stage_B(step - 1)
        if zt_pending and step % 2 == 1:
            z_transpose(zt_pending.pop(0))
    while zt_pending:
        z_transpose(zt_pending.pop(0))
    AR.release(zso)
    for b_ in PT + Otok + dent + [QT, KTs, KTp, KTc, Vs, Vp, Vc, esink, esrow, eunit]:
        AR.release(b_)
    DBG("attnT", attnT, [attnT.R(h) for h in range(0, NH, 2)])

    if stop == '4':
        return finish()
    hyT = AR.alloc([128, 4, NM], BF16)
    osk = _CL["skip"][0]

    def hyena_dft(L, zT, zfm, zfm_c0, tok0, nown, after_fwd=None):
        nt = L // 128
        Ub = AR.alloc([128, nt, HW], BF16)
        Vb = AR.alloc([128, nt, HW], BF16)
        dsl = [AR.alloc([128, nt, 256], BF16) for _ in range(4)]
        pqs = [AR.alloc([128, 2, HW], BF16) for _ in range(2)]
        pv0 = AR.alloc([128, HW], BF16)
        mm_ = [AR.alloc([128, 4, HW], F32) for _ in range(1)]
        DMA("sp", pv0.ap, spec_d[L][nt][:, 0:HW], [DR(f"spec{L}")], [pv0.R()])
        di = 0
        nfp = L // 256
        loads = {}

        def issue(fp):
            nonlocal di
            cs, ms = dsl[di % 4], dsl[(di + 1) % 4]
            di += 2
            DMA("sp", cs.ap, cm_d[L][fp], [], [cs.R()])
            DMA("sp", ms.ap, m2f_d[L][fp], [], [ms.R()])
            loads[fp] = (cs, ms)

        issue(0)
        yield "fwd"
        for fp in range(nfp):
            if fp + 1 < nfp:
                issue(fp + 1)
            cs, ms = loads.pop(fp)
            for f2 in range(2):
                fi = fp * 2 + f2
                p_ = pqs[fi % 2]
                DMA("sp", p_.ap, spec_d[L][fi].rearrange("p (a c) -> p a c", a=2), [DR(f"spec{L}")], [p_.R()])
                bA = (fi % 2) * 2
                bB = bA + 1
                for k in range(nt):
                    MM(banks[bA][:, :], cs[:, k, f2 * 128:(f2 + 1) * 128], zT[:, k, :], k == 0, k == nt - 1,
                       [cs.R(), zT.R(k)], [BK[bA]], inc=(k == nt - 1))
                for k in range(nt):
                    MM(banks[bB][:, :], ms[:, k, f2 * 128:(f2 + 1) * 128], zT[:, k, :], k == 0, k == nt - 1,
                       [ms.R(), zT.R(k)], [BK[bB]], inc=(k == nt - 1))
                m_ = mm_[0]
                pv = pv0.ap if fi == 0 else p_[:, 0, :]
                pvr = [pv0.R()] if fi == 0 else []
                TT("dve", m_[:, 0, :], banks[bA][:, :], p_[:, 0, :], ALU.mult, [BK[bA], p_.R()], [m_.R(0)])
                TT("dve", m_[:, 3, :], banks[bA][:, :], p_[:, 1, :], ALU.mult, [BK[bA], p_.R()], [m_.R(3)])
                TT("dve", m_[:, 1, :], banks[bB][:, :], p_[:, 1, :], ALU.mult, [BK[bB], p_.R()], [m_.R(1)])
                TT("dve", m_[:, 2, :], banks[bB][:, :], pv, ALU.mult, [BK[bB], p_.R()] + pvr, [m_.R(2)])
                TT("dve", Ub[:, fi, :], m_[:, 0, :], m_[:, 1, :], ALU.subtract, [m_.R(0), m_.R(1)], [Ub.R(fi)])
                TT("dve", Vb[:, fi, :], m_[:, 2, :], m_[:, 3, :], ALU.add, [m_.R(2), m_.R(3)], [Vb.R(fi)])
                yield "fwd"
        for b_ in pqs + [pv0] + mm_:
            AR.release(b_)
        if after_fwd is not None:
            after_fwd()
        ytmp = [AR.alloc([128, 256], F32) for _ in range(2)]
        yi = 0
        ntq = nown // 256
        iloads = {}

        def issue_inv(tq):
            nonlocal di
            cs, ms = dsl[di % 4], dsl[(di + 1) % 4]
            di += 2
            DMA("sp", cs.ap, cm_d[L][tq], [], [cs.R()])
            DMA("sp", ms.ap, m2i_d[L][tq], [], [ms.R()])
            iloads[tq] = (cs, ms)

        issue_inv(0)
        yield "inv"
        for tq in range(ntq):
            if tq + 1 < ntq:
                issue_inv(tq + 1)
            cs, ms = iloads.pop(tq)
            for ct in range(4):
                b = 4 + (yi % 4)
                for k in range(2 * nt):
                    src = cs if k < nt else ms
                    uvb = Ub if k < nt else Vb
                    MM(banks[b][:, 0:256], uvb[:, k % nt, ct * 128:(ct + 1) * 128], src[:, k % nt, :], k == 0, k == 2 * nt - 1,
                       [uvb.R(k % nt), src.R()], [BK[b]], inc=(k == 2 * nt - 1))
                y_ = ytmp[yi % 2]
                yi += 1
                zsrc, zres = zfm(ct, tq * 256)
                STT(y_.ap, zsrc, cols[:, osk + ct:osk + ct + 1], banks[b][:, 0:256], ALU.mult, ALU.add,
                    [zres, cols.R(), BK[b]], [y_.R()])
                TT("dve", hyT[:, ct, tok0 + tq * 256:tok0 + (tq + 1) * 256], y_.ap,
                   x0[:, ct, tok0 + tq * 256:tok0 + (tq + 1) * 256], ALU.mult, [y_.R(), x0.R(ct)], [hyT.R(ct)])
                if ct % 2 == 1:
                    yield "inv"
        for b_ in ytmp + dsl + [Ub, Vb]:
            AR.release(b_)

    class _V:
        def __init__(self, s):
            self.s = s

        def __getitem__(self, idx):
            p, k, c = idx
            return zTp[p, 2 * self.s + k, c]

        def R(self, k):
            return zTp.R(2 * self.s + k)

    def prompt_gens():
        for s_ in range(2):
            yield from hyena_dft(LP, _V(s_), lambda ct, c0, s_=s_: (zp[:, ct, s_ * 256 + c0:s_ * 256 + c0 + 256], zp.R(ct)),
                                 0, NOWN + s_ * 256, LP)

    pg = prompt_gens()
    for tag in hyena_dft(LS, zTs, lambda ct, c0: (zs[:, ct, c0:c0 + 256], zs.R(ct)), 0, 0, NOWN,
                         after_fwd=lambda: AR.release(zTs)):
        if tag == "inv":
            try:
                next(pg)
                next(pg)
            except StopIteration:
                pass
    for _ in pg:
        pass
    for b_ in [zTp, zs, zp, x0]:
        AR.release(b_)
    DBG("hyT", hyT, [hyT.R(c) for c in range(4)])


    if stop == '5':
        return finish()
    hTm = AR.alloc([128, 8, NM], BF16)
    ring_extra = [AR.alloc([128, 8, 512], BF16) for _ in range(2)]
    ring.extend(ring_extra)
    xring = [AR.alloc([128, D], F32) for _ in range(8)]
    xi[0] = 0

    def main_rows(i):
        if i < 8:
            return xs_d[i * 128:(i + 1) * 128, :]
        return xp_d[(i - 8) * 128:(i - 7) * 128, :]

    groups = []
    for g in range(3):
        groups.append(dict(srcs=loader([main_rows(4 * g + a) for a in range(4)]),
                           dst_aps=[hTm[:, k, g * 512:(g + 1) * 512] for k in range(8)],
                           dst_res=[hTm.R(k) for k in range(8)], n_idx=0, conds=[1 if g < 2 else 0] * 4))
    hy_all = [hyT.R(c) for c in range(4)]
    phT = AR.alloc([128, 8, NM], BF16)
    ph_todo = [0, 1]

    def ph_block():
        if not ph_todo:
            return
        fb2 = ph_todo.pop(0)
        s_ph = wload(wph_d[:, fb2 * 512:(fb2 + 1) * 512], 4, 512)
        for f4 in range(4):
            fc = fb2 * 4 + f4
            for tb in range(3):
                tsl = slice(tb * 512, (tb + 1) * 512)
                b = 4 + pbank() % 4
                for k in range(4):
                    MM(banks[b][:, :], s_ph[:, k, f4 * 128:(f4 + 1) * 128], hyT[:, k, tsl], k == 0, k == 3,
                       [s_ph.R()] + hy_all, [BK[b]], inc=(k == 3))
                CP("act" if tb % 2 else "dve", phT[:, fc, tsl], banks[b][:, :], [BK[b]], [phT.R(fc)])

    prenorm_run(groups, between=ph_block)
    for t in xring:
        AR.release(t)
    hm_all = [hTm.R(k) for k in range(8)]
    at_all = [attnT.R(h) for h in range(0, NH, 2)]
    while ph_todo:
        ph_block()
    for fb2 in range(0):
        s_ph = wload(wph_d[:, fb2 * 512:(fb2 + 1) * 512], 4, 512)
        for f4 in range(4):
            fc = fb2 * 4 + f4
            for tb in range(3):
                tsl = slice(tb * 512, (tb + 1) * 512)
                b = pbank()
                for k in range(4):
                    MM(banks[b][:, :], s_ph[:, k, f4 * 128:(f4 + 1) * 128], hyT[:, k, tsl], k == 0, k == 3,
                       [s_ph.R()] + hy_all, [BK[b]], inc=(k == 3))
                CP("act" if tb % 2 else "dve", phT[:, fc, tsl], banks[b][:, :], [BK[b]], [phT.R(fc)])
    AR.release(hyT)
    mergedT = AR.alloc([128, 8, NM], BF16)
    sg = [AR.alloc([128, 4, 512], F32) for _ in range(2)]
    obg = _CL["bgate"][0]
    mi = 0
    for fb2 in range(2):
        s_pa = wload(wpa_d[:, fb2 * 512:(fb2 + 1) * 512], 8, 512)
        s_ga = wload(wgate_d[:, fb2 * 512:(fb2 + 1) * 512], 8, 512)
        s_gh = wload(wgate_d[:, D + fb2 * 512:D + (fb2 + 1) * 512], 8, 512)
        for f4 in range(4):
            fc = fb2 * 4 + f4
            for tb in range(3):
                tsl = slice(tb * 512, (tb + 1) * 512)
                base = 4 * (mi % 2)
                g_ = sg[mi % 2]
                mi += 1
                for k in range(8):
                    MM(banks[base][:, :], s_pa[:, k, f4 * 128:(f4 + 1) * 128], attnT[:, k, tsl], k == 0, k == 7,
                       [s_pa.R()] + at_all, [BK[base]], inc=(k == 7))
                for k in range(8):
                    MM(banks[base + 1][:, :], s_ga[:, k, f4 * 128:(f4 + 1) * 128], hTm[:, k, tsl], k == 0, k == 7,
                       [s_ga.R()] + hm_all, [BK[base + 1]], inc=(k == 7))
                for k in range(8):
                    MM(banks[base + 2][:, :], s_gh[:, k, f4 * 128:(f4 + 1) * 128], hTm[:, k, tsl], k == 0, k == 7,
                       [s_gh.R()] + hm_all, [BK[base + 2]], inc=(k == 7))
                ACT(g_[:, 0, :], banks[base + 1][:, :], AF.Sigmoid, [BK[base + 1], cols.R()], [g_.R(0)],
                    bias=cols[:, obg + fc:obg + fc + 1])
                ACT(g_[:, 1, :], banks[base + 2][:, :], AF.Sigmoid, [BK[base + 2], cols.R()], [g_.R(1)],
                    bias=cols[:, obg + 8 + fc:obg + 8 + fc + 1])
                TT("dve", g_[:, 2, :], banks[base][:, :], g_[:, 0, :], ALU.mult, [BK[base], g_.R(0)], [g_.R(2)])
                TT("dve", g_[:, 3, :], phT[:, fc, tsl], g_[:, 1, :], ALU.mult, [phT.R(fc), g_.R(1)], [g_.R(3)])
                TT("dve", mergedT[:, fc, tsl], g_[:, 2, :], g_[:, 3, :], ALU.add, [g_.R(2), g_.R(3)], [mergedT.R(fc)])
    for b_ in sg + [phT, hTm, attnT]:
        AR.release(b_)
    mg_all = [mergedT.R(fc) for fc in range(8)]
    DBG("mergedT", mergedT, mg_all)

    if stop == '6':
        return finish()
    def gate_rows(part, brow_i, nrow_i):
        G = AR.alloc([128, 2, D], F32)
        rw = AR.alloc([128, 2, D], F32)
        DMA("sp", rw[:, 0, :], rows_d[:, brow_i, :], [], [rw.R(0)])
        DMA("sp", rw[:, 1, :], rows_d[:, nrow_i, :], [], [rw.R(1)])
        for half in range(2):
            slot = wload(wmod_d[:, part * D + half * 512:part * D + (half + 1) * 512], 8, 512)
            for c in range(2):
                b = pbank()
                for k in range(8):
                    MM(banks[b][:, :], srep[:, k, c, :], slot[:, k, :], k == 0, k == 7, [srep.R(), slot.R()], [BK[b]],
                       inc=(k == 7))
                hs = slice(half * 512, (half + 1) * 512)
                TT("dve", G[:, c, hs], banks[b][:, :], rw[:, 0, hs], ALU.add, [BK[b], rw.R(0)], [G.R(c)])
                TT("dve", G[:, c, hs], G[:, c, hs], rw[:, 1, hs], ALU.mult, [G.R(c), rw.R(1)], [G.R(c)])
        AR.release(rw)
        return G

    pn_sets = []
    pn_i = [0]

    def postnorm_alloc(n=3):
        for _ in range(n):
            pn_sets.append((AR.alloc([128, 4], F32), AR.alloc([128, D], F32)))

    def postnorm_free():
        for ss_, tmp_ in pn_sets:
            AR.release(ss_)
            AR.release(tmp_)
        pn_sets.clear()

    def postnorm_residual(bk0, bk1, G, cond, xtile, xres):
        ss, tmp = pn_sets[pn_i[0] % len(pn_sets)]
        pn_i[0] += 1
        ACT(junk[:, 0:512], banks[bk0][:, :], AF.Square, [BK[bk0]], [junk.R(), ss.R()], accum=ss[:, 0:1])
        ACT(junk[:, 512:1024], banks[bk1][:, :], AF.Square, [BK[bk1]], [junk.R(), ss.R()], accum=ss[:, 1:2])
        TT("dve", ss[:, 2:3], ss[:, 0:1], ss[:, 1:2], ALU.add, [ss.R()], [ss.R()])
        ACT(ss[:, 2:3], ss[:, 2:3], AF.Sqrt, [ss.R(), epsc.R()], [ss.R()], scale=1.0 / D, bias=epsc[:, 0:1])
        RECIP(ss[:, 3:4], ss[:, 2:3], [ss.R()], [ss.R()])
        STT(tmp[:, 0:512], banks[bk0][:, :], ss[:, 3:4], G[:, cond, 0:512], ALU.mult, ALU.mult, [BK[bk0], ss.R(), G.R(cond)], [tmp.R()])
        STT(tmp[:, 512:1024], banks[bk1][:, :], ss[:, 3:4], G[:, cond, 512:1024], ALU.mult, ALU.mult, [BK[bk1], ss.R(), G.R(cond)], [tmp.R()])
        TT("dve", xtile, xtile, tmp.ap, ALU.add, [tmp.R(), xres], [xres])

    G1 = gate_rows(2, 2, 0)
    x1 = AR.alloc([128, 12, D], F32)
    for i in range(12):
        DMA("sp", x1[:, i, :], main_rows(i), [], [x1.R(i)])
    postnorm_alloc(3)
    wo = [wload(wo_d[:, half * 512:(half + 1) * 512], 8, 512) for half in range(2)]
    for i in range(12):
        bb = [(2 * i) % 8, (2 * i + 1) % 8]
        for half in range(2):
            for k in range(8):
                MM(banks[bb[half]][:, :], mergedT[:, k, i * 128:(i + 1) * 128], wo[half][:, k, :], k == 0, k == 7,
                   [wo[half].R()] + mg_all, [BK[bb[half]]], inc=(k == 7))
        postnorm_residual(bb[0], bb[1], G1, 1 if i < 8 else 0, x1[:, i, :], x1.R(i))
    postnorm_free()
    AR.release(mergedT)
    AR.release(G1)
    DBG("x1", x1, [x1.R(i) for i in range(12)])

    if stop == '7':
        return finish()
    h2T = AR.alloc([128, 8, NM], BF16)
    groups = []
    for g in range(3):
        groups.append(dict(srcs=(lambda g=g: ([x1[:, 4 * g + a, :] for a in range(4)], [x1.R(4 * g + a) for a in range(4)])),
                           dst_aps=[h2T[:, k, g * 512:(g + 1) * 512] for k in range(8)],
                           dst_res=[h2T.R(k) for k in range(8)], n_idx=1, conds=[1 if g < 2 else 0] * 4))
    g2box = []

    def g2_once():
        if not g2box:
            g2box.append(gate_rows(5, 3, 1))

    prenorm_run(groups, between=g2_once)
    h2_all = [h2T.R(k) for k in range(8)]

    for b_ in ring_extra:
        ring.remove(b_)
        AR.release(b_)
    aT_bufs = []
    aT_map = []
    need = 22
    while need > 0:
        best = max(AR.free, key=lambda se: se[1] - se[0])
        can = min(need, (best[1] - best[0]) // (NM * 2))
        assert can > 0, ("arena too fragmented for aT", AR.free)
        b_ = AR.alloc([128, can, NM], BF16)
        for j_ in range(can):
            aT_map.append((b_, j_))
        aT_bufs.append(b_)
        need -= can
    sil = [AR.alloc([128, 512], F32) for _ in range(2)]
    ui = 0
    halves = []
    hres_all = []
    for sl_ in ring:
        full = sl_.R()
        toks = _reduce([full.w] + full.r)
        hr = [Res("h0", toks), Res("h1", toks)]
        hres_all.append((sl_, hr))
        halves += [(sl_[:, :, 0:256], hr[0]), (sl_[:, :, 256:512], hr[1])]
    hi_ = [0]

    def hload(dram_ap):
        ap_, r_ = halves[hi_[0] % len(halves)]
        hi_[0] += 1
        DMA("pool", ap_, dram_ap.rearrange("(k p) n -> p k n", p=128), [], [r_])
        return ap_, r_

    for g2 in range(11):
        ga, gr = hload(wup_d[:, g2 * 256:(g2 + 1) * 256])
        ua, ur = hload(wup_d[:, DFF + g2 * 256:DFF + (g2 + 1) * 256])
        for f2 in range(2):
            fc = g2 * 2 + f2
            for tb in range(3):
                tsl = slice(tb * 512, (tb + 1) * 512)
                base = 2 * (ui % 4)
                s_ = sil[ui % 2]
                ui += 1
                for k in range(8):
                    MM(banks[base][:, :], ga[:, k, f2 * 128:(f2 + 1) * 128], h2T[:, k, tsl], k == 0, k == 7,
                       [gr] + h2_all, [BK[base]], inc=(k == 7))
                for k in range(8):
                    MM(banks[base + 1][:, :], ua[:, k, f2 * 128:(f2 + 1) * 128], h2T[:, k, tsl], k == 0, k == 7,
                       [ur] + h2_all, [BK[base + 1]], inc=(k == 7))
                ACT(s_.ap, banks[base][:, :], AF.Silu, [BK[base]], [s_.R()])
                TT("dve", aT_map[fc][0][:, aT_map[fc][1], tsl], banks[base + 1][:, :], s_.ap, ALU.mult, [BK[base + 1], s_.R()],
                   [aT_map[fc][0].R(aT_map[fc][1])])
    for sl_, hr in hres_all:
        full = sl_.R()
        toks = list(full.r) + [full.w]
        for h_ in hr:
            toks += [h_.w] + h_.r
        full.r = _reduce(toks)
    for b_ in sil + [h2T]:
        AR.release(b_)
    a_all = [aT_map[fc][0].R(aT_map[fc][1]) for fc in range(22)]

    if stop == '9':
        return finish()
    G2 = g2box[0]
    postnorm_alloc(2)
    xs_ = [AR.alloc([128, 8, 512], BF16) for _ in range(3)]
    wd = {}
    for kg in range(3):
        kc = 8 if kg < 2 else 6
        for half in range(2):
            if half == 0:
                wd[(kg, half)] = wload(wdown_d[kg * 1024:kg * 1024 + kc * 128, 0:512], kc, 512)
            else:
                sl_ = xs_[kg]
                DMA("pool", sl_[:, 0:kc, :], wdown_d[kg * 1024:kg * 1024 + kc * 128, 512:1024].rearrange("(k p) n -> p k n", p=128),
                    [], [sl_.R()])
                wd[(kg, half)] = sl_
    for i in range(12):
        bb = [(2 * i) % 8, (2 * i + 1) % 8]
        for half in range(2):
            for k in range(22):
                w_ = wd[(k // 8, half)]
                MM(banks[bb[half]][:, :], aT_map[k][0][:, aT_map[k][1], i * 128:(i + 1) * 128], w_[:, k % 8, :], k == 0, k == 21,
                   [w_.R()] + a_all, [BK[bb[half]]], inc=(k == 21))
        postnorm_residual(bb[0], bb[1], G2, 1 if i < 8 else 0, x1[:, i, :], x1.R(i))
        if i < 8:
            out_toks.append(DMA("sp", ys_d[i * 128:(i + 1) * 128, :], x1[:, i, :], [x1.R(i)], [DR("ys")]))
        else:
            out_toks.append(DMA("sp", yp_d[(i - 8) * 128:(i - 7) * 128, :], x1[:, i, :], [x1.R(i)], [DR("yp")]))

    return finish()


def _colform(v):
    v = np.asarray(v, np.float32).reshape(-1, 128)
    return np.ascontiguousarray(v.T)


def _consts():
    c = {}
    c["ident"] = np.eye(128, dtype=np.float32).astype(BF)
    perm = np.zeros((128, 128), np.float32)
    for m in range(128):
        blk, i = divmod(m, 64)
        src = blk * 64 + (i + 32) % 64
        perm[src, m] = 1.0
    c["permf"] = perm
    j = np.arange(128)[:, None]
    i = np.arange(128)[None, :]
    c["maskl"] = np.where(j >= i, 0.0, NEG).astype(np.float32).astype(BF)
    c["maskr"] = np.where(j <= i, 0.0, NEG).astype(np.float32).astype(BF)
    for L in (LS, LP):
        t = np.arange(L, dtype=np.float32)
        bands = np.linspace(1e-4, 7, 8).astype(np.float32)
        ang = (2.0 * np.float32(math.pi) * t / np.float32(L))[:, None] * bands[None, :]
        feat = np.concatenate([(t / np.float32(L))[:, None], np.cos(ang), np.sin(ang)], axis=-1).astype(np.float32)
        c[f"feat{L}"] = np.ascontiguousarray(feat.T)
        ii = np.arange(L, dtype=np.int64)
        prod = np.outer(ii, ii) % (2 * L)
        th = prod.astype(np.float64) * (math.pi / L)
        cm = np.cos(th)
        m2 = np.sin(th)
        m2[:, 0] = (-1.0) ** ii
        nown = NOWN if L == LS else LP

        def tile_(mat):
            Lr, W = mat.shape
            return np.ascontiguousarray(mat.reshape(Lr // 128, 128, W // 256, 256).transpose(2, 1, 0, 3)).astype(np.float32).astype(BF)

        c[f"cm{L}"] = tile_(cm)
        c[f"m2f{L}"] = tile_(m2)
        c[f"m2i{L}"] = tile_(np.ascontiguousarray(m2.T[:, :nown]))
    return c


_CONSTS = None


def _rope_tables(tglob):
    half = 64
    inv = (10000.0 ** (-np.arange(0, half, 2, dtype=np.float32) / half)).astype(np.float32)
    row = (tglob // 64).astype(np.float32)
    colp = (tglob % 64).astype(np.float32)
    cosT = np.zeros((128, len(tglob)), np.float32)
    sinT = np.zeros((128, len(tglob)), np.float32)
    for i in range(128):
        pos = row if i < 64 else colp
        a = pos * inv[i % 32]
        cosT[i] = np.cos(a)
        sinT[i] = np.sin(a) * (-1.0 if (i % 64) < 32 else 1.0)
    return cosT, sinT


def prep_inputs(inp):
    global _CONSTS
    if _CONSTS is None:
        _CONSTS = _consts()
    f = lambda k: np.asarray(inp[k], np.float32)
    maps = []
    shared = dict(_CONSTS)
    shared["fw1"] = np.ascontiguousarray(f("filt_w1")[0])
    shared["fw2"] = np.ascontiguousarray(f("filt_w2")[0])
    shared["fw3"] = np.ascontiguousarray(np.concatenate([f("filt_w3")[0], f("filt_b3")[0][None, :]], axis=0))
    for k in ("w_mod", "w_in", "w_gate", "w_pa", "w_ph", "w_o", "w_up", "w_down"):
        shared[k] = np.ascontiguousarray(f(k)[0])
    bmod = f("b_mod")[0]
    rows_common = [f("norm_mix_post")[0], f("norm_ffn_post")[0], bmod[2 * D:3 * D], bmod[5 * D:6 * D], f("filt_decay")[0]]
    rows = np.ascontiguousarray(np.broadcast_to(np.stack(rows_common)[None], (128, 5, D))).astype(np.float32)
    for r in range(8):
        b, half = divmod(r, 2)
        m = dict(shared)
        xs = f("x_sample")[b]
        tglob = np.arange(LS)
        if half:
            xs = xs[::-1]
            tglob = tglob[::-1]
        m["xs"] = np.ascontiguousarray(xs)
        xpr = f("x_prompt")[2 * r:2 * r + 2]
        if half:
            xpr = xpr[:, ::-1]
        m["xp"] = np.ascontiguousarray(xpr).reshape(2 * LP, D)
        m["rows"] = rows
        m["kc"] = np.ascontiguousarray(f("cache_k")[b, 0].reshape(LP, 256))
        m["vc"] = np.ascontiguousarray(f("cache_v")[b, 0].reshape(LP, 256))
        cT, sT = _rope_tables(tglob[:NKEY])
        m["ropec"], m["ropes"] = cT, sT
        cols = np.zeros((128, NCOL), np.float32)

        def put(name, arr):
            o, w = _CL[name]
            assert arr.shape == (128, w), (name, arr.shape, w)
            cols[:, o:o + w] = arr

        cc = np.stack([_colform(f("c_ctx")), _colform(f("c")[b])], axis=-1)
        put("ccond", cc.reshape(128, 16))
        put("gpre1", _colform(f("norm_mix_pre")[0]))
        put("gpre2", _colform(f("norm_ffn_pre")[0]))
        put("bmodc", np.concatenate([_colform(bmod[p * D:(p + 1) * D]) for p in (0, 1, 3, 4)], axis=1))
        cw = f("conv_w")[0]
        if half:
            cw = cw[::-1]
        put("convw", np.stack([_colform(cw[t]) for t in range(3)], axis=-1).reshape(128, 36))
        put("convb", _colform(f("conv_b")[0]))
        put("bgate", _colform(f("b_gate")[0]))
        put("skip", _colform(f("hyena_skip")[0]))
        fv = np.zeros((128, 4), np.float32)
        fv[:64, 0] = f("filt_b1")[0]
        fv[:64, 1] = f("filt_freq1")[0]
        fv[:64, 2] = f("filt_b2")[0]
        fv[:64, 3] = f("filt_freq2")[0]
        put("fvec", fv)
        for L, nm in ((LS, "2048"), (LP, "256")):
            t = np.arange(L, dtype=np.float32)
            put("negt" + nm, _colform(-(t / np.float32(L))))
            wp = np.full(L, 1.0 / L, np.float32)
            wp[0] = 1.0 / (2 * L)
            sign = -1.0 if half else 1.0
            wq = np.full(L, sign / L, np.float32)
            wq[0] = 0.0
            put("wP" + nm, _colform(wp))
            put("wQ" + nm, _colform(wq))
        put("sink", np.ascontiguousarray(np.broadcast_to(f("attn_sink")[0][None, :], (128, 8))))
        m["cols"] = cols
        maps.append(m)
    return maps


_NC = None


def kernel(**inputs):
    global _NC
    if _NC is None:
        _NC = build()[0]
    maps = prep_inputs(inputs)
    res = run_bass_kernel_spmd(_NC, maps, core_ids=list(range(8)))
    B, Bd = 16, 4
    y_prompt = np.zeros((B, LP, D), np.float32)
    y_sample = np.zeros((Bd, LS, D), np.float32)
    new_k = np.zeros((B, 1, LP, NKV, HD), np.float32)
    new_v = np.zeros((B, 1, LP, NKV, HD), np.float32)
    for r in range(8):
        o = res.results[r]
        b, half = divmod(r, 2)
        rv = (lambda a: a[:, ::-1]) if half else (lambda a: a)
        y_prompt[2 * r:2 * r + 2] = rv(np.asarray(o["yp"]).reshape(2, LP, D))
        ys = np.asarray(o["ys"])
        if half:
            y_sample[b, NOWN:] = ys[::-1]
        else:
            y_sample[b, :NOWN] = ys
        new_k[2 * r:2 * r + 2, 0] = rv(np.asarray(o["nk"]).reshape(2, LP, NKV, HD))
        new_v[2 * r:2 * r + 2, 0] = rv(np.asarray(o["nv"]).reshape(2, LP, NKV, HD))
    return (y_prompt, y_sample, new_k, new_v)
```

```python
import contextlib
import math
import numpy as np
import ml_dtypes
import concourse.bass as bass
import concourse.mybir as mybir
from concourse.bass_utils import run_bass_kernel_spmd

F32 = mybir.dt.float32
BF16 = mybir.dt.bfloat16
AF = mybir.ActivationFunctionType
ALU = mybir.AluOpType
BF = ml_dtypes.bfloat16

D = 1024
NH, NKV, HD, GRP = 8, 2, 128, 4
LS, LP = 2048, 256
NOWN = 1024
NKEY = 1152
NM = 1536
HW = 512
DFF = 2816
EPS = 1e-6
SCALE = HD ** -0.5
NEG = -30000.0

_CL = {}
_off = 0
for _n, _w in [("ccond", 16), ("gpre1", 8), ("gpre2", 8), ("bmodc", 32), ("convw", 36), ("convb", 12),
               ("bgate", 16), ("skip", 4), ("fvec", 4), ("negt2048", 16), ("negt256", 2),
               ("wP2048", 16), ("wQ2048", 16), ("wP256", 2), ("wQ256", 2), ("sink", 8)]:
    _CL[_n] = (_off, _w)
    _off += _w
NCOL = _off


class Res:
    __slots__ = ("name", "w", "r", "excl")

    def __init__(self, name, inherit=(), excl=False):
        self.name = name
        self.w = None
        self.r = list(inherit)
        self.excl = excl


class Tok:
    __slots__ = ("kind", "key", "val")

    def __init__(self, kind, key, val):
        self.kind, self.key, self.val = kind, key, val


class Sched:
    ENGS = ("pe", "act", "dve", "pool", "sp")

    def __init__(self, nc, ndma=8):
        self.nc = nc
        self.q = {e: [] for e in self.ENGS}
        self.n = {e: 0 for e in self.ENGS}
        self.pending = {e: [] for e in self.ENGS}
        self.seen = {e: {} for e in self.ENGS}
        self.ndma = ndma
        self.dma_i = {e: 0 for e in self.ENGS}
        self.dma_cnt = {}
        self.ninstr = 0

    def _need_waits(self, eng, toks):
        best = {}
        for t in toks:
            if t is None:
                continue
            if t.kind == "e":
                if t.key == eng and eng == "pe":
                    continue
                if t.val is None:
                    raise RuntimeError(f"dependency on pending token of {t.key} from {eng}")
            key = (t.kind, t.key)
            if self.seen[eng].get(key, 0) >= t.val:
                continue
            best[key] = max(best.get(key, 0), t.val)
        for k, v in best.items():
            self.seen[eng][k] = v
        return list(best.items())

    def _deps(self, eng, reads, writes):
        toks = []
        for r in reads:
            toks.append(r.w)
        for w in writes:
            if w.w is not None and not (w.w.kind == "e" and w.w.key == eng):
                toks.append(w.w)
            for t in w.r:
                if t.kind == "e" and t.key == eng:
                    continue
                toks.append(t)
        return toks

    def _finish(self, tok, reads, writes):
        for r in reads:
            r.r.append(tok)
            if len(r.r) > 24:
                r.r = _reduce(r.r)
        for w in writes:
            w.w = tok
            w.r = []
        self.ninstr += 1
        return tok

    def op(self, eng, fn, reads=(), writes=(), inc=True, extra=()):
        ex = [r for r in reads if r.excl]
        if ex:
            reads = [r for r in reads if not r.excl]
            writes = list(writes) + [r for r in ex if r not in writes]
        toks = self._deps(eng, reads, writes) + list(extra)
        waits = self._need_waits(eng, toks)
        if inc:
            self.n[eng] += 1
            tok = Tok("e", eng, self.n[eng])
            for p in self.pending[eng]:
                p.val = self.n[eng]
            self.pending[eng] = []
        else:
            tok = Tok("e", eng, None)
            self.pending[eng].append(tok)
        self.q[eng].append((waits, fn, ("e", eng) if inc else None))
        return self._finish(tok, reads, writes)

    def dma(self, eng, fn, reads=(), writes=(), extra=()):
        toks = self._deps(eng, reads, writes) + list(extra)
        slot = self.dma_i[eng] % self.ndma
        self.dma_i[eng] += 1
        key = (eng, slot)
        prev = self.dma_cnt.get(key, 0)
        if prev:
            toks.append(Tok("d", key, prev))
        waits = self._need_waits(eng, toks)
        self.dma_cnt[key] = prev + 16
        tok = Tok("d", key, prev + 16)
        self.q[eng].append((waits, fn, ("d", key)))
        return self._finish(tok, reads, writes)

    def wait_all(self, eng, toks):
        waits = self._need_waits(eng, [t for t in toks if t is not None])
        self.q[eng].append((waits, None, None))

    def run(self, stack):
        nc = self.nc
        semobj = {}
        for e in self.ENGS:
            if self.n[e] > 0:
                semobj[("e", e)] = stack.enter_context(nc.semaphore(f"s_{e}"))
        for key in self.dma_cnt:
            semobj[("d", key)] = stack.enter_context(nc.semaphore(f"d_{key[0]}_{key[1]}"))
        block = stack.enter_context(nc.Block())
        names = {"pe": "tensor", "act": "scalar", "dve": "vector", "pool": "gpsimd", "sp": "sync"}

        def make(e):
            q = self.q[e]

            def body(h):
                for waits, fn, inc in q:
                    for key, val in waits:
                        h.wait_ge(semobj[key], val)
                    if fn is None:
                        continue
                    ins = fn()
                    if inc is not None:
                        ins.then_inc(semobj[inc], 16 if inc[0] == "d" else 1)
            return body

        for e in self.ENGS:
            if self.q[e]:
                getattr(block, names[e])(make(e))


def _reduce(toks):
    best = {}
    out = []
    for t in toks:
        if t is None:
            continue
        if t.val is None:
            out.append(t)
            continue
        k = (t.kind, t.key)
        if k not in best or best[k].val < t.val:
            best[k] = t
    return out + list(best.values())


class Buf:
    def __init__(self, ap, off, nbytes, inherit):
        self.ap = ap
        self.off = off
        self.nbytes = nbytes
        self.inherit = inherit
        self.res = {}

    def R(self, key=0):
        r = self.res.get(key)
        if r is None:
            r = Res(key, self.inherit)
            self.res[key] = r
        return r

    def __getitem__(self, idx):
        return self.ap[idx]


class Arena:
    def __init__(self, base_ap, nbytes):
        self.base = base_ap
        self.free = [(0, nbytes)]
        self.hist = []

    def alloc(self, shape, dtype):
        esz = 4 if dtype == F32 else 2
        n = esz
        for s in shape[1:]:
            n *= s
        n = (n + 63) // 64 * 64
        for i, (s, e) in enumerate(self.free):
            if e - s >= n:
                off = s
                if e - s == n:
                    self.free.pop(i)
                else:
                    self.free[i] = (s + n, e)
                break
        else:
            raise RuntimeError(f"arena out of memory for {shape} ({n} B); free={self.free}")
        inherit = []
        keep = []
        for (hs, he, toks) in self.hist:
            if hs < off + n and off < he:
                inherit += toks
                if hs < off:
                    keep.append((hs, off, toks))
                if he > off + n:
                    keep.append((off + n, he, toks))
            else:
                keep.append((hs, he, toks))
        self.hist = keep
        ap = self.base[0:shape[0], off // 2:(off + n) // 2]
        if dtype == F32:
            ap = ap.bitcast(F32)
        cnt = 1
        for s in shape[1:]:
            cnt *= s
        ap = ap[:, 0:cnt]
        if len(shape) == 3:
            ap = ap.rearrange("p (a b) -> p a b", a=shape[1])
        elif len(shape) == 4:
            ap = ap.rearrange("p (a b c) -> p a b c", a=shape[1], b=shape[2])
        return Buf(ap, off, n, _reduce(inherit))

    def release(self, buf):
        toks = list(buf.inherit)
        for r in buf.res.values():
            toks.append(r.w)
            toks += r.r
        toks = _reduce(toks)
        self.hist.append((buf.off, buf.off + buf.nbytes, toks))
        self.free.append((buf.off, buf.off + buf.nbytes))
        self.free.sort()
        merged = []
        for s, e in self.free:
            if merged and merged[-1][1] == s:
                merged[-1] = (merged[-1][0], e)
            else:
                merged.append((s, e))
        self.free = merged


def build(debug=(), stop=None):
    nc = bass.Bass("TRN2", target_bir_lowering=False)
    st = contextlib.ExitStack()
    dbg_outs = {}

    def din(name, shape, dt=F32):
        return nc.dram_tensor(name, list(shape), dt, kind="ExternalInput").ap()

    def dout(name, shape, dt=F32):
        return nc.dram_tensor(name, list(shape), dt, kind="ExternalOutput").ap()

    xs_d = din("xs", [LS, D])
    xp_d = din("xp", [2 * LP, D])
    cols_d = din("cols", [128, NCOL])
    rows_d = din("rows", [128, 5, D])
    kc_d = din("kc", [LP, 256])
    vc_d = din("vc", [LP, 256])
    ropec_d = din("ropec", [128, NKEY])
    ropes_d = din("ropes", [128, NKEY])
    ident_d = din("ident", [128, 128], BF16)
    permf_d = din("permf", [128, 128])
    maskl_d = din("maskl", [128, 128], BF16)
    maskr_d = din("maskr", [128, 128], BF16)
    feat_d = {LS: din("feat2048", [17, LS]), LP: din("feat256", [17, LP])}
    cm_d = {LS: din("cm2048", [LS // 256, 128, LS // 128, 256], BF16), LP: din("cm256", [1, 128, 2, 256], BF16)}
    m2f_d = {LS: din("m2f2048", [LS // 256, 128, LS // 128, 256], BF16), LP: din("m2f256", [1, 128, 2, 256], BF16)}
    m2i_d = {LS: din("m2i2048", [NOWN // 256, 128, LS // 128, 256], BF16), LP: din("m2i256", [1, 128, 2, 256], BF16)}
    fw1_d = din("fw1", [17, 64])
    fw2_d = din("fw2", [64, 64])
    fw3_d = din("fw3", [65, 2 * HW])
    wmod_d = din("w_mod", [D, 6 * D])
    win_d = din("w_in", [D, 3072])
    wgate_d = din("w_gate", [D, 2 * D])
    wpa_d = din("w_pa", [D, D])
    wph_d = din("w_ph", [HW, D])
    wo_d = din("w_o", [D, D])
    wup_d = din("w_up", [D, 2 * DFF])
    wdown_d = din("w_down", [DFF, D])
    ys_d = dout("ys", [NOWN, D])
    yp_d = dout("yp", [2 * LP, D])
    nk_d = dout("nk", [2 * LP, 256])
    nv_d = dout("nv", [2 * LP, 256])
    spec_d = {LS: nc.dram_tensor("spec2048", [17, 128, 1024], BF16).ap(),
              LP: nc.dram_tensor("spec256", [3, 128, 1024], BF16).ap()}

    ARENA_BYTES = 207 * 1024
    arena_t = st.enter_context(nc.sbuf_tensor("arena", [128, ARENA_BYTES // 2], BF16))
    AR = Arena(arena_t[:, :], ARENA_BYTES)
    banks = [st.enter_context(nc.psum_tensor(f"bank{i}", [128, 512], F32)) for i in range(8)]
    BK = [Res(f"bank{i}", excl=True) for i in range(8)]
    S = Sched(nc)
    out_toks = []
    dram_res = {}

    def DR(name):
        if name not in dram_res:
            dram_res[name] = Res(name)
        return dram_res[name]

    def MM(out, lhsT, rhs, st_, sp_, rd, wr, inc=True):
        return S.op("pe", lambda: nc.tensor.matmul(out, lhsT=lhsT, rhs=rhs, start=st_, stop=sp_,
                                                   skip_group_check=True), reads=rd, writes=wr, inc=inc)

    def TR(out, in_, rd, wr, inc=True):
        return S.op("pe", lambda: nc.tensor.transpose(out, in_, ident[:, :]), reads=rd + [ident.R()], writes=wr, inc=inc)

    def ACT(out, in_, func, rd, wr, scale=None, bias=None, accum=None):
        kw = {}
        if scale is not None:
            kw["scale"] = scale
        if bias is not None:
            kw["bias"] = bias
        if accum is not None:
            kw["accum_out"] = accum
        return S.op("act", lambda: nc.scalar.activation(out=out, in_=in_, func=func, **kw), reads=rd, writes=wr)

    def ENG(e):
        return nc.vector if e == "dve" else nc.gpsimd

    def TT(e, out, a, b, op, rd, wr):
        return S.op(e, lambda: ENG(e).tensor_tensor(out=out, in0=a, in1=b, op=op), reads=rd, writes=wr)

    def TS(e, out, a, s1, s2, op0, op1, rd, wr):
        if op1 is None:
            return S.op(e, lambda: ENG(e).tensor_scalar(out=out, in0=a, scalar1=s1, scalar2=None, op0=op0), reads=rd, writes=wr)
        return S.op(e, lambda: ENG(e).tensor_scalar(out=out, in0=a, scalar1=s1, scalar2=s2, op0=op0, op1=op1), reads=rd, writes=wr)

    def STT(out, in0, scalar, in1, op0, op1, rd, wr):
        return S.op("dve", lambda: nc.vector.scalar_tensor_tensor(out=out, in0=in0, scalar=scalar, in1=in1, op0=op0, op1=op1),
                    reads=rd, writes=wr)

    def CP(e, out, in_, rd, wr):
        if e == "act":
            return S.op("act", lambda: nc.scalar.copy(out=out, in_=in_), reads=rd, writes=wr)
        return S.op(e, lambda: ENG(e).tensor_copy(out=out, in_=in_), reads=rd, writes=wr)

    def MSET(e, ap, val, wr):
        return S.op(e, lambda: ENG(e).memset(ap, val), writes=wr)

    def RECIP(out, in_, rd, wr):
        return S.op("dve", lambda: nc.vector.reciprocal(out=out, in_=in_), reads=rd, writes=wr)

    def DMA(e, out, in_, rd, wr):
        h = {"sp": nc.sync, "pool": nc.gpsimd, "act": nc.scalar}[e]
        return S.dma(e, lambda: h.dma_start(out=out, in_=in_), reads=rd, writes=wr)

    def DBG(name, buf, rd):
        if name not in debug:
            return
        shape = list(buf.ap.shape)
        dt = buf.ap.dtype
        d = nc.dram_tensor("dbg_" + name, shape, dt, kind="ExternalOutput").ap()
        dbg_outs[name] = d
        out_toks.append(DMA("sp", d, buf.ap, rd, [DR("dbg_" + name)]))

    def finish():
        S.wait_all("sp", out_toks)
        S.run(st)
        st.close()
        return nc, dbg_outs, S

    ident = AR.alloc([128, 128], BF16)
    permf = AR.alloc([128, 128], F32)
    maskl = AR.alloc([128, 128], BF16)
    maskr = AR.alloc([128, 128], BF16)
    cols = AR.alloc([128, NCOL], F32)
    zeros = AR.alloc([128, 128], F32)
    junk = AR.alloc([128, 1024], BF16)
    DMA("sp", ident.ap, ident_d, [], [ident.R()])
    DMA("sp", permf.ap, permf_d, [], [permf.R()])
    DMA("sp", maskl.ap, maskl_d, [], [maskl.R()])
    DMA("sp", maskr.ap, maskr_d, [], [maskr.R()])
    DMA("sp", cols.ap, cols_d, [], [cols.R()])
    MSET("dve", zeros.ap, 0.0, [zeros.R()])

    def col(name, j=0, n=1):
        o, w = _CL[name]
        return cols[:, o + j:o + j + n]

    NRING = 4
    ring = [AR.alloc([128, 8, 512], BF16) for _ in range(NRING)]
    ring_i = [0]

    def wload(dram_ap, kc, ncols):
        slot = ring[ring_i[0] % len(ring)]
        ring_i[0] += 1
        DMA("pool", slot[:, 0:kc, 0:ncols], dram_ap.rearrange("(k p) n -> p k n", p=128), [], [slot.R()])
        return slot

    wmod_pre = []
    for pi_, part_ in enumerate([0, 1]):
        for half_ in range(2):
            if len(wmod_pre) < 3:
                wmod_pre.append(wload(wmod_d[:, part_ * D + half_ * 512: part_ * D + (half_ + 1) * 512], 8, 512))
    fw1 = AR.alloc([17, 64], F32)
    fw2 = AR.alloc([64, 64], F32)
    fw3 = AR.alloc([65, 2 * HW], F32)
    adec = AR.alloc([128, 2 * HW], F32)
    fb = AR.alloc([64, 2], F32)
    DMA("sp", fw1.ap, fw1_d, [], [fw1.R()])
    DMA("sp", fw2.ap, fw2_d, [], [fw2.R()])
    DMA("sp", fw3.ap, fw3_d, [], [fw3.R()])
    fw3b = AR.alloc([65, 2 * HW], BF16)
    CP("dve", fw3b.ap, fw3.ap, [fw3.R()], [fw3b.R()])
    DMA("sp", adec.ap, rows_d[:, 4, :], [], [adec.R()])
    ACT(adec.ap, adec.ap, AF.Abs, [adec.R()], [adec.R()])
    fo = _CL["fvec"][0]
    fv = cols.ap
    TT("dve", fb[:, 0:1], fv[0:64, fo + 0:fo + 1], fv[0:64, fo + 1:fo + 2], ALU.mult, [cols.R()], [fb.R()])
    TT("dve", fb[:, 1:2], fv[0:64, fo + 2:fo + 3], fv[0:64, fo + 3:fo + 4], ALU.mult, [cols.R()], [fb.R()])

    def wrap_pi(a, ares, t, tres):
        TS("dve", t, a, -math.pi, 2 * math.pi, ALU.is_lt, ALU.mult, [ares], [tres])
        TT("dve", a, a, t, ALU.add, [ares, tres], [ares])
        TS("dve", t, a, math.pi, -2 * math.pi, ALU.is_gt, ALU.mult, [ares], [tres])
        TT("dve", a, a, t, ALU.add, [ares, tres], [ares])

    def filter_spectrum(L):
        nt = L // 128
        feat = AR.alloc([17, L], F32)
        nb_ = max(1, L // 512)
        h1s = [AR.alloc([64, 512], F32) for _ in range(nb_)]
        args = [AR.alloc([64, 512], F32) for _ in range(nb_)]
        h2 = AR.alloc([65, L], BF16)
        wtmps = [AR.alloc([64, 512], F32) for _ in range(nb_)]
        DMA("sp", feat.ap, feat_d[L], [], [feat.R()])
        MSET("dve", h2[64:65, :], 1.0, [h2.R()])
        nb = max(1, L // 512)
        bw = min(512, L)
        def blk(tb):
            return slice(tb * bw, (tb + 1) * bw), h1s[tb], args[tb], wtmps[tb], (2 * tb) % 8, (2 * tb + 1) % 8

        for tb in range(nb):
            sl, h1, arg, wtmp, b0, b1 = blk(tb)
            MM(banks[b0][0:64, 0:bw], fw1.ap, feat[:, sl], True, True, [fw1.R(), feat.R()], [BK[b0]])
            ACT(arg[:, 0:bw], banks[b0][0:64, 0:bw], AF.Identity, [BK[b0], cols.R(), fb.R()], [arg.R()],
                scale=fv[0:64, fo + 1:fo + 2], bias=fb[:, 0:1])
        yield "mlp"
        for tb in range(nb):
            sl, h1, arg, wtmp, b0, b1 = blk(tb)
            wrap_pi(arg[:, 0:bw], arg.R(), wtmp[:, 0:bw], wtmp.R())
        yield "mlp"
        for tb in range(nb):
            sl, h1, arg, wtmp, b0, b1 = blk(tb)
            ACT(h1[:, 0:bw], arg[:, 0:bw], AF.Sin, [arg.R()], [h1.R()])
            MM(banks[b1][0:64, 0:bw], fw2.ap, h1[:, 0:bw], True, True, [fw2.R(), h1.R()], [BK[b1]])
            ACT(arg[:, 0:bw], banks[b1][0:64, 0:bw], AF.Identity, [BK[b1], cols.R(), fb.R()], [arg.R()],
                scale=fv[0:64, fo + 3:fo + 4], bias=fb[:, 1:2])
        yield "mlp"
        for tb in range(nb):
            sl, h1, arg, wtmp, b0, b1 = blk(tb)
            wrap_pi(arg[:, 0:bw], arg.R(), wtmp[:, 0:bw], wtmp.R())
        yield "mlp"
        for tb in range(nb):
            sl, h1, arg, wtmp, b0, b1 = blk(tb)
            ACT(h2[0:64, sl], arg[:, 0:bw], AF.Sin, [arg.R()], [h2.R()])
        for b_ in h1s + args + wtmps:
            AR.release(b_)
        AR.release(feat)
        eT = AR.alloc([128, nt, HW], BF16)
        oT = AR.alloc([128, nt, HW], BF16)
        edec = [AR.alloc([128, 2 * HW], F32) for _ in range(2)]
        ftap = [AR.alloc([128, 2 * HW], F32) for _ in range(2)]
        ngt = "negt2048" if L == LS else "negt256"
        for j in range(nt):
            ed = edec[j % 2]
            ft = ftap[j % 2]
            ACT(ed.ap, adec.ap, AF.Exp, [adec.R(), cols.R()], [ed.R()], scale=col(ngt, j))
            for hlf in range(2):
                b = 2 + (2 * j + hlf) % 4
                MM(banks[b][:, :], h2[0:65, j * 128:(j + 1) * 128], fw3b[0:65, hlf * HW:(hlf + 1) * HW], True, True,
                   [h2.R(), fw3b.R()], [BK[b]])
                TT("dve", ft[:, hlf * HW:(hlf + 1) * HW], banks[b][:, :], ed[:, hlf * HW:(hlf + 1) * HW], ALU.mult,
                   [BK[b], ed.R()], [ft.R()])
            if j == 0:
                MSET("dve", ft[0:1, HW:2 * HW], 0.0, [ft.R()])
            TT("dve", eT[:, j, :], ft[:, 0:HW], ft[:, HW:2 * HW], ALU.add, [ft.R()], [eT.R(j)])
            TT("pool", oT[:, j, :], ft[:, 0:HW], ft[:, HW:2 * HW], ALU.subtract, [ft.R()], [oT.R(j)])
            if j % 4 == 3:
                yield "taps"
        for b_ in edec + ftap:
            AR.release(b_)
        AR.release(h2)
        yield "spec"
        wPn, wQn = ("wP2048", "wQ2048") if L == LS else ("wP256", "wQ256")
        nfp = L // 256
        dslots = [AR.alloc([128, nt, 256], BF16) for _ in range(4)]
        pq = [AR.alloc([128, 2, HW], BF16) for _ in range(2)]
        pl = AR.alloc([1, HW], F32)
        pv0 = AR.alloc([128, HW], BF16)
        di = 0
        for fp in range(nfp):
            cs = dslots[di % 4]
            ms = dslots[(di + 1) % 4]
            di += 2
            DMA("sp", cs.ap, cm_d[L][fp], [], [cs.R()])
            DMA("sp", ms.ap, m2f_d[L][fp], [], [ms.R()])
            for f2 in range(2):
                ft_i = fp * 2 + f2
                bP = 4 + (ft_i % 2) * 2
                bQ = bP + 1
                for k in range(nt):
                    MM(banks[bP][:, :], cs[:, k, f2 * 128:(f2 + 1) * 128], eT[:, k, :], k == 0, k == nt - 1,
                       [cs.R(), eT.R(k)], [BK[bP]], inc=(k == nt - 1))
                for k in range(nt):
                    MM(banks[bQ][:, :], ms[:, k, f2 * 128:(f2 + 1) * 128], oT[:, k, :], k == 0, k == nt - 1,
                       [ms.R(), oT.R(k)], [BK[bQ]], inc=(k == nt - 1))
                p_ = pq[ft_i % 2]
                ACT(p_[:, 0, :], banks[bP][:, :], AF.Copy, [BK[bP], cols.R()], [p_.R()], scale=col(wPn, ft_i))
                ACT(p_[:, 1, :], banks[bQ][:, :], AF.Copy, [BK[bQ], cols.R()], [p_.R()], scale=col(wQn, ft_i))
                DMA("act", spec_d[L][ft_i].rearrange("p (a c) -> p a c", a=2), p_.ap, [p_.R()], [DR(f"spec{L}")])
                if ft_i == 0:
                    for k in range(nt):
                        MM(banks[3][0:1, :], ms[:, k, 0:1], eT[:, k, :], k == 0, k == nt - 1,
                           [ms.R(), eT.R(k)], [BK[3]], inc=(k == nt - 1))
                    CP("dve", pv0.ap, p_[:, 0, :], [p_.R()], [pv0.R()])
                    ACT(pv0[0:1, :], banks[3][0:1, :], AF.Copy, [BK[3]], [pv0.R()], scale=1.0 / (2 * L))
                    DMA("act", spec_d[L][nt][:, 0:HW], pv0.ap, [pv0.R()], [DR(f"spec{L}")])
                yield "ft"
        for b_ in dslots + pq + [pl, pv0, eT, oT]:
            AR.release(b_)

    genS = filter_spectrum(LS)
    genP = filter_spectrum(LP)
    tS = tP = None
    while tS != "spec" or tP != "spec":
        if tS != "spec":
            tS = next(genS)
        if tP != "spec":
            tP = next(genP)
    for b_ in [fw1, fw2, fw3, fw3b, adec, fb]:
        AR.release(b_)

    pstate = {"lp": True}

    def pump(n=1):
        for _ in range(n):
            if pstate["lp"]:
                try:
                    next(genP)
                except StopIteration:
                    pstate["lp"] = False
            try:
                next(genS)
            except StopIteration:
                return
    pump(2)

    if stop == 'F':
        pump(100)
        return finish()
    scol = AR.alloc([128, 8, 2], BF16)
    srep = AR.alloc([128, 8, 2, 128], BF16)
    modc = AR.alloc([128, 4, 8, 2], F32)
    gain = AR.alloc([128, 2, 8, 2], F32)
    oc = _CL["ccond"][0]
    ACT(scol.ap.rearrange("p k c -> p (k c)"), cols[:, oc:oc + 16], AF.Silu, [cols.R()], [scol.R()])
    scolf = AR.alloc([128, 16], F32)
    ACT(scolf.ap, cols[:, oc:oc + 16], AF.Silu, [cols.R()], [scolf.R()])
    for k in range(8):
        for c in range(2):
            ACT(srep[:, k, c, :], zeros[:, 0:128], AF.Identity, [zeros.R(), scolf.R()], [srep.R()],
                bias=scolf[:, 2 * k + c:2 * k + c + 1], scale=1.0)
    parts = [0, 1, 3, 4]
    for pi, part in enumerate(parts):
        for half in range(2):
            slot = wmod_pre.pop(0) if wmod_pre else wload(wmod_d[:, part * D + half * 512: part * D + (half + 1) * 512], 8, 512)
            for f4 in range(4):
                fc = half * 4 + f4
                o_ = (pi * 8 + fc) * 2
                for k in range(8):
                    MM(banks[0][:, o_:o_ + 2], slot[:, k, f4 * 128:(f4 + 1) * 128], scol[:, k, :], k == 0, k == 7,
                       [slot.R(), scol.R()], [BK[0]], inc=(k == 7))
            pump(1)
    ob = _CL["bmodc"][0]
    for c in range(2):
        TT("dve", modc.ap.rearrange("p a k c -> p (a k) c")[:, :, c],
           banks[0][:, 0:64].rearrange("p (a c) -> p a c", c=2)[:, :, c],
           cols[:, ob:ob + 32], ALU.add, [BK[0], cols.R()], [modc.R()])
    for n_, (gname, sc_i) in enumerate([("gpre1", 1), ("gpre2", 3)]):
        og = _CL[gname][0]
        for c in range(2):
            STT(gain[:, n_, :, c], modc[:, sc_i, :, c], 1.0, cols[:, og:og + 8], ALU.add, ALU.mult,
                [modc.R(), cols.R()], [gain.R()])
    AR.release(scolf)

    if stop == '0':
        pump(100)
        return finish()
    def prenorm_run(groups, between=None):
        ssb = [AR.alloc([128, 8], F32) for _ in range(2)]
        xnb = [[AR.alloc([128, D], BF16) for _ in range(4)] for _ in range(2)]

        def part1(gi):
            G = groups[gi]
            srcs, src_res = G["srcs"]()
            G["n"] = len(srcs)
            ss = ssb[gi % 2]
            xn = xnb[gi % 2]
            for a, src in enumerate(srcs):
                S.op("dve", lambda src=src, acc=ss[:, a:a + 1]: nc.vector.scalar_tensor_tensor(
                    out=junk.ap, in0=src, scalar=1.0, in1=src, op0=ALU.mult, op1=ALU.mult, accum_out=acc),
                    reads=[src_res[a]], writes=[junk.R(), ss.R(a)])
            for a, src in enumerate(srcs):
                ACT(ss[:, a:a + 1], ss[:, a:a + 1], AF.Sqrt, [ss.R(a), epsc.R()], [ss.R(a)], scale=1.0 / D, bias=epsc[:, 0:1])
            for a, src in enumerate(srcs):
                RECIP(ss[:, 4 + a:5 + a], ss[:, a:a + 1], [ss.R(a)], [ss.R(a)])
            for a, src in enumerate(srcs):
                if a == 0:
                    TS("dve", xn[a].ap, src, ss[:, 4 + a:5 + a], 0.0, ALU.mult, ALU.add, [src_res[a], ss.R(a)], [xn[a].R()])
                else:
                    ACT(xn[a].ap, src, AF.Copy, [src_res[a], ss.R(a)], [xn[a].R()], scale=ss[:, 4 + a:5 + a])

        def part2(gi):
            G = groups[gi]
            nt_ = G["n"]
            xn = xnb[gi % 2]
            dst_aps, dst_res, n_idx, conds = G["dst_aps"], G["dst_res"], G["n_idx"], G["conds"]
            for a in range(nt_):
                for k in range(8):
                    b = k // 2
                    pT = banks[b][:, :].bitcast(BF16)
                    TR(pT[:, (k % 2) * 512 + a * 128:(k % 2) * 512 + (a + 1) * 128], xn[a][:, k * 128:(k + 1) * 128],
                       [xn[a].R()], [BK[b]], inc=(k % 2 == 1))
            c = conds[0]
            for k in range(8):
                b = k // 2
                pT = banks[b][:, :].bitcast(BF16)
                i_ap = pT[:, (k % 2) * 512:(k % 2) * 512 + nt_ * 128]
                sc_ap = gain[:, n_idx, k, c:c + 1]
                sh_ap = modc[:, 0 if n_idx == 0 else 2, k, c:c + 1]
                o_ap = dst_aps[k]
                if k % 2 == 0:
                    ACT(o_ap, i_ap, AF.Identity, [BK[b], gain.R(), modc.R()], [dst_res[k]], scale=sc_ap, bias=sh_ap)
                else:
                    TS("dve", o_ap, i_ap, sc_ap, sh_ap, ALU.mult, ALU.add, [BK[b], gain.R(), modc.R()], [dst_res[k]])

        part1(0)
        for gi in range(len(groups)):
            if gi + 1 < len(groups):
                part1(gi + 1)
            if between is not None:
                between()
            part2(gi)
        for b_ in ssb + xnb[0] + xnb[1]:
            AR.release(b_)

    epsc = AR.alloc([128, 1], F32)
    MSET("dve", epsc.ap, EPS, [epsc.R()])

    hTs = AR.alloc([128, 8, LS + 2], BF16)
    hTp = AR.alloc([128, 8, 2 * (LP + 2)], BF16)
    for k in range(8):
        MSET("dve", hTs[:, k, 0:1], 0.0, [hTs.R(k)])
        MSET("dve", hTs[:, k, LS + 1:LS + 2], 0.0, [hTs.R(k)])
        for s in range(2):
            MSET("dve", hTp[:, k, s * 258:s * 258 + 1], 0.0, [hTp.R(k)])
            MSET("dve", hTp[:, k, s * 258 + 257:s * 258 + 258], 0.0, [hTp.R(k)])
    xring = [AR.alloc([128, D], F32) for _ in range(8)]
    xi = [0]

    def load_x(dram_rows):
        t = xring[xi[0] % len(xring)]
        xi[0] += 1
        DMA("sp", t.ap, dram_rows, [], [t.R()])
        return t

    def loader(row_aps):
        def f():
            tiles = [load_x(r) for r in row_aps]
            return [t.ap for t in tiles], [t.R() for t in tiles]
        return f

    groups = []
    for g in range(4):
        groups.append(dict(srcs=loader([xs_d[(4 * g + a) * 128:(4 * g + a + 1) * 128, :] for a in range(4)]),
                           dst_aps=[hTs[:, k, 1 + g * 512:1 + (g + 1) * 512] for k in range(8)],
                           dst_res=[hTs.R(k) for k in range(8)], n_idx=0, conds=[1] * 4))
    for s_ in range(2):
        groups.append(dict(srcs=loader([xp_d[(2 * s_ + a) * 128:(2 * s_ + a + 1) * 128, :] for a in range(2)]),
                           dst_aps=[hTp[:, k, s_ * 258 + 1:s_ * 258 + 257] for k in range(8)],
                           dst_res=[hTp.R(k) for k in range(8)], n_idx=0, conds=[0] * 2))
    prenorm_run(groups, between=lambda: pump(1))
    pump(100)
    for t in xring:
        AR.release(t)
    DBG("hTs", hTs, [hTs.R(k) for k in range(8)])
    DBG("hTp", hTp, [hTp.R(k) for k in range(8)])

    hs_all = [hTs.R(k) for k in range(8)]
    hp_all = [hTp.R(k) for k in range(8)]

    if stop == '1':
        return finish()
    ring2 = [AR.alloc([128, 8, 512], BF16) for _ in range(3)]
    ring.extend(ring2)
    QT = AR.alloc([128, NH, NM], BF16)
    KTs = AR.alloc([128, NKV, NKEY], BF16)
    KTp = AR.alloc([128, NKV, 2 * LP], BF16)
    KTc = AR.alloc([128, NKV, LP], BF16)
    Vs = AR.alloc([128, 9, NKV, 129], BF16)
    Vp = AR.alloc([128, 4, NKV, 129], BF16)
    Vc = AR.alloc([128, 2, NKV, 129], BF16)
    ropec = AR.alloc([128, NKEY], F32)
    ropes = AR.alloc([128, NKEY], F32)
    DMA("sp", ropec.ap, ropec_d, [], [ropec.R()])
    DMA("sp", ropes.ap, ropes_d, [], [ropes.R()])
    for vb, n_ in ((Vs, 9), (Vp, 4), (Vc, 2)):
        MSET("dve", vb.ap.rearrange("p a g c -> p (a g) c")[:, :, 128:129], 1.0, [vb.R()])

    kcv = AR.alloc([128, 2, 2, 256], F32)
    kcb = AR.alloc([128, 2, 256], BF16)
    DMA("sp", kcv[:, 0, :, :], kc_d.rearrange("(a p) c -> p a c", p=128), [], [kcv.R(0)])
    DMA("sp", kcv[:, 1, :, :], vc_d.rearrange("(a p) c -> p a c", p=128), [], [kcv.R(1)])
    CP("act", kcb.ap, kcv[:, 0, :, :], [kcv.R(0)], [kcb.R()])
    for a in range(2):
        CP("dve", Vc[:, a, :, 0:128], kcv[:, 1, a, :].rearrange("p (g c) -> p g c", g=2), [kcv.R(1)], [Vc.R()])
    pT7 = banks[7][:, :].bitcast(BF16)
    for g in range(2):
        for a in range(2):
            TR(pT7[:, (g * 2 + a) * 128:(g * 2 + a + 1) * 128], kcb[:, a, g * 128:(g + 1) * 128], [kcb.R()], [BK[7]],
               inc=(g == 1 and a == 1))
    CP("dve", KTc.ap, pT7[:, 0:512].rearrange("p (g t) -> p g t", g=2), [BK[7]], [KTc.R()])
    AR.release(kcv)
    AR.release(kcb)

    rr = [0]
    qraw = [AR.alloc([128, 512], BF16) for _ in range(3)]
    rtmp = [AR.alloc([128, 512], F32) for _ in range(3)]
    rtmp2 = [AR.alloc([128, 512], F32) for _ in range(3)]
    ri = [0]
    permb = AR.alloc([128, 128], BF16)
    CP("dve", permb.ap, permf.ap, [permf.R()], [permb.R()])

    def pbank():
        b = rr[0] % 6
        rr[0] += 1
        return b

    deferred = []

    def flush_deferred():
        while deferred:
            deferred.pop(0)()

    def rope_evac(b, n, t0, out_ap, out_res):
        i = ri[0] % 3
        ri[0] += 1
        q_ = qraw[i]
        t_ = rtmp[i]
        t2 = rtmp2[i]
        CP("act", q_[:, 0:n], banks[b][:, 0:n], [BK[b]], [q_.R()])
        TT("dve", t_[:, 0:n], banks[b][:, 0:n], ropec[:, t0:t0 + n], ALU.mult, [BK[b], ropec.R()], [t_.R()])

        def later():
            pb = 6 + (i % 2)
            MM(banks[pb][:, 0:n], permb.ap, q_[:, 0:n], True, True, [permb.R(), q_.R()], [BK[pb]])
            TT("dve", t2[:, 0:n], banks[pb][:, 0:n], ropes[:, t0:t0 + n], ALU.mult, [BK[pb], ropes.R()], [t2.R()])
            TT("dve", out_ap, t_[:, 0:n], t2[:, 0:n], ALU.add, [t_.R(), t2.R()], [out_res])
        deferred.append(later)

    def proj_fm(slot, c0, rhs_fn, n, rd):
        b = pbank()
        for k in range(8):
            MM(banks[b][:, 0:n], slot[:, k, c0:c0 + 128], rhs_fn(k), k == 0, k == 7, [slot.R()] + rd, [BK[b]], inc=(k == 7))
        flush_deferred()
        return b

    slot = wload(win_d[:, 1024:1536], 8, 512)
    for g in range(2):
        for (t0, n) in ((0, 512), (512, 512), (1024, 128)):
            b = proj_fm(slot, g * 128, lambda k, t0=t0, n=n: hTs[:, k, 1 + t0:1 + t0 + n], n, hs_all)
            rope_evac(b, n, t0, KTs[:, g, t0:t0 + n], KTs.R(g))
        for s in range(2):
            b = proj_fm(slot, g * 128, lambda k, s=s: hTp[:, k, s * 258 + 1:s * 258 + 257], 256, hp_all)
            CP("act", KTp[:, g, s * 256:(s + 1) * 256], banks[b][:, 0:256], [BK[b]], [KTp.R(g)])
    kvout = [AR.alloc([128, 512], F32) for _ in range(2)]
    for i in range(4):
        s, a = i // 2, i % 2
        b = pbank()
        for k in range(8):
            MM(banks[b][:, :], hTp[:, k, s * 258 + 1 + a * 128:s * 258 + 1 + (a + 1) * 128], slot[:, k, :], k == 0, k == 7,
               [slot.R()] + hp_all, [BK[b]], inc=(k == 7))
        ko = kvout[i % 2]
        CP("act", ko.ap, banks[b][:, :], [BK[b]], [ko.R()])
        CP("dve", Vp[:, i, :, 0:128], banks[b][:, 256:512].rearrange("p (g c) -> p g c", g=2), [BK[b]], [Vp.R()])
        out_toks.append(DMA("sp", nk_d[i * 128:(i + 1) * 128, :], ko[:, 0:256], [ko.R()], [DR("nk")]))
        out_toks.append(DMA("sp", nv_d[i * 128:(i + 1) * 128, :], ko[:, 256:512], [ko.R()], [DR("nv")]))
    for i in range(9):
        b = pbank()
        for k in range(8):
            MM(banks[b][:, 0:256], hTs[:, k, 1 + i * 128:1 + (i + 1) * 128], slot[:, k, 256:512], k == 0, k == 7,
               [slot.R()] + hs_all, [BK[b]], inc=(k == 7))
        CP("dve", Vs[:, i, :, 0:128], banks[b][:, 0:256].rearrange("p (g c) -> p g c", g=2), [BK[b]], [Vs.R()])
    for ko in kvout:
        AR.release(ko)
    for hb in range(2):
        slot = wload(win_d[:, hb * 512:(hb + 1) * 512], 8, 512)
        for h4 in range(4):
            h = hb * 4 + h4
            for (t0, n) in ((0, 512), (512, 512)):
                b = proj_fm(slot, h4 * 128, lambda k, t0=t0, n=n: hTs[:, k, 1 + t0:1 + t0 + n], n, hs_all)
                rope_evac(b, n, t0, QT[:, h, t0:t0 + n], QT.R(h))
            for s in range(2):
                b = proj_fm(slot, h4 * 128, lambda k, s=s: hTp[:, k, s * 258 + 1:s * 258 + 257], 256, hp_all)
                CP("act", QT[:, h, NOWN + s * 256:NOWN + (s + 1) * 256], banks[b][:, 0:256], [BK[b]], [QT.R(h)])
    flush_deferred()
    for b_ in qraw + rtmp + rtmp2 + [ropec, ropes, permb]:
        AR.release(b_)
    DBG("QT", QT, [QT.R(h) for h in range(NH)])
    DBG("KTs", KTs, [KTs.R(g) for g in range(2)])
    DBG("Vs", Vs, [Vs.R()])

    x0 = AR.alloc([128, 4, NM], BF16)
    zs = AR.alloc([128, 4, NOWN], BF16)
    zso = AR.alloc([128, 4, NOWN], BF16)
    zp = AR.alloc([128, 4, 2 * LP], BF16)
    cacc = [AR.alloc([128, 512], F32) for _ in range(4)]
    ci = [0]
    ocw, ocb = _CL["convw"][0], _CL["convb"][0]

    def conv_from_psum(b, n, chg, out_ap, out_rd_wr):
        a_ = cacc[ci[0] % 4]
        ci[0] += 1
        w = lambda tap: cols[:, ocw + chg * 3 + tap:ocw + chg * 3 + tap + 1]
        ACT(a_[:, 0:n], banks[b][:, 1:n + 1], AF.Identity, [BK[b], cols.R()], [a_.R()], scale=w(1),
            bias=cols[:, ocb + chg:ocb + chg + 1])
        STT(a_[:, 0:n], banks[b][:, 0:n], w(0), a_[:, 0:n], ALU.mult, ALU.add, [BK[b], cols.R(), a_.R()], [a_.R()])
        if out_ap is None:
            STT(a_[:, 0:n], banks[b][:, 2:n + 2], w(2), a_[:, 0:n], ALU.mult, ALU.add, [BK[b], cols.R(), a_.R()], [a_.R()])
            return a_
        STT(out_ap, banks[b][:, 2:n + 2], w(2), a_[:, 0:n], ALU.mult, ALU.add, [BK[b], cols.R(), a_.R()], out_rd_wr)
        return None

    def hy_blocks(T0, T1):
        out = []
        s = T0
        while s < T1:
            n = min(510, T1 - s)
            out.append((s, n))
            s += n
        return out

    slot1 = wload(win_d[:, 2048:2560], 8, 512)
    slot2 = wload(win_d[:, 2560:3072], 8, 512)
    slot_x0 = wload(win_d[:, 1536:2048], 8, 512)
    for ch in range(4):
        segs = [("s", s, n) for (s, n) in hy_blocks(0, NOWN) + hy_blocks(NOWN, LS)] + [("p", 0, 256), ("p", 1, 256)]
        for kind, s, n in segs:
            if kind == "s":
                rhs = lambda k, s=s, n=n: hTs[:, k, s:s + n + 2]
                rd = hs_all
            else:
                rhs = lambda k, s=s: hTp[:, k, s * 258:s * 258 + 258]
                rd = hp_all
            b1 = proj_fm(slot1, ch * 128, rhs, n + 2, rd)
            b2 = proj_fm(slot2, ch * 128, rhs, n + 2, rd)
            a1 = conv_from_psum(b1, n, 4 + ch, None, None)
            a2 = conv_from_psum(b2, n, 8 + ch, None, None)
            if kind == "s":
                zb_, s_ = (zs, s) if s < NOWN else (zso, s - NOWN)
                TT("pool", zb_[:, ch, s_:s_ + n], a1[:, 0:n], a2[:, 0:n], ALU.mult, [a1.R(), a2.R()], [zb_.R(ch)])
            else:
                TT("pool", zp[:, ch, s * 256:(s + 1) * 256], a1[:, 0:n], a2[:, 0:n], ALU.mult, [a1.R(), a2.R()], [zp.R(ch)])
    slot = slot_x0
    for ch in range(4):
        for (s, n) in hy_blocks(0, NOWN):
            b = proj_fm(slot, ch * 128, lambda k, s=s, n=n: hTs[:, k, s:s + n + 2], n + 2, hs_all)
            conv_from_psum(b, n, ch, x0[:, ch, s:s + n], [x0.R(ch)])
        for s in range(2):
            b = proj_fm(slot, ch * 128, lambda k, s=s: hTp[:, k, s * 258:s * 258 + 258], 258, hp_all)
            conv_from_psum(b, 256, ch, x0[:, ch, NOWN + s * 256:NOWN + (s + 1) * 256], [x0.R(ch)])
    for a_ in cacc:
        AR.release(a_)
    for b_ in ring2:
        ring.remove(b_)
        AR.release(b_)
    AR.release(hTs)
    AR.release(hTp)
    DBG("x0", x0, [x0.R(c) for c in range(4)])
    DBG("zs", zs, [zs.R(c) for c in range(4)])
    DBG("zp", zp, [zp.R(c) for c in range(4)])
    zTs = AR.alloc([128, 16, HW], BF16)
    zTp = AR.alloc([128, 4, HW], BF16)

    def z_transpose(j):
        b = 2 if j % 2 == 0 else 6
        pT = banks[b][:, :].bitcast(BF16)
        for hf in range(2):
            for c2 in range(2):
                ch = hf * 2 + c2
                if j < 8:
                    src, sr = zs[:, ch, j * 128:(j + 1) * 128], zs.R(ch)
                elif j < 16:
                    src, sr = zso[:, ch, (j - 8) * 128:(j - 7) * 128], zso.R(ch)
                else:
                    src, sr = zp[:, ch, (j - 16) * 128:(j - 15) * 128], zp.R(ch)
                TR(pT[:, 768 + c2 * 128:768 + (c2 + 1) * 128], src, [sr], [BK[b]], inc=(c2 == 1))
            dst = zTs[:, j, hf * 256:(hf + 1) * 256] if j < 16 else zTp[:, j - 16, hf * 256:(hf + 1) * 256]
            dres = zTs.R(j) if j < 16 else zTp.R(j - 16)
            CP("act" if hf else "dve", dst, pT[:, 768:1024], [BK[b]], [dres])

    zt_pending = list(range(20))
    attnT = AR.alloc([128, NH, NM], BF16)
    esink = AR.alloc([128, 8], F32)
    ACT(esink.ap, col("sink", 0, 8), AF.Exp, [cols.R()], [esink.R()])
    esrow = AR.alloc([1, 8, 128], BF16)
    eunit = AR.alloc([1, 129], BF16)
    MSET("dve", eunit[0:1, 0:128], 0.0, [eunit.R()])
    MSET("dve", eunit[0:1, 128:129], 1.0, [eunit.R()])
    for h in range(8):
        ACT(esrow[0:1, h, :], zeros[0:1, 0:128], AF.Identity, [zeros.R(), esink.R()], [esrow.R()],
            bias=esink[0:1, h:h + 1], scale=1.0)
    PT = [AR.alloc([128, 5, 256], BF16) for _ in range(2)]
    Otok = [AR.alloc([128, 256], BF16) for _ in range(3)]
    dent = [AR.alloc([128, 4], F32) for _ in range(3)]
    it = [0]

    iters = []

    def attn_iter(heads, qcol, slots, tokcol):
        iters.append((heads, qcol, slots, tokcol))

    def stage_A(i):
        heads, qcol, slots, tokcol = iters[i]
        p = i % 2
        sb = [4 * p, 4 * p + 1, 4 * p + 2]
        P_ = PT[p]
        ns = len(slots)
        for si, (kt, v, mask, rd) in enumerate(slots):
            b = sb[si // 2]
            for hh, h in enumerate(heads):
                o_ = banks[b][:, (si % 2) * 256 + hh * 128:(si % 2) * 256 + (hh + 1) * 128]
                MM(o_, kt, QT[:, h, qcol:qcol + 128], True, mask is None, rd + [QT.R(h)], [BK[b]],
                   inc=(mask is None and hh == 1))
                if mask is not None:
                    MM(o_, ident.ap, mask.ap, False, True, [ident.R(), mask.R()], [BK[b]], inc=(hh == 1))
        for bi in range((ns + 1) // 2):
            w = min(2, ns - 2 * bi) * 256
            ACT(P_[:, 2 * bi:2 * bi + w // 256, :].rearrange("p a c -> p (a c)"), banks[sb[bi]][:, 0:w], AF.Exp,
                [BK[sb[bi]]], [P_.R()], scale=SCALE)

    def stage_B(i):
        heads, qcol, slots, tokcol = iters[i]
        p = i % 2
        ob = 4 * p + 3
        P_ = PT[p]
        ns = len(slots)
        for hh, h in enumerate(heads):
            for si, (kt, v, mask, rd) in enumerate(slots):
                MM(banks[ob][:, hh * 129:(hh + 1) * 129], P_[:, si, hh * 128:(hh + 1) * 128], v, si == 0, si == ns - 1,
                   [P_.R()] + rd, [BK[ob]], inc=(si == ns - 1))
        d_ = dent[i % 3]
        O_ = Otok[i % 3]
        h0 = heads[0]
        TT("dve", d_[:, 0:2], banks[ob][:, 0:258].rearrange("p (h c) -> p h c", h=2)[:, :, 128],
           esink[:, h0:h0 + 2], ALU.add, [BK[ob], esink.R()], [d_.R()])
        RECIP(d_[:, 2:4], d_[:, 0:2], [d_.R()], [d_.R()])
        for hh in range(2):
            TS("dve", O_[:, hh * 128:(hh + 1) * 128], banks[ob][:, hh * 129:hh * 129 + 128], d_[:, 2 + hh:3 + hh], 0.0,
               ALU.mult, ALU.add, [BK[ob], d_.R()], [O_.R()])

    def stage_T(i):
        heads, qcol, slots, tokcol = iters[i]
        p = i % 2
        tb = 4 * p + 2
        O_ = Otok[i % 3]
        h0 = heads[0]
        pT = banks[tb][:, :].bitcast(BF16)
        for hh in range(2):
            TR(pT[:, 512 + hh * 128:512 + (hh + 1) * 128], O_[:, hh * 128:(hh + 1) * 128], [O_.R()], [BK[tb]], inc=(hh == 1))
        CP("dve", attnT[:, h0:h0 + 2, tokcol:tokcol + 128], pT[:, 512:768].rearrange("p (h t) -> p h t", h=2),
           [BK[tb]], [attnT.R(h0)])

    for g in range(2):
        for qb in range(8):
            slots = []
            for kb, mask in ((qb - 1, maskl), (qb, None), (qb + 1, maskr)):
                if kb < 0:
                    continue
                slots.append((KTs[:, g, kb * 128:(kb + 1) * 128], Vs[:, kb, g, :], mask, [KTs.R(g), Vs.R()]))
            for cb in range(2):
                slots.append((KTc[:, g, cb * 128:(cb + 1) * 128], Vc[:, cb, g, :], None, [KTc.R(), Vc.R()]))
            for hp_ in range(2):
                heads = [g * 4 + hp_ * 2, g * 4 + hp_ * 2 + 1]
                attn_iter(heads, qb * 128, slots, qb * 128)
    for s in range(2):
        for g in range(2):
            slots = [(KTp[:, g, s * 256 + kb * 128:s * 256 + (kb + 1) * 128], Vp[:, s * 2 + kb, g, :], None, [KTp.R(g), Vp.R()])
                     for kb in range(2)]
            for qb in range(2):
                for hp_ in range(2):
                    heads = [g * 4 + hp_ * 2, g * 4 + hp_ * 2 + 1]
                    c0 = NOWN + s * 256 + qb * 128
                    attn_iter(heads, c0, slots, c0)
    nit = len(iters)
    for step in range(nit + 3):
        if step < nit:
            stage_A(step)
        if 0 <= step - 3 < nit:
            stage_T(step - 3)
        if 0 <= step - 1 < nit:
            stage_B(step - 1)
        if zt_pending and step % 2 == 1:
            z_transpose(zt_pending.pop(0))
    while zt_pending:
        z_transpose(zt_pending.pop(0))
    AR.release(zso)
    for b_ in PT + Otok + dent + [QT, KTs, KTp, KTc, Vs, Vp, Vc, esink, esrow, eunit]:
        AR.release(b_)
    DBG("attnT", attnT, [attnT.R(h) for h in range(0, NH, 2)])

    if stop == '4':
        return finish()
    hyT = AR.alloc([128, 4, NM], BF16)
    osk = _CL["skip"][0]

    def hyena_dft(L, zT, zfm, zfm_c0, tok0, nown, after_fwd=None):
        nt = L // 128
        Ub = AR.alloc([128, nt, HW], BF16)
        Vb = AR.alloc([128, nt, HW], BF16)
        dsl = [AR.alloc([128, nt, 256], BF16) for _ in range(4)]
        pqs = [AR.alloc([128, 2, HW], BF16) for _ in range(2)]
        pv0 = AR.alloc([128, HW], BF16)
        mm_ = [AR.alloc([128, 4, HW], F32) for _ in range(1)]
        DMA("sp", pv0.ap, spec_d[L][nt][:, 0:HW], [DR(f"spec{L}")], [pv0.R()])
        di = 0
        nfp = L // 256
        loads = {}

        def issue(fp):
            nonlocal di
            cs, ms = dsl[di % 4], dsl[(di + 1) % 4]
            di += 2
            DMA("sp", cs.ap, cm_d[L][fp], [], [cs.R()])
            DMA("sp", ms.ap, m2f_d[L][fp], [], [ms.R()])
            loads[fp] = (cs, ms)

        issue(0)
        yield "fwd"
        for fp in range(nfp):
            if fp + 1 < nfp:
                issue(fp + 1)
            cs, ms = loads.pop(fp)
            for f2 in range(2):
                fi = fp * 2 + f2
                p_ = pqs[fi % 2]
                DMA("sp", p_.ap, spec_d[L][fi].rearrange("p (a c) -> p a c", a=2), [DR(f"spec{L}")], [p_.R()])
                bA = (fi % 2) * 2
                bB = bA + 1
                for k in range(nt):
                    MM(banks[bA][:, :], cs[:, k, f2 * 128:(f2 + 1) * 128], zT[:, k, :], k == 0, k == nt - 1,
                       [cs.R(), zT.R(k)], [BK[bA]], inc=(k == nt - 1))
                for k in range(nt):
                    MM(banks[bB][:, :], ms[:, k, f2 * 128:(f2 + 1) * 128], zT[:, k, :], k == 0, k == nt - 1,
                       [ms.R(), zT.R(k)], [BK[bB]], inc=(k == nt - 1))
                m_ = mm_[0]
                pv = pv0.ap if fi == 0 else p_[:, 0, :]
                pvr = [pv0.R()] if fi == 0 else []
                TT("dve", m_[:, 0, :], banks[bA][:, :], p_[:, 0, :], ALU.mult, [BK[bA], p_.R()], [m_.R(0)])
                TT("dve", m_[:, 3, :], banks[bA][:, :], p_[:, 1, :], ALU.mult, [BK[bA], p_.R()], [m_.R(3)])
                TT("dve", m_[:, 1, :], banks[bB][:, :], p_[:, 1, :], ALU.mult, [BK[bB], p_.R()], [m_.R(1)])
                TT("dve", m_[:, 2, :], banks[bB][:, :], pv, ALU.mult, [BK[bB], p_.R()] + pvr, [m_.R(2)])
                TT("dve", Ub[:, fi, :], m_[:, 0, :], m_[:, 1, :], ALU.subtract, [m_.R(0), m_.R(1)], [Ub.R(fi)])
                TT("dve", Vb[:, fi, :], m_[:, 2, :], m_[:, 3, :], ALU.add, [m_.R(2), m_.R(3)], [Vb.R(fi)])
                yield "fwd"
        for b_ in pqs + [pv0] + mm_:
            AR.release(b_)
        if after_fwd is not None:
            after_fwd()
        ytmp = [AR.alloc([128, 256], F32) for _ in range(2)]
        yi = 0
        ntq = nown // 256
        iloads = {}

        def issue_inv(tq):
            nonlocal di
            cs, ms = dsl[di % 4], dsl[(di + 1) % 4]
            di += 2
            DMA("sp", cs.ap, cm_d[L][tq], [], [cs.R()])
            DMA("sp", ms.ap, m2i_d[L][tq], [], [ms.R()])
            iloads[tq] = (cs, ms)

        issue_inv(0)
        yield "inv"
        for tq in range(ntq):
            if tq + 1 < ntq:
                issue_inv(tq + 1)
            cs, ms = iloads.pop(tq)
            for ct in range(4):
                b = 4 + (yi % 4)
                for k in range(2 * nt):
                    src = cs if k < nt else ms
                    uvb = Ub if k < nt else Vb
                    MM(banks[b][:, 0:256], uvb[:, k % nt, ct * 128:(ct + 1) * 128], src[:, k % nt, :], k == 0, k == 2 * nt - 1,
                       [uvb.R(k % nt), src.R()], [BK[b]], inc=(k == 2 * nt - 1))
                y_ = ytmp[yi % 2]
                yi += 1
                zsrc, zres = zfm(ct, tq * 256)
                STT(y_.ap, zsrc, cols[:, osk + ct:osk + ct + 1], banks[b][:, 0:256], ALU.mult, ALU.add,
                    [zres, cols.R(), BK[b]], [y_.R()])
                TT("dve", hyT[:, ct, tok0 + tq * 256:tok0 + (tq + 1) * 256], y_.ap,
                   x0[:, ct, tok0 + tq * 256:tok0 + (tq + 1) * 256], ALU.mult, [y_.R(), x0.R(ct)], [hyT.R(ct)])
                if ct % 2 == 1:
                    yield "inv"
        for b_ in ytmp + dsl + [Ub, Vb]:
            AR.release(b_)

    class _V:
        def __init__(self, s):
            self.s = s

        def __getitem__(self, idx):
            p, k, c = idx
            return zTp[p, 2 * self.s + k, c]

        def R(self, k):
            return zTp.R(2 * self.s + k)

    def prompt_gens():
        for s_ in range(2):
            yield from hyena_dft(LP, _V(s_), lambda ct, c0, s_=s_: (zp[:, ct, s_ * 256 + c0:s_ * 256 + c0 + 256], zp.R(ct)),
                                 0, NOWN + s_ * 256, LP)

    pg = prompt_gens()
    for tag in hyena_dft(LS, zTs, lambda ct, c0: (zs[:, ct, c0:c0 + 256], zs.R(ct)), 0, 0, NOWN,
                         after_fwd=lambda: AR.release(zTs)):
        if tag == "inv":
            try:
                next(pg)
                next(pg)
            except StopIteration:
                pass
    for _ in pg:
        pass
    for b_ in [zTp, zs, zp, x0]:
        AR.release(b_)
    DBG("hyT", hyT, [hyT.R(c) for c in range(4)])


    if stop == '5':
        return finish()
    hTm = AR.alloc([128, 8, NM], BF16)
    ring_extra = [AR.alloc([128, 8, 512], BF16) for _ in range(2)]
    ring.extend(ring_extra)
    xring = [AR.alloc([128, D], F32) for _ in range(8)]
    xi[0] = 0

    def main_rows(i):
        if i < 8:
            return xs_d[i * 128:(i + 1) * 128, :]
        return xp_d[(i - 8) * 128:(i - 7) * 128, :]

    groups = []
    for g in range(3):
        groups.append(dict(srcs=loader([main_rows(4 * g + a) for a in range(4)]),
                           dst_aps=[hTm[:, k, g * 512:(g + 1) * 512] for k in range(8)],
                           dst_res=[hTm.R(k) for k in range(8)], n_idx=0, conds=[1 if g < 2 else 0] * 4))
    hy_all = [hyT.R(c) for c in range(4)]
    phT = AR.alloc([128, 8, NM], BF16)
    ph_todo = [0, 1]

    def ph_block():
        if not ph_todo:
            return
        fb2 = ph_todo.pop(0)
        s_ph = wload(wph_d[:, fb2 * 512:(fb2 + 1) * 512], 4, 512)
        for f4 in range(4):
            fc = fb2 * 4 + f4
            for tb in range(3):
                tsl = slice(tb * 512, (tb + 1) * 512)
                b = 4 + pbank() % 4
                for k in range(4):
                    MM(banks[b][:, :], s_ph[:, k, f4 * 128:(f4 + 1) * 128], hyT[:, k, tsl], k == 0, k == 3,
                       [s_ph.R()] + hy_all, [BK[b]], inc=(k == 3))
                CP("act" if tb % 2 else "dve", phT[:, fc, tsl], banks[b][:, :], [BK[b]], [phT.R(fc)])

    prenorm_run(groups, between=ph_block)
    for t in xring:
        AR.release(t)
    hm_all = [hTm.R(k) for k in range(8)]
    at_all = [attnT.R(h) for h in range(0, NH, 2)]
    while ph_todo:
        ph_block()
    for fb2 in range(0):
        s_ph = wload(wph_d[:, fb2 * 512:(fb2 + 1) * 512], 4, 512)
        for f4 in range(4):
            fc = fb2 * 4 + f4
            for tb in range(3):
                tsl = slice(tb * 512, (tb + 1) * 512)
                b = pbank()
                for k in range(4):
                    MM(banks[b][:, :], s_ph[:, k, f4 * 128:(f4 + 1) * 128], hyT[:, k, tsl], k == 0, k == 3,
                       [s_ph.R()] + hy_all, [BK[b]], inc=(k == 3))
                CP("act" if tb % 2 else "dve", phT[:, fc, tsl], banks[b][:, :], [BK[b]], [phT.R(fc)])
    AR.release(hyT)
    mergedT = AR.alloc([128, 8, NM], BF16)
    sg = [AR.alloc([128, 4, 512], F32) for _ in range(2)]
    obg = _CL["bgate"][0]
    mi = 0
    for fb2 in range(2):
        s_pa = wload(wpa_d[:, fb2 * 512:(fb2 + 1) * 512], 8, 512)
        s_ga = wload(wgate_d[:, fb2 * 512:(fb2 + 1) * 512], 8, 512)
        s_gh = wload(wgate_d[:, D + fb2 * 512:D + (fb2 + 1) * 512], 8, 512)
        for f4 in range(4):
            fc = fb2 * 4 + f4
            for tb in range(3):
                tsl = slice(tb * 512, (tb + 1) * 512)
                base = 4 * (mi % 2)
                g_ = sg[mi % 2]
                mi += 1
                for k in range(8):
                    MM(banks[base][:, :], s_pa[:, k, f4 * 128:(f4 + 1) * 128], attnT[:, k, tsl], k == 0, k == 7,
                       [s_pa.R()] + at_all, [BK[base]], inc=(k == 7))
                for k in range(8):
                    MM(banks[base + 1][:, :], s_ga[:, k, f4 * 128:(f4 + 1) * 128], hTm[:, k, tsl], k == 0, k == 7,
                       [s_ga.R()] + hm_all, [BK[base + 1]], inc=(k == 7))
                for k in range(8):
                    MM(banks[base + 2][:, :], s_gh[:, k, f4 * 128:(f4 + 1) * 128], hTm[:, k, tsl], k == 0, k == 7,
                       [s_gh.R()] + hm_all, [BK[base + 2]], inc=(k == 7))
                ACT(g_[:, 0, :], banks[base + 1][:, :], AF.Sigmoid, [BK[base + 1], cols.R()], [g_.R(0)],
                    bias=cols[:, obg + fc:obg + fc + 1])
                ACT(g_[:, 1, :], banks[base + 2][:, :], AF.Sigmoid, [BK[base + 2], cols.R()], [g_.R(1)],
                    bias=cols[:, obg + 8 + fc:obg + 8 + fc + 1])
                TT("dve", g_[:, 2, :], banks[base][:, :], g_[:, 0, :], ALU.mult, [BK[base], g_.R(0)], [g_.R(2)])
                TT("dve", g_[:, 3, :], phT[:, fc, tsl], g_[:, 1, :], ALU.mult, [phT.R(fc), g_.R(1)], [g_.R(3)])
                TT("dve", mergedT[:, fc, tsl], g_[:, 2, :], g_[:, 3, :], ALU.add, [g_.R(2), g_.R(3)], [mergedT.R(fc)])
    for b_ in sg + [phT, hTm, attnT]:
        AR.release(b_)
    mg_all = [mergedT.R(fc) for fc in range(8)]
    DBG("mergedT", mergedT, mg_all)

    if stop == '6':
        return finish()
    def gate_rows(part, brow_i, nrow_i):
        G = AR.alloc([128, 2, D], F32)
        rw = AR.alloc([128, 2, D], F32)
        DMA("sp", rw[:, 0, :], rows_d[:, brow_i, :], [], [rw.R(0)])
        DMA("sp", rw[:, 1, :], rows_d[:, nrow_i, :], [], [rw.R(1)])
        for half in range(2):
            slot = wload(wmod_d[:, part * D + half * 512:part * D + (half + 1) * 512], 8, 512)
            for c in range(2):
                b = pbank()
                for k in range(8):
                    MM(banks[b][:, :], srep[:, k, c, :], slot[:, k, :], k == 0, k == 7, [srep.R(), slot.R()], [BK[b]],
                       inc=(k == 7))
                hs = slice(half * 512, (half + 1) * 512)
                TT("dve", G[:, c, hs], banks[b][:, :], rw[:, 0, hs], ALU.add, [BK[b], rw.R(0)], [G.R(c)])
                TT("dve", G[:, c, hs], G[:, c, hs], rw[:, 1, hs], ALU.mult, [G.R(c), rw.R(1)], [G.R(c)])
        AR.release(rw)
        return G

    pn_sets = []
    pn_i = [0]

    def postnorm_alloc(n=3):
        for _ in range(n):
            pn_sets.append((AR.alloc([128, 4], F32), AR.alloc([128, D], F32)))

    def postnorm_free():
        for ss_, tmp_ in pn_sets:
            AR.release(ss_)
            AR.release(tmp_)
        pn_sets.clear()

    def postnorm_residual(bk0, bk1, G, cond, xtile, xres):
        ss, tmp = pn_sets[pn_i[0] % len(pn_sets)]
        pn_i[0] += 1
        ACT(junk[:, 0:512], banks[bk0][:, :], AF.Square, [BK[bk0]], [junk.R(), ss.R()], accum=ss[:, 0:1])
        ACT(junk[:, 512:1024], banks[bk1][:, :], AF.Square, [BK[bk1]], [junk.R(), ss.R()], accum=ss[:, 1:2])
        TT("dve", ss[:, 2:3], ss[:, 0:1], ss[:, 1:2], ALU.add, [ss.R()], [ss.R()])
        ACT(ss[:, 2:3], ss[:, 2:3], AF.Sqrt, [ss.R(), epsc.R()], [ss.R()], scale=1.0 / D, bias=epsc[:, 0:1])
        RECIP(ss[:, 3:4], ss[:, 2:3], [ss.R()], [ss.R()])
        STT(tmp[:, 0:512], banks[bk0][:, :], ss[:, 3:4], G[:, cond, 0:512], ALU.mult, ALU.mult, [BK[bk0], ss.R(), G.R(cond)], [tmp.R()])
        STT(tmp[:, 512:1024], banks[bk1][:, :], ss[:, 3:4], G[:, cond, 512:1024], ALU.mult, ALU.mult, [BK[bk1], ss.R(), G.R(cond)], [tmp.R()])
        TT("pool", xtile, xtile, tmp.ap, ALU.add, [tmp.R(), xres], [xres])

    G1 = gate_rows(2, 2, 0)
    x1 = AR.alloc([128, 12, D], F32)
    for i in range(12):
        DMA("sp", x1[:, i, :], main_rows(i), [], [x1.R(i)])
    postnorm_alloc(3)
    wo = [wload(wo_d[:, half * 512:(half + 1) * 512], 8, 512) for half in range(2)]
    for i in range(12):
        bb = [(2 * i) % 8, (2 * i + 1) % 8]
        for half in range(2):
            for k in range(8):
                MM(banks[bb[half]][:, :], mergedT[:, k, i * 128:(i + 1) * 128], wo[half][:, k, :], k == 0, k == 7,
                   [wo[half].R()] + mg_all, [BK[bb[half]]], inc=(k == 7))
        postnorm_residual(bb[0], bb[1], G1, 1 if i < 8 else 0, x1[:, i, :], x1.R(i))
    postnorm_free()
    AR.release(mergedT)
    AR.release(G1)
    DBG("x1", x1, [x1.R(i) for i in range(12)])

    if stop == '7':
        return finish()
    h2T = AR.alloc([128, 8, NM], BF16)
    groups = []
    for g in range(3):
        groups.append(dict(srcs=(lambda g=g: ([x1[:, 4 * g + a, :] for a in range(4)], [x1.R(4 * g + a) for a in range(4)])),
                           dst_aps=[h2T[:, k, g * 512:(g + 1) * 512] for k in range(8)],
                           dst_res=[h2T.R(k) for k in range(8)], n_idx=1, conds=[1 if g < 2 else 0] * 4))
    g2box = []

    def g2_once():
        if not g2box:
            g2box.append(gate_rows(5, 3, 1))

    prenorm_run(groups, between=g2_once)
    h2_all = [h2T.R(k) for k in range(8)]

    for b_ in ring_extra:
        ring.remove(b_)
        AR.release(b_)
    aT_bufs = []
    aT_map = []
    need = 22
    while need > 0:
        best = max(AR.free, key=lambda se: se[1] - se[0])
        can = min(need, (best[1] - best[0]) // (NM * 2))
        assert can > 0, ("arena too fragmented for aT", AR.free)
        b_ = AR.alloc([128, can, NM], BF16)
        for j_ in range(can):
            aT_map.append((b_, j_))
        aT_bufs.append(b_)
        need -= can
    sil = [AR.alloc([128, 512], F32) for _ in range(2)]
    ui = 0
    halves = []
    hres_all = []
    for sl_ in ring:
        full = sl_.R()
        toks = _reduce([full.w] + full.r)
        hr = [Res("h0", toks), Res("h1", toks)]
        hres_all.append((sl_, hr))
        halves += [(sl_[:, :, 0:256], hr[0]), (sl_[:, :, 256:512], hr[1])]
    hi_ = [0]

    def hload(dram_ap):
        ap_, r_ = halves[hi_[0] % len(halves)]
        hi_[0] += 1
        DMA("pool", ap_, dram_ap.rearrange("(k p) n -> p k n", p=128), [], [r_])
        return ap_, r_

    for g2 in range(11):
        ga, gr = hload(wup_d[:, g2 * 256:(g2 + 1) * 256])
        ua, ur = hload(wup_d[:, DFF + g2 * 256:DFF + (g2 + 1) * 256])
        for f2 in range(2):
            fc = g2 * 2 + f2
            for tb in range(3):
                tsl = slice(tb * 512, (tb + 1) * 512)
                base = 2 * (ui % 4)
                s_ = sil[ui % 2]
                ui += 1
                for k in range(8):
                    MM(banks[base][:, :], ga[:, k, f2 * 128:(f2 + 1) * 128], h2T[:, k, tsl], k == 0, k == 7,
                       [gr] + h2_all, [BK[base]], inc=(k == 7))
                for k in range(8):
                    MM(banks[base + 1][:, :], ua[:, k, f2 * 128:(f2 + 1) * 128], h2T[:, k, tsl], k == 0, k == 7,
                       [ur] + h2_all, [BK[base + 1]], inc=(k == 7))
                ACT(s_.ap, banks[base][:, :], AF.Silu, [BK[base]], [s_.R()])
                TT("dve", aT_map[fc][0][:, aT_map[fc][1], tsl], banks[base + 1][:, :], s_.ap, ALU.mult, [BK[base + 1], s_.R()],
                   [aT_map[fc][0].R(aT_map[fc][1])])
    for sl_, hr in hres_all:
        full = sl_.R()
        toks = list(full.r) + [full.w]
        for h_ in hr:
            toks += [h_.w] + h_.r
        full.r = _reduce(toks)
    for b_ in sil + [h2T]:
        AR.release(b_)
    a_all = [aT_map[fc][0].R(aT_map[fc][1]) for fc in range(22)]

    if stop == '9':
        return finish()
    G2 = g2box[0]
    postnorm_alloc(2)
    xs_ = [AR.alloc([128, 8, 512], BF16) for _ in range(3)]
    wd = {}
    for kg in range(3):
        kc = 8 if kg < 2 else 6
        for half in range(2):
            if half == 0:
                wd[(kg, half)] = wload(wdown_d[kg * 1024:kg * 1024 + kc * 128, 0:512], kc, 512)
            else:
                sl_ = xs_[kg]
                DMA("pool", sl_[:, 0:kc, :], wdown_d[kg * 1024:kg * 1024 + kc * 128, 512:1024].rearrange("(k p) n -> p k n", p=128),
                    [], [sl_.R()])
                wd[(kg, half)] = sl_
    for i in range(12):
        bb = [(2 * i) % 8, (2 * i + 1) % 8]
        for half in range(2):
            for k in range(22):
                w_ = wd[(k // 8, half)]
                MM(banks[bb[half]][:, :], aT_map[k][0][:, aT_map[k][1], i * 128:(i + 1) * 128], w_[:, k % 8, :], k == 0, k == 21,
                   [w_.R()] + a_all, [BK[bb[half]]], inc=(k == 21))
        postnorm_residual(bb[0], bb[1], G2, 1 if i < 8 else 0, x1[:, i, :], x1.R(i))
        if i < 8:
            out_toks.append(DMA("sp", ys_d[i * 128:(i + 1) * 128, :], x1[:, i, :], [x1.R(i)], [DR("ys")]))
        else:
            out_toks.append(DMA("sp", yp_d[(i - 8) * 128:(i - 7) * 128, :], x1[:, i, :], [x1.R(i)], [DR("yp")]))

    return finish()


def _colform(v):
    v = np.asarray(v, np.float32).reshape(-1, 128)
    return np.ascontiguousarray(v.T)


def _consts():
    c = {}
    c["ident"] = np.eye(128, dtype=np.float32).astype(BF)
    perm = np.zeros((128, 128), np.float32)
    for m in range(128):
        blk, i = divmod(m, 64)
        src = blk * 64 + (i + 32) % 64
        perm[src, m] = 1.0
    c["permf"] = perm
    j = np.arange(128)[:, None]
    i = np.arange(128)[None, :]
    c["maskl"] = np.where(j >= i, 0.0, NEG).astype(np.float32).astype(BF)
    c["maskr"] = np.where(j <= i, 0.0, NEG).astype(np.float32).astype(BF)
    for L in (LS, LP):
        t = np.arange(L, dtype=np.float32)
        bands = np.linspace(1e-4, 7, 8).astype(np.float32)
        ang = (2.0 * np.float32(math.pi) * t / np.float32(L))[:, None] * bands[None, :]
        feat = np.concatenate([(t / np.float32(L))[:, None], np.cos(ang), np.sin(ang)], axis=-1).astype(np.float32)
        c[f"feat{L}"] = np.ascontiguousarray(feat.T)
        ii = np.arange(L, dtype=np.int64)
        prod = np.outer(ii, ii) % (2 * L)
        th = prod.astype(np.float64) * (math.pi / L)
        cm = np.cos(th)
        m2 = np.sin(th)
        m2[:, 0] = (-1.0) ** ii
        nown = NOWN if L == LS else LP

        def tile_(mat):
            Lr, W = mat.shape
            return np.ascontiguousarray(mat.reshape(Lr // 128, 128, W // 256, 256).transpose(2, 1, 0, 3)).astype(np.float32).astype(BF)

        c[f"cm{L}"] = tile_(cm)
        c[f"m2f{L}"] = tile_(m2)
        c[f"m2i{L}"] = tile_(np.ascontiguousarray(m2.T[:, :nown]))
    return c


_CONSTS = None


def _rope_tables(tglob):
    half = 64
    inv = (10000.0 ** (-np.arange(0, half, 2, dtype=np.float32) / half)).astype(np.float32)
    row = (tglob // 64).astype(np.float32)
    colp = (tglob % 64).astype(np.float32)
    cosT = np.zeros((128, len(tglob)), np.float32)
    sinT = np.zeros((128, len(tglob)), np.float32)
    for i in range(128):
        pos = row if i < 64 else colp
        a = pos * inv[i % 32]
        cosT[i] = np.cos(a)
        sinT[i] = np.sin(a) * (-1.0 if (i % 64) < 32 else 1.0)
    return cosT, sinT


def prep_inputs(inp):
    global _CONSTS
    if _CONSTS is None:
        _CONSTS = _consts()
    f = lambda k: np.asarray(inp[k], np.float32)
    maps = []
    shared = dict(_CONSTS)
    shared["fw1"] = np.ascontiguousarray(f("filt_w1")[0])
    shared["fw2"] = np.ascontiguousarray(f("filt_w2")[0])
    shared["fw3"] = np.ascontiguousarray(np.concatenate([f("filt_w3")[0], f("filt_b3")[0][None, :]], axis=0))
    for k in ("w_mod", "w_in", "w_gate", "w_pa", "w_ph", "w_o", "w_up", "w_down"):
        shared[k] = np.ascontiguousarray(f(k)[0])
    bmod = f("b_mod")[0]
    rows_common = [f("norm_mix_post")[0], f("norm_ffn_post")[0], bmod[2 * D:3 * D], bmod[5 * D:6 * D], f("filt_decay")[0]]
    rows = np.ascontiguousarray(np.broadcast_to(np.stack(rows_common)[None], (128, 5, D))).astype(np.float32)
    for r in range(8):
        b, half = divmod(r, 2)
        m = dict(shared)
        xs = f("x_sample")[b]
        tglob = np.arange(LS)
        if half:
            xs = xs[::-1]
            tglob = tglob[::-1]
        m["xs"] = np.ascontiguousarray(xs)
        xpr = f("x_prompt")[2 * r:2 * r + 2]
        if half:
            xpr = xpr[:, ::-1]
        m["xp"] = np.ascontiguousarray(xpr).reshape(2 * LP, D)
        m["rows"] = rows
        m["kc"] = np.ascontiguousarray(f("cache_k")[b, 0].reshape(LP, 256))
        m["vc"] = np.ascontiguousarray(f("cache_v")[b, 0].reshape(LP, 256))
        cT, sT = _rope_tables(tglob[:NKEY])
        m["ropec"], m["ropes"] = cT, sT
        cols = np.zeros((128, NCOL), np.float32)

        def put(name, arr):
            o, w = _CL[name]
            assert arr.shape == (128, w), (name, arr.shape, w)
            cols[:, o:o + w] = arr

        cc = np.stack([_colform(f("c_ctx")), _colform(f("c")[b])], axis=-1)
        put("ccond", cc.reshape(128, 16))
        put("gpre1", _colform(f("norm_mix_pre")[0]))
        put("gpre2", _colform(f("norm_ffn_pre")[0]))
        put("bmodc", np.concatenate([_colform(bmod[p * D:(p + 1) * D]) for p in (0, 1, 3, 4)], axis=1))
        cw = f("conv_w")[0]
        if half:
            cw = cw[::-1]
        put("convw", np.stack([_colform(cw[t]) for t in range(3)], axis=-1).reshape(128, 36))
        put("convb", _colform(f("conv_b")[0]))
        put("bgate", _colform(f("b_gate")[0]))
        put("skip", _colform(f("hyena_skip")[0]))
        fv = np.zeros((128, 4), np.float32)
        fv[:64, 0] = f("filt_b1")[0]
        fv[:64, 1] = f("filt_freq1")[0]
        fv[:64, 2] = f("filt_b2")[0]
        fv[:64, 3] = f("filt_freq2")[0]
        put("fvec", fv)
        for L, nm in ((LS, "2048"), (LP, "256")):
            t = np.arange(L, dtype=np.float32)
            put("negt" + nm, _colform(-(t / np.float32(L))))
            wp = np.full(L, 1.0 / L, np.float32)
            wp[0] = 1.0 / (2 * L)
            sign = -1.0 if half else 1.0
            wq = np.full(L, sign / L, np.float32)
            wq[0] = 0.0
            put("wP" + nm, _colform(wp))
            put("wQ" + nm, _colform(wq))
        put("sink", np.ascontiguousarray(np.broadcast_to(f("attn_sink")[0][None, :], (128, 8))))
        m["cols"] = cols
        maps.append(m)
    return maps


_NC = None


def kernel(**inputs):
    global _NC
    if _NC is None:
        _NC = build()[0]
    maps = prep_inputs(inputs)
    res = run_bass_kernel_spmd(_NC, maps, core_ids=list(range(8)))
    B, Bd = 16, 4
    y_prompt = np.zeros((B, LP, D), np.float32)
    y_sample = np.zeros((Bd, LS, D), np.float32)
    new_k = np.zeros((B, 1, LP, NKV, HD), np.float32)
    new_v = np.zeros((B, 1, LP, NKV, HD), np.float32)
    for r in range(8):
        o = res.results[r]
        b, half = divmod(r, 2)
        rv = (lambda a: a[:, ::-1]) if half else (lambda a: a)
        y_prompt[2 * r:2 * r + 2] = rv(np.asarray(o["yp"]).reshape(2, LP, D))
        ys = np.asarray(o["ys"])
        if half:
            y_sample[b, NOWN:] = ys[::-1]
        else:
            y_sample[b, :NOWN] = ys
        new_k[2 * r:2 * r + 2, 0] = rv(np.asarray(o["nk"]).reshape(2, LP, NKV, HD))
        new_v[2 * r:2 * r + 2, 0] = rv(np.asarray(o["nv"]).reshape(2, LP, NKV, HD))
    return (y_prompt, y_sample, new_k, new_v)
```
